# Optimizing a Trainium2 kernel written in Bass

```python
import math
import jax, jax.numpy as jnp
from jax import lax
import numpy as np

D_MODEL = 2048
BATCH = 16
SEQ = 2048
DEPTH = 4

N_MIXERS = 3
HEAD_DIM = 128
GRID_W = 64
ROPE_THETA = 10000.0
Q_BLOCK = 128
D_FF = ((8 * D_MODEL // 3 + 255) // 256) * 256
ALPHA = (2 * DEPTH) ** 0.25
BETA = (8 * DEPTH) ** -0.25
LN_EPS = 1e-5
RMS_EPS = 1e-6
NEG_INF = -1e30
A_HEADS = D_MODEL // HEAD_DIM
A_KV_HEADS = A_HEADS // 4
A_GROUP = A_HEADS // A_KV_HEADS
DILATED_PATTERNS = ((128, 1), (512, 4), (2048, 16))
B_GROUPS = len(DILATED_PATTERNS)
B_HEADS = D_MODEL // (2 * HEAD_DIM)
C_HEADS = D_MODEL // HEAD_DIM
C_Q_LORA = 3 * D_MODEL // 8
C_KV_LORA = D_MODEL // 4
C_NOPE = HEAD_DIM
C_ROPE = HEAD_DIM // 2
C_V = HEAD_DIM

kernel_name = "hybrid_interleaved_macaron_deepnorm_encoder"


def layer_norm(x, g, b):
    xf = x.astype(jnp.float32)
    mu = jnp.mean(xf, axis=-1, keepdims=True)
    var = jnp.mean(jnp.square(xf - mu), axis=-1, keepdims=True)
    return ((xf - mu) * lax.rsqrt(var + LN_EPS)).astype(x.dtype) * g + b


def rms_norm(x, g):
    xf = x.astype(jnp.float32)
    y = xf * lax.rsqrt(jnp.mean(jnp.square(xf), axis=-1, keepdims=True) + RMS_EPS)
    return y.astype(x.dtype) * g


def swiglu(x, w_in, w_out):
    gate, up = jnp.split(x @ w_in, 2, axis=-1)
    return (jax.nn.silu(gate) * up) @ w_out


def rope_cos_sin(pos, dim):
    inv = ROPE_THETA ** (-jnp.arange(0, dim, 2, dtype=jnp.float32) / dim)
    ang = pos.astype(jnp.float32)[:, None] * inv[None, :]
    return jnp.cos(ang), jnp.sin(ang)


def apply_rope(x, cos, sin):
    c = cos[None, :, None, :].astype(x.dtype)
    s = sin[None, :, None, :].astype(x.dtype)
    x1, x2 = jnp.split(x, 2, axis=-1)
    return jnp.concatenate([x1 * c - x2 * s, x1 * s + x2 * c], axis=-1)


def apply_axial_rope(x, row_cs, col_cs):
    xr, xc = jnp.split(x, 2, axis=-1)
    return jnp.concatenate([apply_rope(xr, *row_cs), apply_rope(xc, *col_cs)], axis=-1)


def block_attention(q, k, v, scale):
    bsz, seq, kh, grp, dk = q.shape
    nb = seq // Q_BLOCK
    qb = q.reshape(bsz, nb, Q_BLOCK, kh, grp, dk).transpose(1, 0, 2, 3, 4, 5)

    def one_block(q_blk):
        s = jnp.einsum('bqhgd,bkhd->bhgqk', q_blk, k).astype(jnp.float32) * scale
        p = jax.nn.softmax(s, axis=-1).astype(v.dtype)
        return jnp.einsum('bhgqk,bkhd->bqhgd', p, v)

    o = lax.map(one_block, qb)
    return o.transpose(1, 0, 2, 3, 4, 5).reshape(bsz, seq, kh * grp, v.shape[-1])


def dilated_window_attention(q, k, v, dilation, half):
    bsz, seq, nh, dh = q.shape
    L = seq // dilation
    n = bsz * dilation

    def to_residue(t):
        return t.reshape(bsz, L, dilation, nh, dh).transpose(0, 2, 1, 3, 4).reshape(n, L, nh, dh)

    qr, kr, vr = to_residue(q), to_residue(k), to_residue(v)
    nb = -(-L // Q_BLOCK)
    Lp = nb * Q_BLOCK
    kb_len = Q_BLOCK + 2 * half
    qr = jnp.pad(qr, ((0, 0), (0, Lp - L), (0, 0), (0, 0)))
    kv_pad = ((0, 0), (half, half + Lp - L), (0, 0), (0, 0))
    kr, vr = jnp.pad(kr, kv_pad), jnp.pad(vr, kv_pad)
    idx = (jnp.arange(nb) * Q_BLOCK)[:, None] + jnp.arange(kb_len)[None, :]
    kb, vb = kr[:, idx], vr[:, idx]
    qb = qr.reshape(n, nb, Q_BLOCK, nh, dh)
    s = jnp.einsum('nbqhd,nbkhd->nbhqk', qb, kb).astype(jnp.float32) * (dh ** -0.5)
    rel = jnp.arange(kb_len)[None, :] - half - jnp.arange(Q_BLOCK)[:, None]
    key_pos = idx - half
    valid = (jnp.abs(rel) <= half)[None] & ((key_pos >= 0) & (key_pos < L))[:, None, :]
    s = jnp.where(valid[None, :, None], s, NEG_INF)
    m = jnp.max(s, axis=-1, keepdims=True)
    p = jnp.exp(s - m)
    den = jnp.sum(p, axis=-1, keepdims=True)
    o = jnp.einsum('nbhqk,nbkhd->nbqhd', (p / den).astype(v.dtype), vb)
    lse = (m + jnp.log(den))[..., 0]
    o = o.reshape(n, Lp, nh, dh)[:, :L]
    o = o.reshape(bsz, dilation, L, nh, dh).transpose(0, 2, 1, 3, 4).reshape(bsz, seq, nh, dh)
    lse = lse.transpose(0, 1, 3, 2).reshape(n, Lp, nh)[:, :L]
    lse = lse.reshape(bsz, dilation, L, nh).transpose(0, 2, 1, 3).reshape(bsz, seq, nh)
    return o, lse


def mixer_a(h, w_in, q_gain, k_gain, w_out):
    bsz, seq, _ = h.shape
    rows = seq // GRID_W
    row = jnp.repeat(jnp.arange(rows), GRID_W)
    col = jnp.tile(jnp.arange(GRID_W), rows)
    row_cs = rope_cos_sin(row, HEAD_DIM // 2)
    col_cs = rope_cos_sin(col, HEAD_DIM // 2)
    qkv = h @ w_in
    q, k, v = jnp.split(qkv, [A_HEADS * HEAD_DIM, (A_HEADS + A_KV_HEADS) * HEAD_DIM], axis=-1)
    q = rms_norm(q.reshape(bsz, seq, A_HEADS, HEAD_DIM), q_gain)
    k = rms_norm(k.reshape(bsz, seq, A_KV_HEADS, HEAD_DIM), k_gain)
    v = v.reshape(bsz, seq, A_KV_HEADS, HEAD_DIM)
    q = apply_axial_rope(q, row_cs, col_cs).reshape(bsz, seq, A_KV_HEADS, A_GROUP, HEAD_DIM)
    k = apply_axial_rope(k, row_cs, col_cs)
    o = block_attention(q, k, v, HEAD_DIM ** -0.5)
    return o.reshape(bsz, seq, A_HEADS * HEAD_DIM) @ w_out


def mixer_b(h, w_in, w_out):
    bsz, seq, _ = h.shape
    cos, sin = rope_cos_sin(jnp.arange(seq), HEAD_DIM)
    qkv = (h @ w_in).reshape(bsz, seq, B_GROUPS, 3, B_HEADS, HEAD_DIM)
    outs, lses = [], []
    for g, (window, dilation) in enumerate(DILATED_PATTERNS):
        q = apply_rope(qkv[:, :, g, 0], cos, sin)
        k = apply_rope(qkv[:, :, g, 1], cos, sin)
        o, lse = dilated_window_attention(q, k, qkv[:, :, g, 2], dilation, window // (2 * dilation))
        outs.append(o)
        lses.append(lse)
    wts = jax.nn.softmax(jnp.stack(lses, axis=0), axis=0)
    o = jnp.sum(wts[..., None].astype(h.dtype) * jnp.stack(outs, axis=0), axis=0)
    return o.reshape(bsz, seq, B_HEADS * HEAD_DIM) @ w_out


def mixer_c(h, w_in, q_gain, kv_gain, w_q_up, w_kv_up, w_out):
    bsz, seq, _ = h.shape
    cos, sin = rope_cos_sin(jnp.arange(seq), C_ROPE)
    cq, ckv, k_rope = jnp.split(h @ w_in, [C_Q_LORA, C_Q_LORA + C_KV_LORA], axis=-1)
    q = (rms_norm(cq, q_gain) @ w_q_up).reshape(bsz, seq, C_HEADS, C_NOPE + C_ROPE)
    kv = (rms_norm(ckv, kv_gain) @ w_kv_up).reshape(bsz, seq, C_HEADS, C_NOPE + C_V)
    q_nope, q_rope = jnp.split(q, [C_NOPE], axis=-1)
    k_nope, v = jnp.split(kv, [C_NOPE], axis=-1)
    q_rope = apply_rope(q_rope, cos, sin)
    k_rope = apply_rope(k_rope[:, :, None, :], cos, sin)
    q = jnp.concatenate([q_nope, q_rope], axis=-1)[:, :, :, None, :]
    k = jnp.concatenate([k_nope, jnp.broadcast_to(k_rope, (bsz, seq, C_HEADS, C_ROPE))], axis=-1)
    o = block_attention(q, k, v, (C_NOPE + C_ROPE) ** -0.5)
    return o.reshape(bsz, seq, C_HEADS * C_V) @ w_out


def setup_inputs(seed: int = 0) -> dict:
    key = jax.random.key(seed)
    keys = iter(jax.random.split(key, 1 + 16 * DEPTH))
    f32 = jnp.float32

    def w(shape, fan_in, scale=1.0):
        return jax.random.normal(next(keys), shape, f32) * (fan_in ** -0.5 * scale)

    def gain(n):
        return 1.0 + 0.02 * jax.random.normal(next(keys), (n,), f32)

    def bias(n):
        return 0.02 * jax.random.normal(next(keys), (n,), f32)

    p = {"x": jax.random.normal(next(keys), (BATCH, SEQ, D_MODEL), f32)}
    for i in range(DEPTH):
        p[f"ffn1_w_in_{i}"] = w((D_MODEL, 2 * D_FF), D_MODEL)
        p[f"ffn1_w_out_{i}"] = w((D_FF, D_MODEL), D_FF, BETA)
        p[f"ln1_g_{i}"] = gain(D_MODEL)
        p[f"ln1_b_{i}"] = bias(D_MODEL)
        kind = i % N_MIXERS
        if kind == 0:
            p[f"a_w_in_{i}"] = w((D_MODEL, (A_HEADS + 2 * A_KV_HEADS) * HEAD_DIM), D_MODEL)
            p[f"a_q_gain_{i}"] = gain(HEAD_DIM)
            p[f"a_k_gain_{i}"] = gain(HEAD_DIM)
            p[f"a_w_out_{i}"] = w((A_HEADS * HEAD_DIM, D_MODEL), A_HEADS * HEAD_DIM, BETA)
        elif kind == 1:
            p[f"b_w_in_{i}"] = w((D_MODEL, B_GROUPS * 3 * B_HEADS * HEAD_DIM), D_MODEL)
            p[f"b_w_out_{i}"] = w((B_HEADS * HEAD_DIM, D_MODEL), B_HEADS * HEAD_DIM, BETA)
        else:
            p[f"c_w_in_{i}"] = w((D_MODEL, C_Q_LORA + C_KV_LORA + C_ROPE), D_MODEL)
            p[f"c_q_gain_{i}"] = gain(C_Q_LORA)
            p[f"c_kv_gain_{i}"] = gain(C_KV_LORA)
            p[f"c_w_q_up_{i}"] = w((C_Q_LORA, C_HEADS * (C_NOPE + C_ROPE)), C_Q_LORA)
            p[f"c_w_kv_up_{i}"] = w((C_KV_LORA, C_HEADS * (C_NOPE + C_V)), C_KV_LORA)
            p[f"c_w_out_{i}"] = w((C_HEADS * C_V, D_MODEL), C_HEADS * C_V, BETA)
        p[f"ln2_g_{i}"] = gain(D_MODEL)
        p[f"ln2_b_{i}"] = bias(D_MODEL)
        p[f"ffn2_w_in_{i}"] = w((D_MODEL, 2 * D_FF), D_MODEL)
        p[f"ffn2_w_out_{i}"] = w((D_FF, D_MODEL), D_FF, BETA)
        p[f"ln3_g_{i}"] = gain(D_MODEL)
        p[f"ln3_b_{i}"] = bias(D_MODEL)
    return p


def reference(x,
              ffn1_w_in_0, ffn1_w_out_0, ln1_g_0, ln1_b_0,
              a_w_in_0, a_q_gain_0, a_k_gain_0, a_w_out_0,
              ln2_g_0, ln2_b_0, ffn2_w_in_0, ffn2_w_out_0, ln3_g_0, ln3_b_0,
              ffn1_w_in_1, ffn1_w_out_1, ln1_g_1, ln1_b_1,
              b_w_in_1, b_w_out_1,
              ln2_g_1, ln2_b_1, ffn2_w_in_1, ffn2_w_out_1, ln3_g_1, ln3_b_1,
              ffn1_w_in_2, ffn1_w_out_2, ln1_g_2, ln1_b_2,
              c_w_in_2, c_q_gain_2, c_kv_gain_2, c_w_q_up_2, c_w_kv_up_2, c_w_out_2,
              ln2_g_2, ln2_b_2, ffn2_w_in_2, ffn2_w_out_2, ln3_g_2, ln3_b_2,
              ffn1_w_in_3, ffn1_w_out_3, ln1_g_3, ln1_b_3,
              a_w_in_3, a_q_gain_3, a_k_gain_3, a_w_out_3,
              ln2_g_3, ln2_b_3, ffn2_w_in_3, ffn2_w_out_3, ln3_g_3, ln3_b_3):
    mixers = (mixer_a, mixer_b, mixer_c)
    ffn1 = [(ffn1_w_in_0, ffn1_w_out_0), (ffn1_w_in_1, ffn1_w_out_1),
            (ffn1_w_in_2, ffn1_w_out_2), (ffn1_w_in_3, ffn1_w_out_3)]
    ffn2 = [(ffn2_w_in_0, ffn2_w_out_0), (ffn2_w_in_1, ffn2_w_out_1),
            (ffn2_w_in_2, ffn2_w_out_2), (ffn2_w_in_3, ffn2_w_out_3)]
    ln1 = [(ln1_g_0, ln1_b_0), (ln1_g_1, ln1_b_1), (ln1_g_2, ln1_b_2), (ln1_g_3, ln1_b_3)]
    ln2 = [(ln2_g_0, ln2_b_0), (ln2_g_1, ln2_b_1), (ln2_g_2, ln2_b_2), (ln2_g_3, ln2_b_3)]
    ln3 = [(ln3_g_0, ln3_b_0), (ln3_g_1, ln3_b_1), (ln3_g_2, ln3_b_2), (ln3_g_3, ln3_b_3)]
    mix_params = [(a_w_in_0, a_q_gain_0, a_k_gain_0, a_w_out_0),
                  (b_w_in_1, b_w_out_1),
                  (c_w_in_2, c_q_gain_2, c_kv_gain_2, c_w_q_up_2, c_w_kv_up_2, c_w_out_2),
                  (a_w_in_3, a_q_gain_3, a_k_gain_3, a_w_out_3)]
    h = x
    for i in range(DEPTH):
        h = layer_norm(ALPHA * h + 0.5 * swiglu(h, *ffn1[i]), *ln1[i])
        h = layer_norm(ALPHA * h + mixers[i % N_MIXERS](h, *mix_params[i]), *ln2[i])
        h = layer_norm(ALPHA * h + 0.5 * swiglu(h, *ffn2[i]), *ln3[i])
    return h
```

```python
import numpy as np
import concourse.bass as bass
import concourse.mybir as mybir
from concourse.bass_utils import run_bass_kernel_spmd

F32 = mybir.dt.float32
BF16 = mybir.dt.bfloat16
AF = mybir.ActivationFunctionType
ALU = mybir.AluOpType

NCORES = 8
D = 2048
SEQ = 2048
NTOK = 4096
DFF = 5632
NCH = DFF // 128
DEPTH = 4
ALPHA = (2 * DEPTH) ** 0.25
LN_EPS = 1e-5
RMS_EPS = 1e-6

ENGS = ("pe", "act", "dve", "pool", "sp")


class T:
    __slots__ = ("h", "name", "w", "r", "dsem", "dcnt")

    def __init__(self, h, name=""):
        self.h = h
        self.name = name
        self.w = []
        self.r = []
        self.dsem = None
        self.dcnt = 0

    def __getitem__(self, k):
        return self.h[k]


class Prog:
    def __init__(self, nc):
        self.nc = nc
        self.ops = {e: [] for e in ENGS}
        self.seen = {e: {} for e in ENGS}
        self.ctx = []
        self.nsb = 0
        self.tiles = []
        self.stage_tiles = []
        self.sempool = []
        self.bar = None

    def newT(self, h, name=""):
        t = T(h, name)
        self.tiles.append(t)
        self.stage_tiles.append(t)
        return t

    def view(self, t, name=""):
        return self.newT(t.h, name)

    def init_pool(self, n):
        for i in range(n):
            self.sempool.append((self.enter(self.nc.semaphore("dp%d" % i)), 0))
        self.bscr = {e: self.sb("bscr_" + e, [128, 8], F32) for e in ("act", "dve", "pool", "sp")}
        for en in ("act", "dve", "pool", "sp"):
            self.op("dve", lambda e, en=en: e.memset(self.bscr[en][:], 0.0), writes=[self.bscr[en]])
        self.bdram = self.dram("bar_dram", [128, 8], F32)
        self.bar = self.newT(None, "BAR")
        self.persist = list(self.stage_tiles)
        self.stage_tiles = []

    def mark(self):
        self.stage_tiles = []
        return len(self.ctx)

    def barrier(self):
        waits = []
        for t in self.stage_tiles + self.persist:
            for ev in t.w + t.r:
                self._need("act", ev, waits)
        scr = self.bscr
        idx = len(self.ops["act"])
        self.ops["act"].append({"fn": lambda e: e.copy(out=scr["act"][:, 0:1], in_=scr["act"][:, 1:2]),
                                "waits": waits, "flag": False, "dma": None})
        self.bar.w = [("eng", "act", idx)]
        self.bar.r = []
        for en in ("dve", "pool"):
            self.op(en, lambda e, en=en: e.memset(scr[en][:], 0.0), writes=[self.bar])
        self.dma("sp", lambda e: e.dma_start(out=self.bdram.h.ap(), in_=scr["sp"][:]), scr["sp"], writes=[self.bar])

    def release(self, mark):
        self.barrier()
        for t in self.stage_tiles:
            if t.dsem is not None:
                self.sempool.append((t.dsem, t.dcnt))
        self.stage_tiles = []
        while len(self.ctx) > mark:
            self.ctx.pop().__exit__(None, None, None)

    def enter(self, cm):
        v = cm.__enter__()
        self.ctx.append(cm)
        return v

    def sb(self, name, shape, dtype):
        self.nsb += 1
        name = "%s_u%d" % (name, self.nsb)
        return self.newT(self.enter(self.nc.sbuf_tensor(name, list(shape), dtype)), name)

    def ps(self, name, shape, dtype=F32):
        return self.newT(self.enter(self.nc.psum_tensor(name, list(shape), dtype)), name)

    def dram(self, name, shape, dtype, kind="Internal"):
        return self.newT(self.nc.dram_tensor(name, list(shape), dtype, kind=kind), name)

    def close(self):
        for cm in reversed(self.ctx):
            cm.__exit__(None, None, None)
        self.ctx = []

    def _need(self, eng, ev, waits):
        if ev[0] == "eng":
            _, e2, idx = ev
            if e2 == eng and eng in ("pe", "sp"):
                return
            if self.seen[eng].get(e2, -1) >= idx:
                return
            self.seen[eng][e2] = idx
            self.ops[e2][idx]["flag"] = True
            waits.append(ev)
        else:
            _, t, val = ev
            key = ("d", id(t))
            if self.seen[eng].get(key, -1) >= val:
                return
            self.seen[eng][key] = val
            waits.append(ev)

    def _deps(self, eng, reads, writes):
        waits = []
        for t in reads:
            for ev in t.w:
                self._need(eng, ev, waits)
        for t in writes:
            for ev in t.w:
                self._need(eng, ev, waits)
            for ev in t.r:
                self._need(eng, ev, waits)
        return waits

    def _commit(self, ev, reads, writes):
        src = ev[1] if ev[0] == "eng" else id(ev[1])
        for t in reads:
            if t in writes:
                continue
            t.r = [e for e in t.r if (e[1] if e[0] == "eng" else id(e[1])) != src]
            t.r.append(ev)
        for t in writes:
            t.w = [ev]
            t.r = []

    def op(self, eng, fn, reads=(), writes=()):
        waits = self._deps(eng, reads, writes)
        idx = len(self.ops[eng])
        self.ops[eng].append({"fn": fn, "waits": waits, "flag": False, "dma": None})
        self._commit(("eng", eng, idx), reads, writes)

    def dma(self, eng, fn, sb_tile, reads=(), writes=()):
        t = sb_tile
        if t.dsem is None:
            if self.sempool:
                t.dsem, t.dcnt = self.sempool.pop()
            else:
                t.dsem = self.enter(self.nc.semaphore("ds%d_%s" % (self.nsb, t.name)))
                self.nsb += 1
        waits = self._deps(eng, reads, writes)
        if t.dcnt > 0:
            self._need(eng, ("dma", t, t.dcnt), waits)
        t.dcnt += 16
        ev = ("dma", t, t.dcnt)
        self.ops[eng].append({"fn": fn, "waits": waits, "flag": False, "dma": t})
        self._commit(ev, reads, writes)
        return ev

    def emit(self, final_events=()):
        nc = self.nc
        fw = []
        for ev in final_events:
            self._need("sp", ev, fw)
        esem = {e: self.enter(nc.semaphore("es_" + e)) for e in ENGS}
        pref = {}
        for e in ENGS:
            c = 0
            p = []
            for o in self.ops[e]:
                if o["flag"]:
                    c += 1
                p.append(c)
            pref[e] = p

        def dowait(eo, ev):
            if ev[0] == "eng":
                eo.wait_ge(esem[ev[1]], pref[ev[1]][ev[2]])
            else:
                eo.wait_ge(ev[1].dsem, ev[2])

        def run(e, eo):
            for o in self.ops[e]:
                for ev in o["waits"]:
                    dowait(eo, ev)
                ins = o["fn"](eo)
                if o["dma"] is not None:
                    ins.then_inc(o["dma"].dsem, 16)
                elif o["flag"]:
                    ins.then_inc(esem[e], 1)
            if e == "sp":
                for ev in fw:
                    dowait(eo, ev)

        with nc.Block() as block:
            @block.tensor
            def _(eo):
                run("pe", eo)

            @block.scalar
            def _(eo):
                run("act", eo)

            @block.vector
            def _(eo):
                run("dve", eo)

            @block.gpsimd
            def _(eo):
                run("pool", eo)

            @block.sync
            def _(eo):
                run("sp", eo)
        self.close()


class Common:
    def __init__(self, P, ident_d):
        self.P = P
        self.banks = [P.ps("bank%d" % i, [128, 512], F32) for i in range(8)]
        self.ident = P.sb("ident_sb", [128, 128], F32)
        P.dma("sp", lambda e: e.dma_start(out=self.ident[:], in_=ident_d.h.ap()), self.ident,
              writes=[self.ident])
        self.cp = 0

    def copy_eng(self):
        self.cp += 1
        return "dve" if self.cp % 2 else "act"


def emit_copy(P, eng, out, in_, reads, writes):
    if eng == "act":
        P.op("act", lambda e: e.copy(out=out, in_=in_), reads=reads, writes=writes)
    else:
        P.op(eng, lambda e: e.tensor_copy(out=out, in_=in_), reads=reads, writes=writes)


def emit_ln_store(P, L, xsm, xs_h, M, g_t, b_t, out_rows_fn, out_tiles, evs):
    for m in range(M):
        for j in range(4):
            P.op("dve", lambda e, m=m, j=j: e.bn_stats(out=L["st"][:, m, j, :],
                                                       in_=xs_h[:, m, j * 512:(j + 1) * 512]),
                 reads=[xsm[m]], writes=[L["stT"]])
        P.op("dve", lambda e, m=m: e.bn_aggr(out=L["mv"][:, m, :], in_=L["st"][:, m, :, :]),
             reads=[L["stT"]], writes=[L["mvT"]])
    P.op("dve", lambda e: e.tensor_scalar(out=L["rs"][:, 0:M], in0=L["mv"][:, 0:M, 1], scalar1=LN_EPS,
                                          scalar2=None, op0=ALU.add),
         reads=[L["mvT"]], writes=[L["rsT"]])
    P.op("act", lambda e: e.sqrt(out=L["rs"][:, 0:M], in_=L["rs"][:, 0:M]), reads=[], writes=[L["rsT"]])
    P.op("dve", lambda e: e.reciprocal(out=L["rs"][:, 0:M], in_=L["rs"][:, 0:M]), reads=[], writes=[L["rsT"]])
    for m in range(M):
        P.op("dve", lambda e, m=m: e.scalar_tensor_tensor(out=xs_h[:, m, :], in0=xs_h[:, m, :],
                                                          scalar=L["mv"][:, m, 0:1], in1=g_t[:],
                                                          op0=ALU.subtract, op1=ALU.mult),
             reads=[L["mvT"], g_t], writes=[xsm[m]])
        P.op("dve", lambda e, m=m: e.scalar_tensor_tensor(out=xs_h[:, m, :], in0=xs_h[:, m, :],
                                                          scalar=L["rs"][:, m:m + 1], in1=b_t[:],
                                                          op0=ALU.mult, op1=ALU.add),
             reads=[L["rsT"], b_t], writes=[xsm[m]])
        evs.append(P.dma("sp", lambda e, m=m: e.dma_start(out=out_rows_fn(m), in_=xs_h[:, m, :]), xsm[m],
                         reads=[xsm[m]], writes=[out_tiles[m]]))


def alloc_ln(P, pfx, M):
    L = {}
    L["st"] = P.sb(pfx + "st", [128, M, 4, 6], F32)
    L["stT"] = L["st"]
    L["mv"] = P.sb(pfx + "mv", [128, M, 2], F32)
    L["mvT"] = L["mv"]
    L["rs"] = P.sb(pfx + "rs", [128, M], F32)
    L["rsT"] = L["rs"]
    return L


def alloc_ffn(P):
    B = {}
    B["xs"] = [P.sb("f_xs%d" % i, [128, 4, D], F32) for i in range(2)]
    B["xsm"] = [[P.view(B["xs"][i], "f_xs%d_%d" % (i, m)) for m in range(4)] for i in range(2)]
    B["hT"] = P.sb("f_hT", [128, 16, 512], BF16)
    B["hTm"] = [P.view(B["hT"], "f_hT%d" % m) for m in range(4)]
    B["hid"] = P.sb("f_hid", [128, NCH, 512], BF16)
    B["hidc"] = [P.view(B["hid"], "f_hid%d" % c) for c in range(NCH)]
    B["wgu"] = [P.sb("f_wgu%d" % i, [128, 16, 256], BF16) for i in range(3)]
    B["wo"] = [P.sb("f_wo%d" % i, [128, 11, 512], BF16) for i in range(3)]
    B["sg"] = [P.sb("f_sg%d" % i, [128, 512], F32) for i in range(2)]
    B["g"] = P.sb("f_g", [128, D], F32)
    B["b"] = P.sb("f_b", [128, D], F32)
    B["ln"] = alloc_ln(P, "f_", 4)
    return B


def emit_ffn(P, C, B, x_in, x_in_tiles, x_out, x_out_tiles, wgu_d, wo_d, g_d, b_d, evs, ntok=NTOK):
    TT = 512
    ntile = ntok // TT
    banks = C.banks
    P.dma("sp", lambda e: e.dma_start(out=B["g"][:], in_=g_d.h.ap().partition_broadcast(128)), B["g"],
          writes=[B["g"]])
    P.dma("sp", lambda e: e.dma_start(out=B["b"][:], in_=b_d.h.ap().partition_broadcast(128)), B["b"],
          writes=[B["b"]])

    items = []
    for t in range(ntile):
        for c in range(NCH):
            items.append(("gu", t, c))
        for n in range(4):
            for s in range(4):
                items.append(("wo", t, n, s))
    cnt = {"gu": 0, "wo": 0}
    slot_of = {}

    def load(i):
        it = items[i]
        if it[0] == "gu":
            k = cnt["gu"]
            cnt["gu"] += 1
            w = B["wgu"][k % 3]
            slot_of[i] = w
            c = it[2]
            P.dma("pool", lambda e: e.dma_start(out=w[:], in_=wgu_d.h.ap()[c].rearrange("p (k j) -> p k j", k=16)),
                  w, writes=[w])
        else:
            k = cnt["wo"]
            cnt["wo"] += 1
            w = B["wo"][k % 3]
            slot_of[i] = w
            n, s = it[2], it[3]
            P.dma("pool", lambda e: e.dma_start(
                out=w[:], in_=wo_d.h.ap()[n].rearrange("p (c j) -> p c j", c=NCH)[:, s * 11:(s + 1) * 11, :]),
                w, writes=[w])

    def load_x(t):
        b = t % 2
        xs = B["xs"][b]
        P.dma("sp", lambda e: e.dma_start(
            out=xs[:], in_=x_in.h.ap()[t * TT:(t + 1) * TT, :].rearrange("(m p) d -> p m d", p=128)),
            B["xsm"][b][0], reads=x_in_tiles[t * 4:(t + 1) * 4], writes=B["xsm"][b])

    PD = 2
    for i in range(min(PD, len(items))):
        load(i)
    load_x(0)
    gi = 0
    for t in range(ntile):
        b = t % 2
        xs = B["xs"][b]
        xsm = B["xsm"][b]
        for m in range(4):
            for kb in range(4):
                bank = banks[4 + (m * 4 + kb) % 4]
                for j in range(4):
                    k = kb * 4 + j
                    P.op("pe", lambda e, bank=bank, m=m, k=k, j=j, xs=xs: e.transpose(
                        out=bank[:, j * 128:(j + 1) * 128], in_=xs[:, m, k * 128:(k + 1) * 128],
                        identity=C.ident[:]), reads=[xsm[m], C.ident], writes=[bank])
                emit_copy(P, C.copy_eng(), B["hT"][:, kb * 4:(kb + 1) * 4, m * 128:(m + 1) * 128],
                          bank[:].rearrange("p (k t) -> p k t", k=4), reads=[], writes=[bank, B["hTm"][m]])
            P.op("pool", lambda e, m=m, xs=xs: e.tensor_scalar(out=xs[:, m, :], in0=xs[:, m, :], scalar1=ALPHA,
                                                        scalar2=0.0, op0=ALU.mult, op1=ALU.add),
                 reads=[], writes=[xsm[m]])
        for c in range(NCH):
            if gi + PD < len(items):
                load(gi + PD)
            w = slot_of[gi]
            gi += 1
            G = banks[(c % 2) * 2]
            U = banks[(c % 2) * 2 + 1]
            sg = B["sg"][c % 2]
            for half, bk in ((0, G), (1, U)):
                for ko in range(16):
                    P.op("pe", lambda e, bk=bk, w=w, ko=ko, half=half: e.matmul(
                        bk[:], lhsT=w[:, ko, half * 128:(half + 1) * 128], rhs=B["hT"][:, ko, :],
                        start=(ko == 0), stop=(ko == 15)), reads=[w] + B["hTm"], writes=[bk])
            P.op("act", lambda e, G=G, sg=sg: e.activation(out=sg[:], in_=G[:], func=AF.Silu),
                 reads=[], writes=[G, sg])
            P.op("dve", lambda e, U=U, sg=sg, c=c: e.tensor_tensor(out=B["hid"][:, c, :], in0=U[:], in1=sg[:],
                                                                   op=ALU.mult),
                 reads=[sg], writes=[U, B["hidc"][c]])
        if t + 1 < ntile:
            load_x(t + 1)
        for n in range(4):
            for s in range(4):
                if gi + PD < len(items):
                    load(gi + PD)
                w = slot_of[gi]
                gi += 1
                for m in range(4):
                    acc = banks[4 + m]
                    for cc in range(11):
                        c = s * 11 + cc
                        P.op("pe", lambda e, acc=acc, w=w, c=c, cc=cc, m=m: e.matmul(
                            acc[:], lhsT=B["hid"][:, c, m * 128:(m + 1) * 128], rhs=w[:, cc, :],
                            start=(c == 0), stop=(c == NCH - 1)), reads=[w, B["hidc"][c]], writes=[acc])
            for m in range(4):
                acc = banks[4 + m]
                P.op("dve", lambda e, acc=acc, m=m, n=n, xs=xs: e.scalar_tensor_tensor(
                    out=xs[:, m, n * 512:(n + 1) * 512], in0=acc[:], scalar=0.5,
                    in1=xs[:, m, n * 512:(n + 1) * 512], op0=ALU.mult, op1=ALU.add),
                    reads=[], writes=[acc, xsm[m]])
        emit_ln_store(P, B["ln"], xsm, xs, 4, B["g"], B["b"],
                      lambda m, t=t: x_out.h.ap()[t * TT + m * 128:t * TT + (m + 1) * 128, :],
                      x_out_tiles[t * 4:(t + 1) * 4], evs)


def build_ffn_prog(ntok=NTOK, debug=False):
    nc = bass.Bass("TRN2", target_bir_lowering=False)
    P = Prog(nc)
    x = P.dram("x", [ntok, D], F32, kind="ExternalInput")
    wgu = P.dram("wgu", [NCH, 128, 16 * 256], F32, kind="ExternalInput")
    wo = P.dram("wo", [4, 128, NCH * 512], F32, kind="ExternalInput")
    g = P.dram("g", [D], F32, kind="ExternalInput")
    b = P.dram("b", [D], F32, kind="ExternalInput")
    ident = P.dram("ident", [128, 128], F32, kind="ExternalInput")
    y = P.dram("y", [ntok, D], F32, kind="ExternalOutput")
    C = Common(P, ident)
    B = alloc_ffn(P)
    xt = [P.newT(None, "xin%d" % i) for i in range(ntok // 128)]
    yt = [P.newT(None, "yout%d" % i) for i in range(ntok // 128)]
    evs = []
    emit_ffn(P, C, B, x, xt, y, yt, wgu, wo, g, b, evs, ntok=ntok)
    if debug:
        dh = P.dram("dbg_hT", [128, 16 * 512], BF16, kind="ExternalOutput")
        dhid = P.dram("dbg_hid", [128, NCH * 512], BF16, kind="ExternalOutput")
        evs.append(P.dma("sp", lambda e: e.dma_start(out=dh.h.ap(), in_=B["hT"][:].rearrange("p k t -> p (k t)")),
                         B["hT"], reads=B["hTm"], writes=[]))
        evs.append(P.dma("sp", lambda e: e.dma_start(out=dhid.h.ap(), in_=B["hid"][:].rearrange("p c t -> p (c t)")),
                         B["hid"], reads=B["hidc"], writes=[]))
    P.emit(evs)
    return nc


def lay_wgu(w_in):
    w = np.asarray(w_in, dtype=np.float32).reshape(16, 128, 2, NCH, 128)
    return np.ascontiguousarray(w.transpose(3, 1, 0, 2, 4)).reshape(NCH, 128, 16 * 256)


def lay_wo(w_out):
    w = np.asarray(w_out, dtype=np.float32).reshape(NCH, 128, 4, 512)
    return np.ascontiguousarray(w.transpose(2, 1, 0, 3)).reshape(4, 128, NCH * 512)


_PROGS = {}


def run_ffn(h, w_in, w_out, g, b):
    if "ffn" not in _PROGS:
        _PROGS["ffn"] = build_ffn_prog()
    nc = _PROGS["ffn"]
    wgu = lay_wgu(w_in)
    wo = lay_wo(w_out)
    ident = np.eye(128, dtype=np.float32)
    g = np.ascontiguousarray(g, dtype=np.float32)
    b = np.ascontiguousarray(b, dtype=np.float32)
    in_maps = [{"x": h[i], "wgu": wgu, "wo": wo, "g": g, "b": b, "ident": ident} for i in range(NCORES)]
    res = run_bass_kernel_spmd(nc, in_maps, core_ids=list(range(NCORES)))
    return [res.results[i]["y"] for i in range(NCORES)]


def load_wres(P, wb, w_ap, ncols, kch):
    views = []
    nb = (ncols + 511) // 512
    for b in range(nb):
        c0, c1 = b * 512, min(ncols, (b + 1) * 512)
        v = P.view(wb, "wres%d" % b)
        views.append(v)
        P.dma("pool", lambda e, c0=c0, c1=c1: e.dma_start(
            out=wb[:, :, c0:c1], in_=w_ap.rearrange("p (k n) -> p k n", k=kch)[:, :, c0:c1]),
            v, writes=[v])
    return views


def emit_rope(P, eng, x_h, xT, H, Gt, J, cos_ap, sin_ap, tmp, tmpT, col0=0, hstride=None, tabs=()):
    hs = hstride if hstride is not None else Gt * 2 * J
    W = Gt * 2 * J

    def xv(f):
        if hs == W:
            v = x_h[:, col0:col0 + H * hs].rearrange("p (h g f j) -> p h g f j", h=H, g=Gt, f=2, j=J)
        else:
            v = x_h[:, 0:H * hs].rearrange("p (h w) -> p h w", h=H)[:, :, col0:col0 + W].rearrange(
                "p h (g f j) -> p h g f j", g=Gt, f=2, j=J)
        return v[:, :, :, f, :]

    def tv(i):
        return tmp[:, i * H * Gt * J:(i + 1) * H * Gt * J].rearrange("p (h g j) -> p h g j", h=H, g=Gt, j=J)

    def tb(ap):
        return ap.rearrange("p (g j) -> p g j", g=Gt, j=J).unsqueeze(1).broadcast_to([128, H, Gt, J])

    c, s_ = tb(cos_ap), tb(sin_ap)
    P.op(eng, lambda e: e.tensor_tensor(out=tv(0), in0=xv(0), in1=c, op=ALU.mult), reads=[xT] + list(tabs), writes=[tmpT])
    P.op(eng, lambda e: e.tensor_tensor(out=tv(1), in0=xv(1), in1=s_, op=ALU.mult), reads=[xT] + list(tabs), writes=[tmpT])
    P.op(eng, lambda e: e.tensor_tensor(out=tv(2), in0=xv(1), in1=c, op=ALU.mult), reads=[xT] + list(tabs), writes=[tmpT])
    P.op(eng, lambda e: e.tensor_tensor(out=tv(3), in0=xv(0), in1=s_, op=ALU.mult), reads=[xT] + list(tabs), writes=[tmpT])
    P.op(eng, lambda e: e.tensor_tensor(out=xv(0), in0=tv(0), in1=tv(1), op=ALU.subtract), reads=[tmpT], writes=[xT])
    P.op(eng, lambda e: e.tensor_tensor(out=xv(1), in0=tv(2), in1=tv(3), op=ALU.add), reads=[tmpT], writes=[xT])


def emit_proj_pass(P, C, x_in, row0_fn, ntiles, wviews, wb, ncols, post):
    banks = C.banks
    xs = [P.sb("p_xs%d" % i, [128, D], F32) for i in range(2)]
    hT = [P.sb("p_hT%d" % i, [128, 16, 128], BF16) for i in range(2)]
    qkv = [P.sb("p_qkv%d" % i, [128, ncols], F32) for i in range(2)]
    nb = (ncols + 511) // 512

    def load_x(i):
        x = xs[i % 2]
        r0 = row0_fn(i)
        P.dma("sp", lambda e: e.dma_start(out=x[:], in_=x_in.h.ap()[r0:r0 + 128, :]), x, writes=[x])

    load_x(0)
    for i in range(ntiles):
        if i + 1 < ntiles:
            load_x(i + 1)
        x, h, q = xs[i % 2], hT[i % 2], qkv[i % 2]
        for kb in range(4):
            bank = banks[4 + kb]
            for j in range(4):
                k = kb * 4 + j
                P.op("pe", lambda e, bank=bank, k=k, j=j, x=x: e.transpose(
                    out=bank[:, j * 128:(j + 1) * 128], in_=x[:, k * 128:(k + 1) * 128], identity=C.ident[:]),
                    reads=[x, C.ident], writes=[bank])
            emit_copy(P, C.copy_eng(), h[:, kb * 4:(kb + 1) * 4, :], bank[:].rearrange("p (k t) -> p k t", k=4),
                      reads=[], writes=[bank, h])
        for b in range(nb):
            c0, c1 = b * 512, min(ncols, (b + 1) * 512)
            bank = banks[b % 4]
            for k in range(16):
                P.op("pe", lambda e, bank=bank, k=k, c0=c0, c1=c1, h=h: e.matmul(
                    bank[:, 0:c1 - c0], lhsT=h[:, k, :], rhs=wb[:, k, c0:c1], start=(k == 0), stop=(k == 15)),
                    reads=[h, wviews[b]], writes=[bank])
            emit_copy(P, C.copy_eng(), q[:, c0:c1], bank[:, 0:c1 - c0], reads=[], writes=[bank, q])
        post(i, q)


def emit_headT(P, C, src, srcT, col0, nh, width, stage, stageT, h0, m):
    banks = C.banks
    for g0 in range(0, nh, 4):
        g = min(4, nh - g0)
        bank = banks[4 + (C.cp % 4)]
        for j in range(g):
            c = col0[g0 + j]
            P.op("pe", lambda e, bank=bank, j=j, c=c: e.transpose(
                out=bank[0:width, j * 128:(j + 1) * 128], in_=src[:, c:c + width], identity=C.ident[:]),
                reads=[srcT, C.ident], writes=[bank])
        emit_copy(P, C.copy_eng(), stage[0:width, h0 + g0:h0 + g0 + g, m * 128:(m + 1) * 128],
                  bank[0:width, 0:g * 128].rearrange("p (k t) -> p k t", k=g), reads=[], writes=[bank, stageT])


def emit_attn_core(P, C, nseq, nout, sources_fn, scale, OT_d, masks=None):
    banks = C.banks
    NSRC = max(len(sources_fn(h)) for h in range(nout))
    kt_sb = [[P.sb("a_kt%d_%d" % (b, s), [128, SEQ], BF16) for s in range(NSRC)] for b in range(2)]
    v_sb = [[P.sb("a_v%d_%d" % (b, s), [128, 16, 128], BF16) for s in range(NSRC)] for b in range(2)]
    q_sb = [[P.sb("a_q%d_%d" % (b, s), [128, 512], BF16) for s in range(NSRC)] for b in range(2)]
    has_r = any(src.get("qr") is not None for src in sources_fn(0))
    if has_r:
        kr_sb = [P.sb("a_kr%d" % b, [64, SEQ], BF16) for b in range(2)]
        qr_sb = [P.sb("a_qr%d" % b, [64, 512], BF16) for b in range(2)]
    NPT = 6
    pt = [P.sb("a_pt%d" % i, [128, 512], BF16) for i in range(NPT)]
    ones = P.sb("a_ones", [128, 128], BF16)
    P.op("pool", lambda e: e.memset(ones[:], 1.0), writes=[ones])
    rden = [P.sb("a_rden%d" % i, [128, 512], F32) for i in range(2)]
    ot = [P.sb("a_ot%d" % i, [128, 512], BF16) for i in range(2)]
    ctr = {"hk": 0, "qk": 0, "pk": 0, "sk": 0}

    def qtile(srcs, kb, t0, h, qt):
        q0 = t0 + qt * 512
        qb = ctr["qk"] % 2
        ctr["qk"] += 1
        qk = ctr["qk"]
        for si, src in enumerate(srcs):
            qd, qi = src["q"]
            P.dma("sp", lambda e, si=si, qd=qd, qi=qi: e.dma_start(out=q_sb[qb][si][:], in_=qd.h.ap()[qi][:, q0:q0 + 512]),
                  q_sb[qb][si], writes=[q_sb[qb][si]])
        if has_r:
            qd, qi = srcs[0]["qr"]
            P.dma("sp", lambda e, qd=qd, qi=qi: e.dma_start(out=qr_sb[qb][:], in_=qd.h.ap()[qi][:, q0:q0 + 512]),
                  qr_sb[qb], writes=[qr_sb[qb]])
        work = []
        for si, src in enumerate(srcs):
            g = src.get("mask")
            if g is None:
                kts = list(range(16))
            else:
                Wd = masks["W"][g]
                kts = [kt for kt in range(16) if -Wd - 127 <= 128 * kt - 512 * qt <= Wd + 511]
            for kt in kts:
                work.append((si, kt, g))
        num = banks[4 + 2 * (qk % 2)]
        den = banks[5 + 2 * (qk % 2)]
        nw = len(work)
        sk = ctr["sk"]

        def issue_s(w):
            si, kt, g = work[w]
            bank = banks[(sk + w) % 4]
            P.op("pe", lambda e: e.matmul(
                bank[:], lhsT=kt_sb[kb][si][:, kt * 128:(kt + 1) * 128], rhs=q_sb[qb][si][:],
                start=True, stop=not has_r), reads=[kt_sb[kb][si], q_sb[qb][si]], writes=[bank])
            if has_r:
                P.op("pe", lambda e: e.matmul(
                    bank[:], lhsT=kr_sb[kb][:, kt * 128:(kt + 1) * 128], rhs=qr_sb[qb][:],
                    start=False, stop=True), reads=[kr_sb[kb], qr_sb[qb]], writes=[bank])

        def consume(w):
            si, kt, g = work[w]
            bank = banks[(sk + w) % 4]
            p = pt[ctr["pk"] % NPT]
            ctr["pk"] += 1
            P.op("act", lambda e: e.activation(out=p[:], in_=bank[:], func=AF.Exp, scale=scale),
                 reads=[], writes=[bank, p])
            if g is not None:
                off = masks["CMAX"][g] - (128 * kt - 512 * qt)
                mt = masks["tab"][g]
                P.op("dve", lambda e: e.tensor_tensor(out=p[:], in0=p[:], in1=mt[:, off:off + 512], op=ALU.mult),
                     reads=[mt], writes=[p])
            P.op("pe", lambda e: e.matmul(num[:], lhsT=v_sb[kb][si][:, kt, :], rhs=p[:], start=(w == 0),
                                          stop=(w == nw - 1)), reads=[v_sb[kb][si], p], writes=[num])
            P.op("pe", lambda e: e.matmul(den[:], lhsT=ones[:], rhs=p[:], start=(w == 0), stop=(w == nw - 1)),
                 reads=[ones, p], writes=[den])

        issue_s(0)
        for w in range(nw):
            if w + 1 < nw:
                issue_s(w + 1)
            consume(w)
        ctr["sk"] += nw
        rd = rden[qk % 2]
        o = ot[qk % 2]
        P.op("dve", lambda e: e.reciprocal(out=rd[:], in_=den[:]), reads=[], writes=[den, rd])
        P.op("dve", lambda e: e.tensor_tensor(out=o[:], in0=num[:], in1=rd[:], op=ALU.mult),
             reads=[rd], writes=[num, o])
        P.dma("sp", lambda e: e.dma_start(out=OT_d.h.ap()[h][:, q0:q0 + 512], in_=o[:]), o, reads=[o])

    def head(sq, h):
        t0 = sq * SEQ
        srcs = sources_fn(h)
        kb = ctr["hk"] % 2
        ctr["hk"] += 1
        for si, src in enumerate(srcs):
            kd, ki = src["k"]
            vd, vc = src["v"]
            P.dma("sp", lambda e, si=si, kd=kd, ki=ki: e.dma_start(out=kt_sb[kb][si][:], in_=kd.h.ap()[ki][:, t0:t0 + SEQ]),
                  kt_sb[kb][si], writes=[kt_sb[kb][si]])
            P.dma("sp", lambda e, si=si, vd=vd, vc=vc: e.dma_start(
                out=v_sb[kb][si][:], in_=vd.h.ap()[t0:t0 + SEQ, vc:vc + 128].rearrange("(t p) d -> p t d", p=128)),
                v_sb[kb][si], writes=[v_sb[kb][si]])
        if has_r:
            kd, ki = srcs[0]["kr"]
            P.dma("sp", lambda e, kd=kd, ki=ki: e.dma_start(out=kr_sb[kb][:], in_=kd.h.ap()[ki][0:64, t0:t0 + SEQ]),
                  kr_sb[kb], writes=[kr_sb[kb]])
        for qt in range(SEQ // 512):
            qtile(srcs, kb, t0, h, qt)

    for sq in range(nseq):
        for h in range(nout):
            head(sq, h)


def emit_oproj(P, C, x_in, x_out, OT_d, nh, wo_d, g_d, b_d, evs, ntok=NTOK):
    banks = C.banks
    wo = P.sb("o_wo", [128, nh, D], BF16)
    wov = load_wres(P, wo, wo_d.h.ap(), D, nh)
    g_t = P.sb("o_g", [128, D], F32)
    b_t = P.sb("o_b", [128, D], F32)
    P.dma("sp", lambda e: e.dma_start(out=g_t[:], in_=g_d.h.ap().partition_broadcast(128)), g_t, writes=[g_t])
    P.dma("sp", lambda e: e.dma_start(out=b_t[:], in_=b_d.h.ap().partition_broadcast(128)), b_t, writes=[b_t])
    xs = [P.sb("o_xs%d" % i, [128, 4, D], F32) for i in range(2)]
    xsm = [[P.view(xs[i], "o_xs%d_%d" % (i, m)) for m in range(4)] for i in range(2)]
    ots = [P.sb("o_ot%d" % i, [128, nh, 512], BF16) for i in range(2)]
    L = alloc_ln(P, "o_", 4)
    TT = 512
    dummy = [P.newT(None, "oy%d" % m) for m in range(4)]
    for t in range(ntok // TT):
        b = t % 2
        x, xm, o = xs[b], xsm[b], ots[b]
        P.dma("sp", lambda e, x=x, t=t: e.dma_start(
            out=x[:], in_=x_in.h.ap()[t * TT:(t + 1) * TT, :].rearrange("(m p) d -> p m d", p=128)), xm[0], writes=xm)
        P.dma("sp", lambda e, o=o, t=t: e.dma_start(
            out=o[:], in_=OT_d.h.ap().rearrange("h d t -> d h t")[:, :, t * TT:(t + 1) * TT]), o, writes=[o])
        for m in range(4):
            P.op("pool", lambda e, m=m, x=x: e.tensor_scalar(out=x[:, m, :], in0=x[:, m, :], scalar1=ALPHA, scalar2=0.0,
                                                        op0=ALU.mult, op1=ALU.add), reads=[], writes=[xm[m]])
        for n in range(4):
            for m in range(4):
                acc = banks[(n * 4 + m) % 8]
                for hh in range(nh):
                    P.op("pe", lambda e, acc=acc, hh=hh, m=m, n=n, o=o: e.matmul(
                        acc[:], lhsT=o[:, hh, m * 128:(m + 1) * 128], rhs=wo[:, hh, n * 512:(n + 1) * 512],
                        start=(hh == 0), stop=(hh == nh - 1)), reads=[o, wov[n]], writes=[acc])
                P.op("dve", lambda e, acc=acc, m=m, n=n, x=x: e.tensor_tensor(
                    out=x[:, m, n * 512:(n + 1) * 512], in0=acc[:], in1=x[:, m, n * 512:(n + 1) * 512], op=ALU.add),
                    reads=[], writes=[acc, xm[m]])
        emit_ln_store(P, L, xm, x, 4, g_t, b_t,
                      lambda m, t=t: x_out.h.ap()[t * TT + m * 128:t * TT + (m + 1) * 128, :], dummy, evs)


def emit_mixer_a(P, C, x_in, x_out, S, w_in_d, qg_d, kg_d, w_out_d, g_d, b_d, cosA_d, sinA_d, evs, ntok=NTOK):
    mk = P.mark()
    wb = P.sb("a_wb", [128, 16, 3072], BF16)
    wv = load_wres(P, wb, w_in_d.h.ap(), 3072, 16)
    gq = P.sb("a_gq", [128, 128], F32)
    gk = P.sb("a_gk", [128, 128], F32)
    P.dma("sp", lambda e: e.dma_start(out=gq[:], in_=qg_d.h.ap().partition_broadcast(128)), gq, writes=[gq])
    P.dma("sp", lambda e: e.dma_start(out=gk[:], in_=kg_d.h.ap().partition_broadcast(128)), gk, writes=[gk])
    cs = P.sb("a_cos", [128, 16, 64], F32)
    sn = P.sb("a_sin", [128, 16, 64], F32)
    P.dma("sp", lambda e: e.dma_start(out=cs[:], in_=cosA_d.h.ap().rearrange("(t p) j -> p t j", p=128)), cs, writes=[cs])
    P.dma("sp", lambda e: e.dma_start(out=sn[:], in_=sinA_d.h.ap().rearrange("(t p) j -> p t j", p=128)), sn, writes=[sn])
    ss = P.sb("a_ss", [128, 20], F32)
    tmp = P.sb("a_tmp", [128, 4 * 20 * 64], F32)
    sq = tmp
    stq = [P.sb("a_stq%d" % i, [128, 20, 512], BF16) for i in range(1)]
    vst = [P.sb("a_vst%d" % i, [128, 512], BF16) for i in range(2)]

    def post(i, q):
        m = i % 4
        st = stq[0]
        tpos = i % 16
        P.op("dve", lambda e: e.tensor_tensor(out=sq[:, 0:2560], in0=q[:, 0:2560], in1=q[:, 0:2560], op=ALU.mult),
             reads=[q], writes=[sq])
        P.op("dve", lambda e: e.tensor_reduce(out=ss[:], in_=sq[:, 0:2560].rearrange("p (h d) -> p h d", h=20),
                                              axis=mybir.AxisListType.X, op=ALU.add), reads=[sq], writes=[ss])
        P.op("dve", lambda e: e.tensor_scalar(out=ss[:], in0=ss[:], scalar1=1.0 / 128, scalar2=RMS_EPS,
                                              op0=ALU.mult, op1=ALU.add), reads=[], writes=[ss])
        P.op("act", lambda e: e.sqrt(out=ss[:], in_=ss[:]), reads=[], writes=[ss])
        P.op("dve", lambda e: e.reciprocal(out=ss[:], in_=ss[:]), reads=[], writes=[ss])
        q3 = q[:, 0:2560].rearrange("p (h d) -> p h d", h=20)
        P.op("dve", lambda e: e.tensor_tensor(out=q3, in0=q3, in1=ss[:].unsqueeze(2).broadcast_to([128, 20, 128]),
                                              op=ALU.mult), reads=[ss], writes=[q])
        P.op("pool", lambda e: e.tensor_tensor(out=q3[:, 0:16, :], in0=q3[:, 0:16, :],
                                               in1=gq[:].unsqueeze(1).broadcast_to([128, 16, 128]), op=ALU.mult),
             reads=[gq], writes=[q])
        P.op("pool", lambda e: e.tensor_tensor(out=q3[:, 16:20, :], in0=q3[:, 16:20, :],
                                               in1=gk[:].unsqueeze(1).broadcast_to([128, 4, 128]), op=ALU.mult),
             reads=[gk], writes=[q])
        emit_rope(P, "dve", q, q, 20, 2, 32, cs[:, tpos, :], sn[:, tpos, :], tmp, tmp, tabs=[cs, sn])
        emit_headT(P, C, q, q, [hh * 128 for hh in range(20)], 20, 128, st, st, 0, m)
        vs = vst[i % 2]
        P.op("act", lambda e: e.copy(out=vs[:], in_=q[:, 2560:3072]), reads=[q], writes=[vs])
        P.dma("sp", lambda e: e.dma_start(out=S["V"].h.ap()[i * 128:(i + 1) * 128, :], in_=vs[:]), vs, reads=[vs])
        if m == 3:
            t0 = (i // 4) * 512
            P.dma("sp", lambda e: e.dma_start(out=S["QT"].h.ap().rearrange("h d t -> d h t")[:, :, t0:t0 + 512],
                                              in_=st[:, 0:16, :]), st, reads=[st])
            P.dma("sp", lambda e: e.dma_start(out=S["KT"].h.ap().rearrange("h d t -> d h t")[:, :, t0:t0 + 512],
                                              in_=st[:, 16:20, :]), st, reads=[st])

    emit_proj_pass(P, C, x_in, lambda i: i * 128, ntok // 128, wv, wb, 3072, post)
    P.release(mk)
    mk = P.mark()
    emit_attn_core(P, C, ntok // SEQ, 16,
                   lambda h: [dict(q=(S["QT"], h), k=(S["KT"], h // 4), v=(S["V"], (h // 4) * 128))],
                   128 ** -0.5, S["OT"])
    P.release(mk)
    mk = P.mark()
    emit_oproj(P, C, x_in, x_out, S["OT"], 16, w_out_d, g_d, b_d, evs, ntok=ntok)
    P.release(mk)


def rope_tables_axial():
    pos = np.arange(SEQ)
    inv = 10000.0 ** (-np.arange(0, 64, 2, dtype=np.float32) / 64)
    ar = (pos // 64).astype(np.float32)[:, None] * inv[None, :]
    ac = (pos % 64).astype(np.float32)[:, None] * inv[None, :]
    cos = np.concatenate([np.cos(ar), np.cos(ac)], 1).astype(np.float32)
    sin = np.concatenate([np.sin(ar), np.sin(ac)], 1).astype(np.float32)
    return cos, sin


def lay_kn(w, kch):
    w = np.asarray(w, dtype=np.float32)
    n = w.shape[1]
    return np.ascontiguousarray(w.reshape(kch, 128, n).transpose(1, 0, 2)).reshape(128, kch * n)


def build_mixa_prog(ntok=NTOK):
    nc = bass.Bass("TRN2", target_bir_lowering=False)
    P = Prog(nc)
    x = P.dram("x", [ntok, D], F32, kind="ExternalInput")
    w_in = P.dram("w_in", [128, 16 * 3072], F32, kind="ExternalInput")
    w_out = P.dram("w_out", [128, 16 * 2048], F32, kind="ExternalInput")
    qg = P.dram("qg", [128], F32, kind="ExternalInput")
    kg = P.dram("kg", [128], F32, kind="ExternalInput")
    g = P.dram("g", [D], F32, kind="ExternalInput")
    b = P.dram("b", [D], F32, kind="ExternalInput")
    cosA = P.dram("cosA", [SEQ, 64], F32, kind="ExternalInput")
    sinA = P.dram("sinA", [SEQ, 64], F32, kind="ExternalInput")
    ident = P.dram("ident", [128, 128], F32, kind="ExternalInput")
    y = P.dram("y", [ntok, D], F32, kind="ExternalOutput")
    S = {"QT": P.dram("s_qt", [16, 128, ntok], BF16), "KT": P.dram("s_kt", [4, 128, ntok], BF16),
         "V": P.dram("s_v", [ntok, 512], BF16), "OT": P.dram("s_ot", [16, 128, ntok], BF16)}
    P.init_pool(40)
    C = Common(P, ident)
    P.persist += P.stage_tiles
    evs = []
    emit_mixer_a(P, C, x, y, S, w_in, qg, kg, w_out, g, b, cosA, sinA, evs, ntok=ntok)
    P.emit(evs)
    return nc


B_W = [64, 256, 1024]
B_DIL = [1, 4, 16]
B_CMAX = [w + 512 for w in B_W]
B_TW = 3200


def mask_tables_b():
    tabs = np.zeros((3, 128, B_TW), dtype=np.float32)
    kk = np.arange(128)[:, None]
    c = np.arange(B_TW)[None, :]
    for g in range(3):
        d = kk - c + B_CMAX[g]
        tabs[g] = ((np.abs(d) <= B_W[g]) & (d % B_DIL[g] == 0)).astype(np.float32)
    return tabs


def rope_tables_seq(dim):
    pos = np.arange(SEQ, dtype=np.float32)
    inv = 10000.0 ** (-np.arange(0, dim, 2, dtype=np.float32) / dim)
    ang = pos[:, None] * inv[None, :]
    return np.cos(ang).astype(np.float32), np.sin(ang).astype(np.float32)


def emit_mixer_b(P, C, x_in, x_out, S, w_in_d, w_out_d, g_d, b_d, cosB_d, sinB_d, mask_d, evs, ntok=NTOK):
    for g in range(3):
        mk = P.mark()
        wb = P.sb("b_wb", [128, 16, 3072], BF16)
        wv = load_wres(P, wb, w_in_d.h.ap()[g], 3072, 16)
        cs = P.sb("b_cos", [128, 16, 64], F32)
        sn = P.sb("b_sin", [128, 16, 64], F32)
        P.dma("sp", lambda e, cs=cs: e.dma_start(out=cs[:], in_=cosB_d.h.ap().rearrange("(t p) j -> p t j", p=128)),
              cs, writes=[cs])
        P.dma("sp", lambda e, sn=sn: e.dma_start(out=sn[:], in_=sinB_d.h.ap().rearrange("(t p) j -> p t j", p=128)),
              sn, writes=[sn])
        tmp = P.sb("b_tmp", [128, 4 * 16 * 64], F32)
        st = P.sb("b_st", [128, 16, 512], BF16)
        vst = [P.sb("b_vst%d" % i, [128, 1024], BF16) for i in range(2)]

        def post(i, q, g=g, cs=cs, sn=sn, tmp=tmp, st=st, vst=vst):
            m = i % 4
            tpos = i % 16
            emit_rope(P, "dve", q, q, 16, 1, 64, cs[:, tpos, :], sn[:, tpos, :], tmp, tmp, tabs=[cs, sn])
            emit_headT(P, C, q, q, [hh * 128 for hh in range(16)], 16, 128, st, st, 0, m)
            vs = vst[i % 2]
            P.op("act", lambda e: e.copy(out=vs[:], in_=q[:, 2048:3072]), reads=[q], writes=[vs])
            P.dma("sp", lambda e: e.dma_start(out=S["V"].h.ap()[i * 128:(i + 1) * 128, g * 1024:(g + 1) * 1024],
                                              in_=vs[:]), vs, reads=[vs])
            if m == 3:
                t0 = (i // 4) * 512
                P.dma("sp", lambda e: e.dma_start(
                    out=S["QT"].h.ap().rearrange("h d t -> d h t")[:, g * 8:(g + 1) * 8, t0:t0 + 512],
                    in_=st[:, 0:8, :]), st, reads=[st])
                P.dma("sp", lambda e: e.dma_start(
                    out=S["KT"].h.ap().rearrange("h d t -> d h t")[:, g * 8:(g + 1) * 8, t0:t0 + 512],
                    in_=st[:, 8:16, :]), st, reads=[st])

        emit_proj_pass(P, C, x_in, lambda i: i * 128, ntok // 128, wv, wb, 3072, post)
        P.release(mk)
    mk = P.mark()
    tabs = []
    for g in range(3):
        mt = P.sb("b_mask%d" % g, [128, B_TW], BF16)
        P.dma("pool", lambda e, mt=mt, g=g: e.dma_start(out=mt[:], in_=mask_d.h.ap()[g]), mt, writes=[mt])
        tabs.append(mt)
    masks = {"W": B_W, "CMAX": B_CMAX, "tab": tabs}
    emit_attn_core(P, C, ntok // SEQ, 8,
                   lambda h: [dict(q=(S["QT"], g * 8 + h), k=(S["KT"], g * 8 + h), v=(S["V"], g * 1024 + h * 128), mask=g)
                              for g in range(3)],
                   128 ** -0.5, S["OT"], masks=masks)
    P.release(mk)
    mk = P.mark()
    emit_oproj(P, C, x_in, x_out, S["OT"], 8, w_out_d, g_d, b_d, evs, ntok=ntok)
    P.release(mk)


def lay_w_in_b(w):
    w = np.asarray(w, dtype=np.float32).reshape(16, 128, 3, 3072)
    return np.ascontiguousarray(w.transpose(2, 1, 0, 3)).reshape(3, 128, 16 * 3072)


def build_mixb_prog(ntok=NTOK):
    nc = bass.Bass("TRN2", target_bir_lowering=False)
    P = Prog(nc)
    x = P.dram("x", [ntok, D], F32, kind="ExternalInput")
    w_in = P.dram("w_in", [3, 128, 16 * 3072], F32, kind="ExternalInput")
    w_out = P.dram("w_out", [128, 8 * 2048], F32, kind="ExternalInput")
    g = P.dram("g", [D], F32, kind="ExternalInput")
    b = P.dram("b", [D], F32, kind="ExternalInput")
    cosB = P.dram("cosB", [SEQ, 64], F32, kind="ExternalInput")
    sinB = P.dram("sinB", [SEQ, 64], F32, kind="ExternalInput")
    maskd = P.dram("maskB", [3, 128, B_TW], F32, kind="ExternalInput")
    ident = P.dram("ident", [128, 128], F32, kind="ExternalInput")
    y = P.dram("y", [ntok, D], F32, kind="ExternalOutput")
    S = {"QT": P.dram("s_qt", [24, 128, ntok], BF16), "KT": P.dram("s_kt", [24, 128, ntok], BF16),
         "V": P.dram("s_v", [ntok, 3072], BF16), "OT": P.dram("s_ot", [8, 128, ntok], BF16)}
    P.init_pool(40)
    C = Common(P, ident)
    P.persist += P.stage_tiles
    evs = []
    emit_mixer_b(P, C, x, y, S, w_in, w_out, g, b, cosB, sinB, maskd, evs, ntok=ntok)
    P.emit(evs)
    return nc


def emit_mixer_c(P, C, x_in, x_out, S, w_in_d, qg_d, kvg_d, wq_d, wkv_d, w_out_d, g_d, b_d, cosC_d, sinC_d, evs,
                 ntok=NTOK, only_p1=False):
    mk = P.mark()
    wb = P.sb("c_wb", [128, 16, 1536], BF16)
    wv = load_wres(P, wb, w_in_d.h.ap(), 1536, 16)
    gq = P.sb("c_gq", [128, 1280], F32)
    P.dma("sp", lambda e: e.dma_start(out=gq[:, 0:768], in_=qg_d.h.ap().partition_broadcast(128)), gq, writes=[gq])
    P.dma("sp", lambda e: e.dma_start(out=gq[:, 768:1280], in_=kvg_d.h.ap().partition_broadcast(128)), gq, writes=[gq])
    cs = P.sb("c_cos", [128, 16, 32], F32)
    sn = P.sb("c_sin", [128, 16, 32], F32)
    P.dma("sp", lambda e: e.dma_start(out=cs[:], in_=cosC_d.h.ap().rearrange("(t p) j -> p t j", p=128)), cs, writes=[cs])
    P.dma("sp", lambda e: e.dma_start(out=sn[:], in_=sinC_d.h.ap().rearrange("(t p) j -> p t j", p=128)), sn, writes=[sn])
    tmp = P.sb("c_tmp", [128, 1280], F32)
    ss = P.sb("c_ss", [128, 2], F32)
    stl = P.sb("c_stl", [128, 10, 512], BF16)
    stk4 = P.sb("c_stk4", [64, 4, 512], BF16)
    ctab = [P.sb("c_ctab%d" % i, [128, 64], F32) for i in range(2)]

    def post1(i, q):
        m = i % 4
        tpos = i % 16
        P.op("dve", lambda e: e.tensor_tensor(out=tmp[:], in0=q[:, 0:1280], in1=q[:, 0:1280], op=ALU.mult),
             reads=[q], writes=[tmp])
        P.op("dve", lambda e: e.tensor_reduce(out=ss[:, 0:1], in_=tmp[:, 0:768], axis=mybir.AxisListType.X, op=ALU.add),
             reads=[tmp], writes=[ss])
        P.op("dve", lambda e: e.tensor_reduce(out=ss[:, 1:2], in_=tmp[:, 768:1280], axis=mybir.AxisListType.X,
                                              op=ALU.add), reads=[tmp], writes=[ss])
        P.op("dve", lambda e: e.tensor_scalar(out=ss[:, 0:1], in0=ss[:, 0:1], scalar1=1.0 / 768, scalar2=RMS_EPS,
                                              op0=ALU.mult, op1=ALU.add), reads=[], writes=[ss])
        P.op("dve", lambda e: e.tensor_scalar(out=ss[:, 1:2], in0=ss[:, 1:2], scalar1=1.0 / 512, scalar2=RMS_EPS,
                                              op0=ALU.mult, op1=ALU.add), reads=[], writes=[ss])
        P.op("act", lambda e: e.sqrt(out=ss[:], in_=ss[:]), reads=[], writes=[ss])
        P.op("dve", lambda e: e.reciprocal(out=ss[:], in_=ss[:]), reads=[], writes=[ss])
        P.op("dve", lambda e: e.scalar_tensor_tensor(out=q[:, 0:768], in0=q[:, 0:768], scalar=ss[:, 0:1],
                                                     in1=gq[:, 0:768], op0=ALU.mult, op1=ALU.mult),
             reads=[ss, gq], writes=[q])
        P.op("dve", lambda e: e.scalar_tensor_tensor(out=q[:, 768:1280], in0=q[:, 768:1280], scalar=ss[:, 1:2],
                                                     in1=gq[:, 768:1280], op0=ALU.mult, op1=ALU.mult),
             reads=[ss, gq], writes=[q])
        x0, x1 = q[:, 1280:1312], q[:, 1312:1344]
        ct = ctab[i % 2]
        P.dma("sp", lambda e: e.dma_start(out=ct[:, 0:32], in_=cosC_d.h.ap()[tpos * 128:(tpos + 1) * 128, :]), ct, writes=[ct])
        P.dma("sp", lambda e: e.dma_start(out=ct[:, 32:64], in_=sinC_d.h.ap()[tpos * 128:(tpos + 1) * 128, :]), ct, writes=[ct])
        cc, sc = ct[:, 0:32], ct[:, 32:64]
        cs = sn = ct
        if "dbg_early" in S and i == 5:
            evs.append(P.dma("sp", lambda e: e.dma_start(out=S["dbg_q"].h.ap()[:, 0, :], in_=q[:, 1280:1344]), q, reads=[q]))
            evs.append(P.dma("sp", lambda e: e.dma_start(out=S["dbg_q"].h.ap()[:, 2, 0:32], in_=cc), cs, reads=[cs]))
            evs.append(P.dma("sp", lambda e: e.dma_start(out=S["dbg_q"].h.ap()[:, 2, 32:64], in_=sc), sn, reads=[sn]))
        tt = [tmp[:, j * 32:(j + 1) * 32] for j in range(4)]
        P.op("dve", lambda e: e.tensor_tensor(out=tt[0], in0=x0, in1=cc, op=ALU.mult), reads=[q, cs], writes=[tmp])
        P.op("dve", lambda e: e.tensor_tensor(out=tt[1], in0=x1, in1=sc, op=ALU.mult), reads=[q, sn], writes=[tmp])
        P.op("dve", lambda e: e.tensor_tensor(out=tt[2], in0=x1, in1=cc, op=ALU.mult), reads=[q, cs], writes=[tmp])
        P.op("dve", lambda e: e.tensor_tensor(out=tt[3], in0=x0, in1=sc, op=ALU.mult), reads=[q, sn], writes=[tmp])
        P.op("pool", lambda e: e.tensor_tensor(out=x0, in0=tt[0], in1=tt[1], op=ALU.subtract), reads=[tmp], writes=[q])
        P.op("pool", lambda e: e.tensor_tensor(out=x1, in0=tt[2], in1=tt[3], op=ALU.add), reads=[tmp], writes=[q])
        if "dbg_early" in S and i == 5:
            evs.append(P.dma("sp", lambda e: e.dma_start(out=S["dbg_q"].h.ap()[:, 1, :], in_=q[:, 1280:1344]), q, reads=[q]))
            evs.append(P.dma("sp", lambda e: e.dma_start(out=S["dbg_q"].h.ap()[:, 3, :], in_=tmp[:, 0:64]), tmp, reads=[tmp]))
        emit_headT(P, C, q, q, [k * 128 for k in range(10)], 10, 128, stl, stl, 0, m)
        emit_headT(P, C, q, q, [1280, 1280, 1280, 1280], 4, 64, stk4, stk4, 0, m)
        if m == 3:
            t0 = (i // 4) * 512
            P.dma("sp", lambda e: e.dma_start(out=S["LT"].h.ap().rearrange("h d t -> d h t")[:, 0:10, t0:t0 + 512],
                                              in_=stl[:]), stl, reads=[stl])
            P.dma("sp", lambda e: e.dma_start(out=S["LT"].h.ap()[10][0:64, t0:t0 + 512], in_=stk4[:, 0, :]),
                  stk4, reads=[stk4])

    emit_proj_pass(P, C, x_in, lambda i: i * 128, ntok // 128, wv, wb, 1536, post1)
    P.release(mk)
    if "dbg_early" in S:
        scr_e = P.sb("dbg_early_scr", [1, 8], F32)
        evs.append(P.dma("sp", lambda e: e.dma_start(out=S["dbg_early"].h.ap(), in_=S["LT"].h.ap()[10]), scr_e))
    if only_p1:
        return
    mk = P.mark()
    banks = C.banks
    wq = P.sb("c_wq", [128, 6, 3072], BF16)
    wqv = load_wres(P, wq, wq_d.h.ap(), 3072, 6)
    wkv = P.sb("c_wkv", [128, 4, 4096], BF16)
    wkvv = load_wres(P, wkv, wkv_d.h.ap(), 4096, 4)
    cs = P.sb("c_cos2", [128, 16, 32], F32)
    sn = P.sb("c_sin2", [128, 16, 32], F32)
    P.dma("sp", lambda e: e.dma_start(out=cs[:], in_=cosC_d.h.ap().rearrange("(t p) j -> p t j", p=128)), cs, writes=[cs])
    P.dma("sp", lambda e: e.dma_start(out=sn[:], in_=sinC_d.h.ap().rearrange("(t p) j -> p t j", p=128)), sn, writes=[sn])
    lat = [P.sb("c_lat%d" % i, [128, 11, 512], BF16) for i in range(2)]
    q2 = P.sb("c_q2", [128, 3072], F32)
    kv2 = P.sb("c_kv2", [128, 4096], F32)
    tmp2 = P.sb("c_tmp2", [128, 4 * 16 * 32], F32)
    sqt = P.sb("c_sqt", [128, 16, 512], BF16)
    sqr = P.sb("c_sqr", [64, 16, 512], BF16)
    skt = P.sb("c_skt", [128, 16, 512], BF16)
    vst = [P.sb("c_vst%d" % i, [128, 16, 128], BF16) for i in range(2)]

    def tile2(i):
        m = i % 4
        tpos = i % 16
        la = lat[(i // 4) % 2]
        if m == 0:
            t0 = (i // 4) * 512
            P.dma("sp", lambda e: e.dma_start(out=la[:], in_=S["LT"].h.ap().rearrange("h d t -> d h t")[:, :, t0:t0 + 512]),
                  la, writes=[la])
        for b in range(6):
            bank = banks[b % 4]
            for k in range(6):
                P.op("pe", lambda e, bank=bank, k=k, b=b: e.matmul(
                    bank[:], lhsT=la[:, k, m * 128:(m + 1) * 128], rhs=wq[:, k, b * 512:(b + 1) * 512],
                    start=(k == 0), stop=(k == 5)), reads=[la, wqv[b]], writes=[bank])
            emit_copy(P, C.copy_eng(), q2[:, b * 512:(b + 1) * 512], bank[:], reads=[], writes=[bank, q2])
        for b in range(8):
            bank = banks[(b + 2) % 4]
            for k in range(4):
                P.op("pe", lambda e, bank=bank, k=k, b=b: e.matmul(
                    bank[:], lhsT=la[:, 6 + k, m * 128:(m + 1) * 128], rhs=wkv[:, k, b * 512:(b + 1) * 512],
                    start=(k == 0), stop=(k == 3)), reads=[la, wkvv[b]], writes=[bank])
            emit_copy(P, C.copy_eng(), kv2[:, b * 512:(b + 1) * 512], bank[:], reads=[], writes=[bank, kv2])
        emit_rope(P, "dve", q2, q2, 16, 1, 32, cs[:, tpos, :], sn[:, tpos, :], tmp2, tmp2, col0=128, hstride=192, tabs=[cs, sn])
        emit_headT(P, C, q2, q2, [hh * 192 for hh in range(16)], 16, 128, sqt, sqt, 0, m)
        emit_headT(P, C, q2, q2, [hh * 192 + 128 for hh in range(16)], 16, 64, sqr, sqr, 0, m)
        emit_headT(P, C, kv2, kv2, [hh * 256 for hh in range(16)], 16, 128, skt, skt, 0, m)
        vs = vst[i % 2]
        P.op("act", lambda e: e.copy(out=vs[:], in_=kv2[:].rearrange("p (h c) -> p h c", h=16)[:, :, 128:256]),
             reads=[kv2], writes=[vs])
        P.dma("sp", lambda e: e.dma_start(out=S["V"].h.ap()[i * 128:(i + 1) * 128, :],
                                          in_=vs[:].rearrange("p h c -> p (h c)")), vs, reads=[vs])
        if m == 3:
            t0 = (i // 4) * 512
            for dst, stg in ((S["QT"], sqt), (S["QR"], sqr), (S["KT"], skt)):
                P.dma("sp", lambda e, dst=dst, stg=stg: e.dma_start(
                    out=dst.h.ap().rearrange("h d t -> d h t")[:, :, t0:t0 + 512], in_=stg[:]), stg, reads=[stg])

    for i in range(ntok // 128):
        tile2(i)
    P.release(mk)
    mk = P.mark()
    emit_attn_core(P, C, ntok // SEQ, 16,
                   lambda h: [dict(q=(S["QT"], h), k=(S["KT"], h), v=(S["V"], h * 128), qr=(S["QR"], h), kr=(S["LT"], 10))],
                   192 ** -0.5, S["OT"])
    P.release(mk)
    mk = P.mark()
    emit_oproj(P, C, x_in, x_out, S["OT"], 16, w_out_d, g_d, b_d, evs, ntok=ntok)
    P.release(mk)


def build_mixc_prog(ntok=NTOK, debug=False):
    nc = bass.Bass("TRN2", target_bir_lowering=False)
    P = Prog(nc)
    x = P.dram("x", [ntok, D], F32, kind="ExternalInput")
    w_in = P.dram("w_in", [128, 16 * 1536], F32, kind="ExternalInput")
    wq = P.dram("wq", [128, 6 * 3072], F32, kind="ExternalInput")
    wkv = P.dram("wkv", [128, 4 * 4096], F32, kind="ExternalInput")
    w_out = P.dram("w_out", [128, 16 * 2048], F32, kind="ExternalInput")
    qg = P.dram("qg", [768], F32, kind="ExternalInput")
    kvg = P.dram("kvg", [512], F32, kind="ExternalInput")
    g = P.dram("g", [D], F32, kind="ExternalInput")
    b = P.dram("b", [D], F32, kind="ExternalInput")
    cosC = P.dram("cosC", [SEQ, 32], F32, kind="ExternalInput")
    sinC = P.dram("sinC", [SEQ, 32], F32, kind="ExternalInput")
    ident = P.dram("ident", [128, 128], F32, kind="ExternalInput")
    y = P.dram("y", [ntok, D], F32, kind="ExternalOutput")
    S = {"LT": P.dram("s_lt", [11, 128, ntok], BF16),
         "QT": P.dram("s_qt", [16, 128, ntok], BF16), "QR": P.dram("s_qr", [16, 64, ntok], BF16),
         "KT": P.dram("s_kt", [16, 128, ntok], BF16), "V": P.dram("s_v", [ntok, 2048], BF16),
         "OT": P.dram("s_ot", [16, 128, ntok], BF16)}
    if debug:
        S["dbg_early"] = P.dram("dbg_early", [128, ntok], BF16, kind="ExternalOutput")
        S["dbg_q"] = P.dram("dbg_q", [128, 4, 64], F32, kind="ExternalOutput")
    P.init_pool(40)
    C = Common(P, ident)
    P.persist += P.stage_tiles
    evs = []
    emit_mixer_c(P, C, x, y, S, w_in, qg, kvg, wq, wkv, w_out, g, b, cosC, sinC, evs, ntok=ntok, only_p1=(debug == 2))
    if debug:
        for k, t in S.items():
            if k in ("dbg_early", "dbg_q"):
                continue
            if debug == 2 and k not in ("LT",):
                continue
            shp = [int(v) for v in t.h.shape]
            do = P.dram("dbg_" + k, shp, BF16, kind="ExternalOutput")
            scr = P.sb("dbgscr_" + k, [1, 8], F32)
            evs.append(P.dma("sp", lambda e, do=do, t=t: e.dma_start(out=do.h.ap(), in_=t.h.ap()), scr, writes=[do]))
    P.emit(evs)
    return nc


def lay_w_in_c(w):
    w = np.asarray(w, dtype=np.float32)
    wp = np.zeros((2048, 1536), dtype=np.float32)
    wp[:, :1344] = w
    return lay_kn(wp, 16)


MIX_KIND = ["a", "b", "c", "a"]


def build_full_prog(ntok=NTOK, nlayers=DEPTH):
    nc = bass.Bass("TRN2", target_bir_lowering=False)
    P = Prog(nc)
    I = {}

    def inp(name, shape):
        I[name] = P.dram(name, shape, F32, kind="ExternalInput")
        return I[name]

    x = inp("x", [ntok, D])
    ident = inp("ident", [128, 128])
    inp("cosA", [SEQ, 64]); inp("sinA", [SEQ, 64])
    inp("cosB", [SEQ, 64]); inp("sinB", [SEQ, 64])
    inp("maskB", [3, 128, B_TW])
    inp("cosC", [SEQ, 32]); inp("sinC", [SEQ, 32])
    for i in range(nlayers):
        for f in (1, 2):
            inp("wgu%d_%d" % (f, i), [NCH, 128, 16 * 256])
            inp("wo%d_%d" % (f, i), [4, 128, NCH * 512])
        for n in (1, 2, 3):
            inp("ln%d_g_%d" % (n, i), [D]); inp("ln%d_b_%d" % (n, i), [D])
        k = MIX_KIND[i]
        if k == "a":
            inp("a_w_in_%d" % i, [128, 16 * 3072]); inp("a_w_out_%d" % i, [128, 16 * 2048])
            inp("a_qg_%d" % i, [128]); inp("a_kg_%d" % i, [128])
        elif k == "b":
            inp("b_w_in_%d" % i, [3, 128, 16 * 3072]); inp("b_w_out_%d" % i, [128, 8 * 2048])
        else:
            inp("c_w_in_%d" % i, [128, 16 * 1536]); inp("c_wq_%d" % i, [128, 6 * 3072])
            inp("c_wkv_%d" % i, [128, 4 * 4096]); inp("c_w_out_%d" % i, [128, 16 * 2048])
            inp("c_qg_%d" % i, [768]); inp("c_kvg_%d" % i, [512])
    y = P.dram("y", [ntok, D], F32, kind="ExternalOutput")
    bufs = [P.dram("hbuf%d" % i, [ntok, D], F32) for i in range(2)]
    SA = {"QT": P.dram("sa_qt", [16, 128, ntok], BF16), "KT": P.dram("sa_kt", [4, 128, ntok], BF16),
          "V": P.dram("sa_v", [ntok, 512], BF16), "OT": P.dram("sa_ot", [16, 128, ntok], BF16)}
    SB = {"QT": P.dram("sb_qt", [24, 128, ntok], BF16), "KT": P.dram("sb_kt", [24, 128, ntok], BF16),
          "V": P.dram("sb_v", [ntok, 3072], BF16), "OT": P.dram("sb_ot", [8, 128, ntok], BF16)}
    SC = {"LT": P.dram("sc_lt", [11, 128, ntok], BF16),
          "QT": P.dram("sc_qt", [16, 128, ntok], BF16), "QR": P.dram("sc_qr", [16, 64, ntok], BF16),
          "KT": P.dram("sc_kt", [16, 128, ntok], BF16), "V": P.dram("sc_v", [ntok, 2048], BF16),
          "OT": P.dram("sc_ot", [16, 128, ntok], BF16)}
    P.init_pool(40)
    C = Common(P, ident)
    P.persist += P.stage_tiles
    evs = []
    nstage = 3 * nlayers
    src = x
    k = 0

    def nxt():
        return y if k == nstage - 1 else bufs[k % 2]

    def ffn(f, i, src, dst):
        mk = P.mark()
        B = alloc_ffn(P)
        xt = [P.newT(None, "xi") for _ in range(ntok // 128)]
        yt = [P.newT(None, "yo") for _ in range(ntok // 128)]
        e2 = []
        emit_ffn(P, C, B, src, xt, dst, yt, I["wgu%d_%d" % (f, i)], I["wo%d_%d" % (f, i)],
                 I["ln%d_g_%d" % (1 if f == 1 else 3, i)], I["ln%d_b_%d" % (1 if f == 1 else 3, i)], e2, ntok=ntok)
        P.release(mk)
        return e2

    for i in range(nlayers):
        dst = nxt()
        last = ffn(1, i, src, dst)
        src = dst
        k += 1
        dst = nxt()
        kind = MIX_KIND[i]
        last = []
        if kind == "a":
            emit_mixer_a(P, C, src, dst, SA, I["a_w_in_%d" % i], I["a_qg_%d" % i], I["a_kg_%d" % i], I["a_w_out_%d" % i],
                         I["ln2_g_%d" % i], I["ln2_b_%d" % i], I["cosA"], I["sinA"], last, ntok=ntok)
        elif kind == "b":
            emit_mixer_b(P, C, src, dst, SB, I["b_w_in_%d" % i], I["b_w_out_%d" % i], I["ln2_g_%d" % i], I["ln2_b_%d" % i],
                         I["cosB"], I["sinB"], I["maskB"], last, ntok=ntok)
        else:
            emit_mixer_c(P, C, src, dst, SC, I["c_w_in_%d" % i], I["c_qg_%d" % i], I["c_kvg_%d" % i], I["c_wq_%d" % i],
                         I["c_wkv_%d" % i], I["c_w_out_%d" % i], I["ln2_g_%d" % i], I["ln2_b_%d" % i],
                         I["cosC"], I["sinC"], last, ntok=ntok)
        src = dst
        k += 1
        dst = nxt()
        last = ffn(2, i, src, dst)
        src = dst
        k += 1
    P.emit(last)
    return nc, P


def host_inputs(inputs, nlayers=DEPTH):
    f32 = lambda a: np.ascontiguousarray(np.asarray(a, dtype=np.float32))
    H = {"ident": np.eye(128, dtype=np.float32)}
    H["cosA"], H["sinA"] = rope_tables_axial()
    H["cosB"], H["sinB"] = rope_tables_seq(128)
    H["cosC"], H["sinC"] = rope_tables_seq(64)
    H["maskB"] = mask_tables_b()
    for i in range(nlayers):
        for f in (1, 2):
            H["wgu%d_%d" % (f, i)] = lay_wgu(inputs["ffn%d_w_in_%d" % (f, i)])
            H["wo%d_%d" % (f, i)] = lay_wo(inputs["ffn%d_w_out_%d" % (f, i)])
        for n in (1, 2, 3):
            H["ln%d_g_%d" % (n, i)] = f32(inputs["ln%d_g_%d" % (n, i)])
            H["ln%d_b_%d" % (n, i)] = f32(inputs["ln%d_b_%d" % (n, i)])
        k = MIX_KIND[i]
        if k == "a":
            H["a_w_in_%d" % i] = lay_kn(inputs["a_w_in_%d" % i], 16)
            H["a_w_out_%d" % i] = lay_kn(inputs["a_w_out_%d" % i], 16)
            H["a_qg_%d" % i] = f32(inputs["a_q_gain_%d" % i])
            H["a_kg_%d" % i] = f32(inputs["a_k_gain_%d" % i])
        elif k == "b":
            H["b_w_in_%d" % i] = lay_w_in_b(inputs["b_w_in_%d" % i])
            H["b_w_out_%d" % i] = lay_kn(inputs["b_w_out_%d" % i], 8)
        else:
            H["c_w_in_%d" % i] = lay_w_in_c(inputs["c_w_in_%d" % i])
            H["c_wq_%d" % i] = lay_kn(inputs["c_w_q_up_%d" % i], 6)
            H["c_wkv_%d" % i] = lay_kn(inputs["c_w_kv_up_%d" % i], 4)
            H["c_w_out_%d" % i] = lay_kn(inputs["c_w_out_%d" % i], 16)
            H["c_qg_%d" % i] = f32(inputs["c_q_gain_%d" % i])
            H["c_kvg_%d" % i] = f32(inputs["c_kv_gain_%d" % i])
    return H


def kernel(**inputs):
    x = np.asarray(inputs["x"], dtype=np.float32).reshape(16 * SEQ, D)
    H = host_inputs(inputs)
    nc, _ = build_full_prog()
    in_maps = []
    for c in range(NCORES):
        m = dict(H)
        m["x"] = np.ascontiguousarray(x[c * NTOK:(c + 1) * NTOK])
        in_maps.append(m)
    res = run_bass_kernel_spmd(nc, in_maps, core_ids=list(range(NCORES)))
    out = np.concatenate([res.results[c]["y"] for c in range(NCORES)], axis=0)
    return out.reshape(16, SEQ, D).astype(np.float32)
```

```python
import numpy as np
import concourse.bass as bass
import concourse.mybir as mybir
from concourse.bass_utils import run_bass_kernel_spmd

F32 = mybir.dt.float32
BF16 = mybir.dt.bfloat16
AF = mybir.ActivationFunctionType
ALU = mybir.AluOpType

NCORES = 8
D = 2048
SEQ = 2048
NTOK = 4096
DFF = 5632
NCH = DFF // 128
DEPTH = 4
ALPHA = (2 * DEPTH) ** 0.25
LN_EPS = 1e-5
RMS_EPS = 1e-6

ENGS = ("pe", "act", "dve", "pool", "sp")
SAME_ENGINE_SYNC_ALL = False
SMALL_OPS = ("bn_stats", "bn_aggr", "sqrt", "reciprocal", "tensor_reduce", "memset")


class T:
    __slots__ = ("h", "name", "w", "r", "dsem", "dcnt")

    def __init__(self, h, name=""):
        self.h = h
        self.name = name
        self.w = []
        self.r = []
        self.dsem = None
        self.dcnt = 0

    def __getitem__(self, k):
        return self.h[k]


class Prog:
    def __init__(self, nc):
        self.nc = nc
        self.ops = {e: [] for e in ENGS}
        self.seen = {e: {} for e in ENGS}
        self.ctx = []
        self.nsb = 0
        self.tiles = []
        self.stage_tiles = []
        self.sempool = []
        self.bar = None

    def newT(self, h, name=""):
        t = T(h, name)
        self.tiles.append(t)
        self.stage_tiles.append(t)
        return t

    def view(self, t, name=""):
        return self.newT(t.h, name)

    def init_pool(self, n):
        for i in range(n):
            self.sempool.append((self.enter(self.nc.semaphore("dp%d" % i)), 0))
        self.bscr = {e: self.sb("bscr_" + e, [128, 8], F32) for e in ("act", "dve", "pool", "sp")}
        for en in ("act", "dve", "pool", "sp"):
            self.op("dve", lambda e, en=en: e.memset(self.bscr[en][:], 0.0), writes=[self.bscr[en]])
        self.bdram = self.dram("bar_dram", [128, 8], F32)
        self.bar = self.newT(None, "BAR")
        self.persist = list(self.stage_tiles)
        self.stage_tiles = []

    def mark(self):
        self.stage_tiles = []
        return len(self.ctx)

    def barrier(self):
        waits = []
        for t in self.stage_tiles + self.persist:
            for ev in t.w + t.r:
                self._need("act", ev, waits)
        scr = self.bscr
        idx = len(self.ops["act"])
        self.ops["act"].append({"fn": lambda e: e.copy(out=scr["act"][:, 0:1], in_=scr["act"][:, 1:2]),
                                "waits": waits, "flag": False, "dma": None})
        self.bar.w = [("eng", "act", idx)]
        self.bar.r = []
        for en in ("dve", "pool"):
            self.op(en, lambda e, en=en: e.memset(scr[en][:], 0.0), writes=[self.bar])
        self.dma("sp", lambda e: e.dma_start(out=self.bdram.h.ap(), in_=scr["sp"][:]), scr["sp"], writes=[self.bar])

    def release(self, mark):
        self.barrier()
        for t in self.stage_tiles:
            if t.dsem is not None:
                self.sempool.append((t.dsem, t.dcnt))
        self.stage_tiles = []
        while len(self.ctx) > mark:
            self.ctx.pop().__exit__(None, None, None)

    def enter(self, cm):
        v = cm.__enter__()
        self.ctx.append(cm)
        return v

    def sb(self, name, shape, dtype):
        self.nsb += 1
        name = "%s_u%d" % (name, self.nsb)
        return self.newT(self.enter(self.nc.sbuf_tensor(name, list(shape), dtype)), name)

    def ps(self, name, shape, dtype=F32):
        return self.newT(self.enter(self.nc.psum_tensor(name, list(shape), dtype)), name)

    def dram(self, name, shape, dtype, kind="Internal"):
        return self.newT(self.nc.dram_tensor(name, list(shape), dtype, kind=kind), name)

    def close(self):
        for cm in reversed(self.ctx):
            cm.__exit__(None, None, None)
        self.ctx = []

    def _need(self, eng, ev, waits):
        if ev[0] == "eng":
            _, e2, idx = ev
            if e2 == eng and (eng in ("pe", "sp") or not self.ops[e2][idx].get("small", True)):
                return
            if self.seen[eng].get(e2, -1) >= idx:
                return
            self.seen[eng][e2] = idx
            self.ops[e2][idx]["flag"] = True
            waits.append(ev)
        else:
            _, t, val = ev
            key = ("d", id(t))
            if self.seen[eng].get(key, -1) >= val:
                return
            self.seen[eng][key] = val
            waits.append(ev)

    def _deps(self, eng, reads, writes):
        waits = []
        for t in reads:
            for ev in t.w:
                self._need(eng, ev, waits)
        for t in writes:
            for ev in t.w:
                self._need(eng, ev, waits)
            for ev in t.r:
                self._need(eng, ev, waits)
        return waits

    def _commit(self, ev, reads, writes):
        src = ev[1] if ev[0] == "eng" else id(ev[1])
        for t in reads:
            if t in writes:
                continue
            t.r = [e for e in t.r if (e[1] if e[0] == "eng" else id(e[1])) != src]
            t.r.append(ev)
        for t in writes:
            t.w = [ev]
            t.r = []

    def op(self, eng, fn, reads=(), writes=(), small=False):
        waits = self._deps(eng, reads, writes)
        idx = len(self.ops[eng])
        small = small or SAME_ENGINE_SYNC_ALL or any(n in SMALL_OPS for n in fn.__code__.co_names)
        self.ops[eng].append({"fn": fn, "waits": waits, "flag": False, "dma": None, "small": small})
        self._commit(("eng", eng, idx), reads, writes)

    def dma(self, eng, fn, sb_tile, reads=(), writes=()):
        t = sb_tile
        if t.dsem is None:
            if self.sempool:
                t.dsem, t.dcnt = self.sempool.pop()
            else:
                t.dsem = self.enter(self.nc.semaphore("ds%d_%s" % (self.nsb, t.name)))
                self.nsb += 1
        waits = self._deps(eng, reads, writes)
        if t.dcnt > 0:
            self._need(eng, ("dma", t, t.dcnt), waits)
        t.dcnt += 16
        ev = ("dma", t, t.dcnt)
        self.ops[eng].append({"fn": fn, "waits": waits, "flag": False, "dma": t})
        self._commit(ev, reads, writes)
        return ev

    def emit(self, final_events=()):
        nc = self.nc
        fw = []
        for ev in final_events:
            self._need("sp", ev, fw)
        esem = {e: self.enter(nc.semaphore("es_" + e)) for e in ENGS}
        pref = {}
        for e in ENGS:
            c = 0
            p = []
            for o in self.ops[e]:
                if o["flag"]:
                    c += 1
                p.append(c)
            pref[e] = p

        def dowait(eo, ev):
            if ev[0] == "eng":
                eo.wait_ge(esem[ev[1]], pref[ev[1]][ev[2]])
            else:
                eo.wait_ge(ev[1].dsem, ev[2])

        def run(e, eo):
            for o in self.ops[e]:
                for ev in o["waits"]:
                    dowait(eo, ev)
                ins = o["fn"](eo)
                if o["dma"] is not None:
                    ins.then_inc(o["dma"].dsem, 16)
                elif o["flag"]:
                    ins.then_inc(esem[e], 1)
            if e == "sp":
                for ev in fw:
                    dowait(eo, ev)

        with nc.Block() as block:
            @block.tensor
            def _(eo):
                run("pe", eo)

            @block.scalar
            def _(eo):
                run("act", eo)

            @block.vector
            def _(eo):
                run("dve", eo)

            @block.gpsimd
            def _(eo):
                run("pool", eo)

            @block.sync
            def _(eo):
                run("sp", eo)
        self.close()


class Common:
    def __init__(self, P, ident_d):
        self.P = P
        self.banks = [P.ps("bank%d" % i, [128, 512], F32) for i in range(8)]
        self.ident = P.sb("ident_sb", [128, 128], F32)
        P.dma("sp", lambda e: e.dma_start(out=self.ident[:], in_=ident_d.h.ap()), self.ident,
              writes=[self.ident])
        self.cp = 0

    def copy_eng(self):
        self.cp += 1
        return "dve" if self.cp % 2 else "act"


def emit_copy(P, eng, out, in_, reads, writes):
    if eng == "act":
        P.op("act", lambda e: e.copy(out=out, in_=in_), reads=reads, writes=writes)
    else:
        P.op(eng, lambda e: e.tensor_copy(out=out, in_=in_), reads=reads, writes=writes)


def emit_ln_store(P, L, xsm, xs_h, M, g_t, b_t, out_rows_fn, out_tiles, evs):
    for m in range(M):
        for j in range(4):
            P.op("dve", lambda e, m=m, j=j: e.bn_stats(out=L["st"][:, m, j, :],
                                                       in_=xs_h[:, m, j * 512:(j + 1) * 512]),
                 reads=[xsm[m]], writes=[L["stT"]])
        P.op("dve", lambda e, m=m: e.bn_aggr(out=L["mv"][:, m, :], in_=L["st"][:, m, :, :]),
             reads=[L["stT"]], writes=[L["mvT"]])
    P.op("dve", lambda e: e.tensor_scalar(out=L["rs"][:, 0:M], in0=L["mv"][:, 0:M, 1], scalar1=LN_EPS,
                                          scalar2=None, op0=ALU.add),
         reads=[L["mvT"]], writes=[L["rsT"]])
    P.op("act", lambda e: e.sqrt(out=L["rs"][:, 0:M], in_=L["rs"][:, 0:M]), reads=[], writes=[L["rsT"]])
    P.op("dve", lambda e: e.reciprocal(out=L["rs"][:, 0:M], in_=L["rs"][:, 0:M]), reads=[], writes=[L["rsT"]])
    for m in range(M):
        P.op("dve", lambda e, m=m: e.scalar_tensor_tensor(out=xs_h[:, m, :], in0=xs_h[:, m, :],
                                                          scalar=L["mv"][:, m, 0:1], in1=g_t[:],
                                                          op0=ALU.subtract, op1=ALU.mult),
             reads=[L["mvT"], g_t], writes=[xsm[m]])
        P.op("dve", lambda e, m=m: e.scalar_tensor_tensor(out=xs_h[:, m, :], in0=xs_h[:, m, :],
                                                          scalar=L["rs"][:, m:m + 1], in1=b_t[:],
                                                          op0=ALU.mult, op1=ALU.add),
             reads=[L["rsT"], b_t], writes=[xsm[m]])
        evs.append(P.dma("sp", lambda e, m=m: e.dma_start(out=out_rows_fn(m), in_=xs_h[:, m, :]), xsm[m],
                         reads=[xsm[m]], writes=[out_tiles[m]]))


def alloc_ln(P, pfx, M):
    L = {}
    L["st"] = P.sb(pfx + "st", [128, M, 4, 6], F32)
    L["stT"] = L["st"]
    L["mv"] = P.sb(pfx + "mv", [128, M, 2], F32)
    L["mvT"] = L["mv"]
    L["rs"] = P.sb(pfx + "rs", [128, M], F32)
    L["rsT"] = L["rs"]
    return L


def alloc_ffn(P):
    B = {}
    B["xs"] = [P.sb("f_xs%d" % i, [128, 4, D], F32) for i in range(2)]
    B["xsm"] = [[P.view(B["xs"][i], "f_xs%d_%d" % (i, m)) for m in range(4)] for i in range(2)]
    B["hT"] = P.sb("f_hT", [128, 16, 512], BF16)
    B["hTm"] = [P.view(B["hT"], "f_hT%d" % m) for m in range(4)]
    B["hid"] = P.sb("f_hid", [128, NCH, 512], BF16)
    B["hidc"] = [P.view(B["hid"], "f_hid%d" % c) for c in range(NCH)]
    B["wgu"] = [P.sb("f_wgu%d" % i, [128, 16, 256], BF16) for i in range(3)]
    B["wo"] = [P.sb("f_wo%d" % i, [128, 11, 512], BF16) for i in range(3)]
    B["sg"] = [P.sb("f_sg%d" % i, [128, 512], F32) for i in range(2)]
    B["g"] = P.sb("f_g", [128, D], F32)
    B["b"] = P.sb("f_b", [128, D], F32)
    B["ln"] = alloc_ln(P, "f_", 4)
    return B


def emit_ffn(P, C, B, x_in, x_in_tiles, x_out, x_out_tiles, wgu_d, wo_d, g_d, b_d, evs, ntok=NTOK):
    TT = 512
    ntile = ntok // TT
    banks = C.banks
    P.dma("sp", lambda e: e.dma_start(out=B["g"][:], in_=g_d.h.ap().partition_broadcast(128)), B["g"],
          writes=[B["g"]])
    P.dma("sp", lambda e: e.dma_start(out=B["b"][:], in_=b_d.h.ap().partition_broadcast(128)), B["b"],
          writes=[B["b"]])

    items = []
    for t in range(ntile):
        for c in range(NCH):
            items.append(("gu", t, c))
        for n in range(4):
            for s in range(4):
                items.append(("wo", t, n, s))
    cnt = {"gu": 0, "wo": 0}
    slot_of = {}

    def load(i):
        it = items[i]
        if it[0] == "gu":
            k = cnt["gu"]
            cnt["gu"] += 1
            w = B["wgu"][k % 3]
            slot_of[i] = w
            c = it[2]
            P.dma("pool", lambda e: e.dma_start(out=w[:], in_=wgu_d.h.ap()[c].rearrange("p (k j) -> p k j", k=16)),
                  w, writes=[w])
        else:
            k = cnt["wo"]
            cnt["wo"] += 1
            w = B["wo"][k % 3]
            slot_of[i] = w
            n, s = it[2], it[3]
            P.dma("pool", lambda e: e.dma_start(
                out=w[:], in_=wo_d.h.ap()[n].rearrange("p (c j) -> p c j", c=NCH)[:, s * 11:(s + 1) * 11, :]),
                w, writes=[w])

    def load_x(t):
        b = t % 2
        xs = B["xs"][b]
        P.dma("sp", lambda e: e.dma_start(
            out=xs[:], in_=x_in.h.ap()[t * TT:(t + 1) * TT, :].rearrange("(m p) d -> p m d", p=128)),
            B["xsm"][b][0], reads=x_in_tiles[t * 4:(t + 1) * 4], writes=B["xsm"][b])

    PD = 2
    for i in range(min(PD, len(items))):
        load(i)
    load_x(0)
    gi = 0
    for t in range(ntile):
        b = t % 2
        xs = B["xs"][b]
        xsm = B["xsm"][b]
        for m in range(4):
            for kb in range(4):
                bank = banks[4 + (m * 4 + kb) % 4]
                for j in range(4):
                    k = kb * 4 + j
                    P.op("pe", lambda e, bank=bank, m=m, k=k, j=j, xs=xs: e.transpose(
                        out=bank[:, j * 128:(j + 1) * 128], in_=xs[:, m, k * 128:(k + 1) * 128],
                        identity=C.ident[:]), reads=[xsm[m], C.ident], writes=[bank])
                emit_copy(P, C.copy_eng(), B["hT"][:, kb * 4:(kb + 1) * 4, m * 128:(m + 1) * 128],
                          bank[:].rearrange("p (k t) -> p k t", k=4), reads=[], writes=[bank, B["hTm"][m]])
            P.op("pool", lambda e, m=m, xs=xs: e.tensor_scalar(out=xs[:, m, :], in0=xs[:, m, :], scalar1=ALPHA,
                                                        scalar2=0.0, op0=ALU.mult, op1=ALU.add),
                 reads=[], writes=[xsm[m]])
        for c in range(NCH):
            if gi + PD < len(items):
                load(gi + PD)
            w = slot_of[gi]
            gi += 1
            G = banks[(c % 2) * 2]
            U = banks[(c % 2) * 2 + 1]
            sg = B["sg"][c % 2]
            for half, bk in ((0, G), (1, U)):
                for ko in range(16):
                    P.op("pe", lambda e, bk=bk, w=w, ko=ko, half=half: e.matmul(
                        bk[:], lhsT=w[:, ko, half * 128:(half + 1) * 128], rhs=B["hT"][:, ko, :],
                        start=(ko == 0), stop=(ko == 15)), reads=[w] + B["hTm"], writes=[bk])
            P.op("act", lambda e, G=G, sg=sg: e.activation(out=sg[:], in_=G[:], func=AF.Silu),
                 reads=[], writes=[G, sg])
            P.op("dve", lambda e, U=U, sg=sg, c=c: e.tensor_tensor(out=B["hid"][:, c, :], in0=U[:], in1=sg[:],
                                                                   op=ALU.mult),
                 reads=[sg], writes=[U, B["hidc"][c]])
        if t + 1 < ntile:
            load_x(t + 1)
        for n in range(4):
            for s in range(4):
                if gi + PD < len(items):
                    load(gi + PD)
                w = slot_of[gi]
                gi += 1
                for m in range(4):
                    acc = banks[4 + m]
                    for cc in range(11):
                        c = s * 11 + cc
                        P.op("pe", lambda e, acc=acc, w=w, c=c, cc=cc, m=m: e.matmul(
                            acc[:], lhsT=B["hid"][:, c, m * 128:(m + 1) * 128], rhs=w[:, cc, :],
                            start=(c == 0), stop=(c == NCH - 1)), reads=[w, B["hidc"][c]], writes=[acc])
            for m in range(4):
                acc = banks[4 + m]
                P.op("dve", lambda e, acc=acc, m=m, n=n, xs=xs: e.scalar_tensor_tensor(
                    out=xs[:, m, n * 512:(n + 1) * 512], in0=acc[:], scalar=0.5,
                    in1=xs[:, m, n * 512:(n + 1) * 512], op0=ALU.mult, op1=ALU.add),
                    reads=[], writes=[acc, xsm[m]])
        emit_ln_store(P, B["ln"], xsm, xs, 4, B["g"], B["b"],
                      lambda m, t=t: x_out.h.ap()[t * TT + m * 128:t * TT + (m + 1) * 128, :],
                      x_out_tiles[t * 4:(t + 1) * 4], evs)


def build_ffn_prog(ntok=NTOK, debug=False):
    nc = bass.Bass("TRN2", target_bir_lowering=False)
    P = Prog(nc)
    x = P.dram("x", [ntok, D], F32, kind="ExternalInput")
    wgu = P.dram("wgu", [NCH, 128, 16 * 256], F32, kind="ExternalInput")
    wo = P.dram("wo", [4, 128, NCH * 512], F32, kind="ExternalInput")
    g = P.dram("g", [D], F32, kind="ExternalInput")
    b = P.dram("b", [D], F32, kind="ExternalInput")
    ident = P.dram("ident", [128, 128], F32, kind="ExternalInput")
    y = P.dram("y", [ntok, D], F32, kind="ExternalOutput")
    C = Common(P, ident)
    B = alloc_ffn(P)
    xt = [P.newT(None, "xin%d" % i) for i in range(ntok // 128)]
    yt = [P.newT(None, "yout%d" % i) for i in range(ntok // 128)]
    evs = []
    emit_ffn(P, C, B, x, xt, y, yt, wgu, wo, g, b, evs, ntok=ntok)
    if debug:
        dh = P.dram("dbg_hT", [128, 16 * 512], BF16, kind="ExternalOutput")
        dhid = P.dram("dbg_hid", [128, NCH * 512], BF16, kind="ExternalOutput")
        evs.append(P.dma("sp", lambda e: e.dma_start(out=dh.h.ap(), in_=B["hT"][:].rearrange("p k t -> p (k t)")),
                         B["hT"], reads=B["hTm"], writes=[]))
        evs.append(P.dma("sp", lambda e: e.dma_start(out=dhid.h.ap(), in_=B["hid"][:].rearrange("p c t -> p (c t)")),
                         B["hid"], reads=B["hidc"], writes=[]))
    P.emit(evs)
    return nc


def lay_wgu(w_in):
    w = np.asarray(w_in, dtype=np.float32).reshape(16, 128, 2, NCH, 128)
    return np.ascontiguousarray(w.transpose(3, 1, 0, 2, 4)).reshape(NCH, 128, 16 * 256)


def lay_wo(w_out):
    w = np.asarray(w_out, dtype=np.float32).reshape(NCH, 128, 4, 512)
    return np.ascontiguousarray(w.transpose(2, 1, 0, 3)).reshape(4, 128, NCH * 512)


_PROGS = {}


def run_ffn(h, w_in, w_out, g, b):
    if "ffn" not in _PROGS:
        _PROGS["ffn"] = build_ffn_prog()
    nc = _PROGS["ffn"]
    wgu = lay_wgu(w_in)
    wo = lay_wo(w_out)
    ident = np.eye(128, dtype=np.float32)
    g = np.ascontiguousarray(g, dtype=np.float32)
    b = np.ascontiguousarray(b, dtype=np.float32)
    in_maps = [{"x": h[i], "wgu": wgu, "wo": wo, "g": g, "b": b, "ident": ident} for i in range(NCORES)]
    res = run_bass_kernel_spmd(nc, in_maps, core_ids=list(range(NCORES)))
    return [res.results[i]["y"] for i in range(NCORES)]


def load_wres(P, wb, w_ap, ncols, kch):
    views = []
    nb = (ncols + 511) // 512
    for b in range(nb):
        c0, c1 = b * 512, min(ncols, (b + 1) * 512)
        v = P.view(wb, "wres%d" % b)
        views.append(v)
        P.dma("pool", lambda e, c0=c0, c1=c1: e.dma_start(
            out=wb[:, :, c0:c1], in_=w_ap.rearrange("p (k n) -> p k n", k=kch)[:, :, c0:c1]),
            v, writes=[v])
    return views


def emit_rope(P, eng, x_h, xT, H, Gt, J, cos_ap, sin_ap, tmp, tmpT, col0=0, hstride=None, tabs=()):
    hs = hstride if hstride is not None else Gt * 2 * J
    W = Gt * 2 * J

    def xv(f):
        if hs == W:
            v = x_h[:, col0:col0 + H * hs].rearrange("p (h g f j) -> p h g f j", h=H, g=Gt, f=2, j=J)
        else:
            v = x_h[:, 0:H * hs].rearrange("p (h w) -> p h w", h=H)[:, :, col0:col0 + W].rearrange(
                "p h (g f j) -> p h g f j", g=Gt, f=2, j=J)
        return v[:, :, :, f, :]

    def tv(i):
        return tmp[:, i * H * Gt * J:(i + 1) * H * Gt * J].rearrange("p (h g j) -> p h g j", h=H, g=Gt, j=J)

    def tb(ap):
        return ap.rearrange("p (g j) -> p g j", g=Gt, j=J).unsqueeze(1).broadcast_to([128, H, Gt, J])

    c, s_ = tb(cos_ap), tb(sin_ap)
    P.op(eng, lambda e: e.tensor_tensor(out=tv(0), in0=xv(0), in1=c, op=ALU.mult), reads=[xT] + list(tabs), writes=[tmpT])
    P.op(eng, lambda e: e.tensor_tensor(out=tv(1), in0=xv(1), in1=s_, op=ALU.mult), reads=[xT] + list(tabs), writes=[tmpT])
    P.op(eng, lambda e: e.tensor_tensor(out=tv(2), in0=xv(1), in1=c, op=ALU.mult), reads=[xT] + list(tabs), writes=[tmpT])
    P.op(eng, lambda e: e.tensor_tensor(out=tv(3), in0=xv(0), in1=s_, op=ALU.mult), reads=[xT] + list(tabs), writes=[tmpT])
    P.op(eng, lambda e: e.tensor_tensor(out=xv(0), in0=tv(0), in1=tv(1), op=ALU.subtract), reads=[tmpT], writes=[xT])
    P.op(eng, lambda e: e.tensor_tensor(out=xv(1), in0=tv(2), in1=tv(3), op=ALU.add), reads=[tmpT], writes=[xT])


def emit_proj_pass(P, C, x_in, row0_fn, ntiles, wviews, wb, ncols, post):
    banks = C.banks
    xs = [P.sb("p_xs%d" % i, [128, D], F32) for i in range(2)]
    hT = [P.sb("p_hT%d" % i, [128, 16, 128], BF16) for i in range(2)]
    qkv = [P.sb("p_qkv%d" % i, [128, ncols], F32) for i in range(2)]
    nb = (ncols + 511) // 512

    def load_x(i):
        x = xs[i % 2]
        r0 = row0_fn(i)
        P.dma("sp", lambda e: e.dma_start(out=x[:], in_=x_in.h.ap()[r0:r0 + 128, :]), x, writes=[x])

    load_x(0)
    for i in range(ntiles):
        if i + 1 < ntiles:
            load_x(i + 1)
        x, h, q = xs[i % 2], hT[i % 2], qkv[i % 2]
        for kb in range(4):
            bank = banks[4 + kb]
            for j in range(4):
                k = kb * 4 + j
                P.op("pe", lambda e, bank=bank, k=k, j=j, x=x: e.transpose(
                    out=bank[:, j * 128:(j + 1) * 128], in_=x[:, k * 128:(k + 1) * 128], identity=C.ident[:]),
                    reads=[x, C.ident], writes=[bank])
            emit_copy(P, C.copy_eng(), h[:, kb * 4:(kb + 1) * 4, :], bank[:].rearrange("p (k t) -> p k t", k=4),
                      reads=[], writes=[bank, h])
        for b in range(nb):
            c0, c1 = b * 512, min(ncols, (b + 1) * 512)
            bank = banks[b % 4]
            for k in range(16):
                P.op("pe", lambda e, bank=bank, k=k, c0=c0, c1=c1, h=h: e.matmul(
                    bank[:, 0:c1 - c0], lhsT=h[:, k, :], rhs=wb[:, k, c0:c1], start=(k == 0), stop=(k == 15)),
                    reads=[h, wviews[b]], writes=[bank])
            emit_copy(P, C.copy_eng(), q[:, c0:c1], bank[:, 0:c1 - c0], reads=[], writes=[bank, q])
        post(i, q)


def emit_headT(P, C, src, srcT, col0, nh, width, stage, stageT, h0, m):
    banks = C.banks
    for g0 in range(0, nh, 4):
        g = min(4, nh - g0)
        bank = banks[4 + (C.cp % 4)]
        for j in range(g):
            c = col0[g0 + j]
            P.op("pe", lambda e, bank=bank, j=j, c=c: e.transpose(
                out=bank[0:width, j * 128:(j + 1) * 128], in_=src[:, c:c + width], identity=C.ident[:]),
                reads=[srcT, C.ident], writes=[bank])
        emit_copy(P, C.copy_eng(), stage[0:width, h0 + g0:h0 + g0 + g, m * 128:(m + 1) * 128],
                  bank[0:width, 0:g * 128].rearrange("p (k t) -> p k t", k=g), reads=[], writes=[bank, stageT])


def emit_attn_core(P, C, nseq, nout, sources_fn, scale, OT_d, masks=None):
    banks = C.banks
    NSRC = max(len(sources_fn(h)) for h in range(nout))
    kt_sb = [[P.sb("a_kt%d_%d" % (b, s), [128, SEQ], BF16) for s in range(NSRC)] for b in range(2)]
    v_sb = [[P.sb("a_v%d_%d" % (b, s), [128, 16, 128], BF16) for s in range(NSRC)] for b in range(2)]
    q_sb = [[P.sb("a_q%d_%d" % (b, s), [128, 512], BF16) for s in range(NSRC)] for b in range(2)]
    has_r = any(src.get("qr") is not None for src in sources_fn(0))
    if has_r:
        kr_sb = [P.sb("a_kr%d" % b, [64, SEQ], BF16) for b in range(2)]
        qr_sb = [P.sb("a_qr%d" % b, [64, 512], BF16) for b in range(2)]
    NPT = 6
    pt = [P.sb("a_pt%d" % i, [128, 512], BF16) for i in range(NPT)]
    ones = P.sb("a_ones", [128, 128], BF16)
    P.op("pool", lambda e: e.memset(ones[:], 1.0), writes=[ones])
    rden = [P.sb("a_rden%d" % i, [128, 512], F32) for i in range(2)]
    ot = [P.sb("a_ot%d" % i, [128, 512], BF16) for i in range(2)]
    ctr = {"hk": 0, "qk": 0, "pk": 0, "sk": 0}
    LOOK = 3
    stream = []

    class Unit:
        pass

    def make_unit(srcs, kb, t0, h, qt, head_loads):
        u = Unit()
        q0 = t0 + qt * 512
        qb = ctr["qk"] % 2
        ctr["qk"] += 1
        qk = ctr["qk"]
        work = []
        for si, src in enumerate(srcs):
            g = src.get("mask")
            if g is None:
                kts = list(range(16))
            else:
                Wd = masks["W"][g]
                kts = [kt for kt in range(16) if -Wd - 127 <= 128 * kt - 512 * qt <= Wd + 511]
            for kt in kts:
                work.append((si, kt, g))
        num = banks[4 + 2 * (qk % 2)]
        den = banks[5 + 2 * (qk % 2)]
        nw = len(work)
        pmap = {}

        def prologue():
            for si, src in enumerate(srcs):
                qd, qi = src["q"]
                P.dma("sp", lambda e, si=si, qd=qd, qi=qi: e.dma_start(out=q_sb[qb][si][:], in_=qd.h.ap()[qi][:, q0:q0 + 512]),
                      q_sb[qb][si], writes=[q_sb[qb][si]])
            if has_r:
                qd, qi = srcs[0]["qr"]
                P.dma("sp", lambda e, qd=qd, qi=qi: e.dma_start(out=qr_sb[qb][:], in_=qd.h.ap()[qi][:, q0:q0 + 512]),
                      qr_sb[qb], writes=[qr_sb[qb]])

        def issue_s(w):
            si, kt, g = work[w]
            bank = banks[ctr["sk"] % 4]
            ctr["sk"] += 1
            pmap[w] = bank
            P.op("pe", lambda e: e.matmul(
                bank[:], lhsT=kt_sb[kb][si][:, kt * 128:(kt + 1) * 128], rhs=q_sb[qb][si][:],
                start=True, stop=not has_r), reads=[kt_sb[kb][si], q_sb[qb][si]], writes=[bank])
            if has_r:
                P.op("pe", lambda e: e.matmul(
                    bank[:], lhsT=kr_sb[kb][:, kt * 128:(kt + 1) * 128], rhs=qr_sb[qb][:],
                    start=False, stop=True), reads=[kr_sb[kb], qr_sb[qb]], writes=[bank])

        def consume(w):
            si, kt, g = work[w]
            bank = pmap[w]
            p = pt[ctr["pk"] % NPT]
            ctr["pk"] += 1
            P.op("act", lambda e: e.activation(out=p[:], in_=bank[:], func=AF.Exp, scale=scale),
                 reads=[], writes=[bank, p])
            if g is not None:
                off = masks["CMAX"][g] - (128 * kt - 512 * qt)
                mt = masks["tab"][g]
                P.op("dve", lambda e: e.tensor_tensor(out=p[:], in0=p[:], in1=mt[:, off:off + 512], op=ALU.mult),
                     reads=[mt], writes=[p])
            P.op("pe", lambda e: e.matmul(num[:], lhsT=v_sb[kb][si][:, kt, :], rhs=p[:], start=(w == 0),
                                          stop=(w == nw - 1)), reads=[v_sb[kb][si], p], writes=[num])
            P.op("pe", lambda e: e.matmul(den[:], lhsT=ones[:], rhs=p[:], start=(w == 0), stop=(w == nw - 1)),
                 reads=[ones, p], writes=[den])
            if w == nw - 1:
                rd = rden[qk % 2]
                o = ot[qk % 2]
                P.op("dve", lambda e: e.reciprocal(out=rd[:], in_=den[:]), reads=[], writes=[den, rd])
                P.op("dve", lambda e: e.tensor_tensor(out=o[:], in0=num[:], in1=rd[:], op=ALU.mult),
                     reads=[rd], writes=[num, o])
                P.dma("sp", lambda e: e.dma_start(out=OT_d.h.ap()[h][:, q0:q0 + 512], in_=o[:]), o, reads=[o])

        u.issue_s, u.consume, u.nw, u.prologue, u.head_loads = issue_s, consume, nw, prologue, head_loads
        return u

    def head(sq, h):
        t0 = sq * SEQ
        srcs = sources_fn(h)
        kb = ctr["hk"] % 2
        ctr["hk"] += 1

        def head_loads():
            for si, src in enumerate(srcs):
                kd, ki = src["k"]
                vd, vc = src["v"]
                P.dma("sp", lambda e, si=si, kd=kd, ki=ki: e.dma_start(out=kt_sb[kb][si][:], in_=kd.h.ap()[ki][:, t0:t0 + SEQ]),
                      kt_sb[kb][si], writes=[kt_sb[kb][si]])
                P.dma("sp", lambda e, si=si, vd=vd, vc=vc: e.dma_start(
                    out=v_sb[kb][si][:], in_=vd.h.ap()[t0:t0 + SEQ, vc:vc + 128].rearrange("(t p) d -> p t d", p=128)),
                    v_sb[kb][si], writes=[v_sb[kb][si]])
            if has_r:
                kd, ki = srcs[0]["kr"]
                P.dma("sp", lambda e, kd=kd, ki=ki: e.dma_start(out=kr_sb[kb][:], in_=kd.h.ap()[ki][0:64, t0:t0 + SEQ]),
                      kr_sb[kb], writes=[kr_sb[kb]])

        for qt in range(SEQ // 512):
            u = make_unit(srcs, kb, t0, h, qt, head_loads if qt == 0 else None)
            units.append(u)
            for w in range(u.nw):
                stream.append((u, w))

    units = []
    for sq in range(nseq):
        for h in range(nout):
            head(sq, h)
    uidx = {id(u): i for i, u in enumerate(units)}
    units[0].head_loads()
    units[0].prologue()
    n = len(stream)
    for idx in range(n + LOOK):
        if idx < n:
            u, w = stream[idx]
            if w == 0:
                i = uidx[id(u)]
                if i + 1 < len(units):
                    units[i + 1].prologue()
                nq = SEQ // 512
                if i % nq == 1 and i + nq - 1 < len(units):
                    units[i + nq - 1].head_loads()
            u.issue_s(w)
        j = idx - LOOK
        if j >= 0:
            u, w = stream[j]
            u.consume(w)


def emit_oproj(P, C, x_in, x_out, OT_d, nh, wo_d, g_d, b_d, evs, ntok=NTOK):
    banks = C.banks
    wo = P.sb("o_wo", [128, nh, D], BF16)
    wov = load_wres(P, wo, wo_d.h.ap(), D, nh)
    g_t = P.sb("o_g", [128, D], F32)
    b_t = P.sb("o_b", [128, D], F32)
    P.dma("sp", lambda e: e.dma_start(out=g_t[:], in_=g_d.h.ap().partition_broadcast(128)), g_t, writes=[g_t])
    P.dma("sp", lambda e: e.dma_start(out=b_t[:], in_=b_d.h.ap().partition_broadcast(128)), b_t, writes=[b_t])
    xs = [P.sb("o_xs%d" % i, [128, 4, D], F32) for i in range(2)]
    xsm = [[P.view(xs[i], "o_xs%d_%d" % (i, m)) for m in range(4)] for i in range(2)]
    ots = [P.sb("o_ot%d" % i, [128, nh, 512], BF16) for i in range(2)]
    L = alloc_ln(P, "o_", 4)
    TT = 512
    dummy = [P.newT(None, "oy%d" % m) for m in range(4)]
    for t in range(ntok // TT):
        b = t % 2
        x, xm, o = xs[b], xsm[b], ots[b]
        P.dma("sp", lambda e, x=x, t=t: e.dma_start(
            out=x[:], in_=x_in.h.ap()[t * TT:(t + 1) * TT, :].rearrange("(m p) d -> p m d", p=128)), xm[0], writes=xm)
        P.dma("sp", lambda e, o=o, t=t: e.dma_start(
            out=o[:], in_=OT_d.h.ap().rearrange("h d t -> d h t")[:, :, t * TT:(t + 1) * TT]), o, writes=[o])
        for m in range(4):
            P.op("pool", lambda e, m=m, x=x: e.tensor_scalar(out=x[:, m, :], in0=x[:, m, :], scalar1=ALPHA, scalar2=0.0,
                                                        op0=ALU.mult, op1=ALU.add), reads=[], writes=[xm[m]])
        for n in range(4):
            for m in range(4):
                acc = banks[(n * 4 + m) % 8]
                for hh in range(nh):
                    P.op("pe", lambda e, acc=acc, hh=hh, m=m, n=n, o=o: e.matmul(
                        acc[:], lhsT=o[:, hh, m * 128:(m + 1) * 128], rhs=wo[:, hh, n * 512:(n + 1) * 512],
                        start=(hh == 0), stop=(hh == nh - 1)), reads=[o, wov[n]], writes=[acc])
                P.op("dve", lambda e, acc=acc, m=m, n=n, x=x: e.tensor_tensor(
                    out=x[:, m, n * 512:(n + 1) * 512], in0=acc[:], in1=x[:, m, n * 512:(n + 1) * 512], op=ALU.add),
                    reads=[], writes=[acc, xm[m]])
        emit_ln_store(P, L, xm, x, 4, g_t, b_t,
                      lambda m, t=t: x_out.h.ap()[t * TT + m * 128:t * TT + (m + 1) * 128, :], dummy, evs)


def emit_mixer_a(P, C, x_in, x_out, S, w_in_d, qg_d, kg_d, w_out_d, g_d, b_d, cosA_d, sinA_d, evs, ntok=NTOK):
    mk = P.mark()
    wb = P.sb("a_wb", [128, 16, 3072], BF16)
    wv = load_wres(P, wb, w_in_d.h.ap(), 3072, 16)
    gq = P.sb("a_gq", [128, 128], F32)
    gk = P.sb("a_gk", [128, 128], F32)
    P.dma("sp", lambda e: e.dma_start(out=gq[:], in_=qg_d.h.ap().partition_broadcast(128)), gq, writes=[gq])
    P.dma("sp", lambda e: e.dma_start(out=gk[:], in_=kg_d.h.ap().partition_broadcast(128)), gk, writes=[gk])
    cs = P.sb("a_cos", [128, 16, 64], F32)
    sn = P.sb("a_sin", [128, 16, 64], F32)
    P.dma("sp", lambda e: e.dma_start(out=cs[:], in_=cosA_d.h.ap().rearrange("(t p) j -> p t j", p=128)), cs, writes=[cs])
    P.dma("sp", lambda e: e.dma_start(out=sn[:], in_=sinA_d.h.ap().rearrange("(t p) j -> p t j", p=128)), sn, writes=[sn])
    ss = P.sb("a_ss", [128, 20], F32)
    tmp = P.sb("a_tmp", [128, 4 * 20 * 64], F32)
    sq = tmp
    stq = [P.sb("a_stq%d" % i, [128, 20, 512], BF16) for i in range(1)]
    vst = [P.sb("a_vst%d" % i, [128, 512], BF16) for i in range(2)]

    def post(i, q):
        m = i % 4
        st = stq[0]
        tpos = i % 16
        P.op("dve", lambda e: e.tensor_tensor(out=sq[:, 0:2560], in0=q[:, 0:2560], in1=q[:, 0:2560], op=ALU.mult),
             reads=[q], writes=[sq])
        P.op("dve", lambda e: e.tensor_reduce(out=ss[:], in_=sq[:, 0:2560].rearrange("p (h d) -> p h d", h=20),
                                              axis=mybir.AxisListType.X, op=ALU.add), reads=[sq], writes=[ss])
        P.op("dve", lambda e: e.tensor_scalar(out=ss[:], in0=ss[:], scalar1=1.0 / 128, scalar2=RMS_EPS,
                                              op0=ALU.mult, op1=ALU.add), reads=[], writes=[ss])
        P.op("act", lambda e: e.sqrt(out=ss[:], in_=ss[:]), reads=[], writes=[ss])
        P.op("dve", lambda e: e.reciprocal(out=ss[:], in_=ss[:]), reads=[], writes=[ss])
        q3 = q[:, 0:2560].rearrange("p (h d) -> p h d", h=20)
        P.op("dve", lambda e: e.tensor_tensor(out=q3, in0=q3, in1=ss[:].unsqueeze(2).broadcast_to([128, 20, 128]),
                                              op=ALU.mult), reads=[ss], writes=[q])
        P.op("pool", lambda e: e.tensor_tensor(out=q3[:, 0:16, :], in0=q3[:, 0:16, :],
                                               in1=gq[:].unsqueeze(1).broadcast_to([128, 16, 128]), op=ALU.mult),
             reads=[gq], writes=[q])
        P.op("pool", lambda e: e.tensor_tensor(out=q3[:, 16:20, :], in0=q3[:, 16:20, :],
                                               in1=gk[:].unsqueeze(1).broadcast_to([128, 4, 128]), op=ALU.mult),
             reads=[gk], writes=[q])
        emit_rope(P, "dve", q, q, 20, 2, 32, cs[:, tpos, :], sn[:, tpos, :], tmp, tmp, tabs=[cs, sn])
        emit_headT(P, C, q, q, [hh * 128 for hh in range(20)], 20, 128, st, st, 0, m)
        vs = vst[i % 2]
        P.op("act", lambda e: e.copy(out=vs[:], in_=q[:, 2560:3072]), reads=[q], writes=[vs])
        P.dma("sp", lambda e: e.dma_start(out=S["V"].h.ap()[i * 128:(i + 1) * 128, :], in_=vs[:]), vs, reads=[vs])
        if m == 3:
            t0 = (i // 4) * 512
            P.dma("sp", lambda e: e.dma_start(out=S["QT"].h.ap().rearrange("h d t -> d h t")[:, :, t0:t0 + 512],
                                              in_=st[:, 0:16, :]), st, reads=[st])
            P.dma("sp", lambda e: e.dma_start(out=S["KT"].h.ap().rearrange("h d t -> d h t")[:, :, t0:t0 + 512],
                                              in_=st[:, 16:20, :]), st, reads=[st])

    emit_proj_pass(P, C, x_in, lambda i: i * 128, ntok // 128, wv, wb, 3072, post)
    P.release(mk)
    _ph = 3
    if _ph < 2:
        return
    mk = P.mark()
    emit_attn_core(P, C, ntok // SEQ, 16,
                   lambda h: [dict(q=(S["QT"], h), k=(S["KT"], h // 4), v=(S["V"], (h // 4) * 128))],
                   128 ** -0.5, S["OT"])
    P.release(mk)
    if _ph < 3:
        return
    mk = P.mark()
    emit_oproj(P, C, x_in, x_out, S["OT"], 16, w_out_d, g_d, b_d, evs, ntok=ntok)
    P.release(mk)


def rope_tables_axial():
    pos = np.arange(SEQ)
    inv = 10000.0 ** (-np.arange(0, 64, 2, dtype=np.float32) / 64)
    ar = (pos // 64).astype(np.float32)[:, None] * inv[None, :]
    ac = (pos % 64).astype(np.float32)[:, None] * inv[None, :]
    cos = np.concatenate([np.cos(ar), np.cos(ac)], 1).astype(np.float32)
    sin = np.concatenate([np.sin(ar), np.sin(ac)], 1).astype(np.float32)
    return cos, sin


def lay_kn(w, kch):
    w = np.asarray(w, dtype=np.float32)
    n = w.shape[1]
    return np.ascontiguousarray(w.reshape(kch, 128, n).transpose(1, 0, 2)).reshape(128, kch * n)


def build_mixa_prog(ntok=NTOK):
    nc = bass.Bass("TRN2", target_bir_lowering=False)
    P = Prog(nc)
    x = P.dram("x", [ntok, D], F32, kind="ExternalInput")
    w_in = P.dram("w_in", [128, 16 * 3072], F32, kind="ExternalInput")
    w_out = P.dram("w_out", [128, 16 * 2048], F32, kind="ExternalInput")
    qg = P.dram("qg", [128], F32, kind="ExternalInput")
    kg = P.dram("kg", [128], F32, kind="ExternalInput")
    g = P.dram("g", [D], F32, kind="ExternalInput")
    b = P.dram("b", [D], F32, kind="ExternalInput")
    cosA = P.dram("cosA", [SEQ, 64], F32, kind="ExternalInput")
    sinA = P.dram("sinA", [SEQ, 64], F32, kind="ExternalInput")
    ident = P.dram("ident", [128, 128], F32, kind="ExternalInput")
    y = P.dram("y", [ntok, D], F32, kind="ExternalOutput")
    S = {"QT": P.dram("s_qt", [16, 128, ntok], BF16), "KT": P.dram("s_kt", [4, 128, ntok], BF16),
         "V": P.dram("s_v", [ntok, 512], BF16), "OT": P.dram("s_ot", [16, 128, ntok], BF16)}
    P.init_pool(40)
    C = Common(P, ident)
    P.persist += P.stage_tiles
    evs = []
    emit_mixer_a(P, C, x, y, S, w_in, qg, kg, w_out, g, b, cosA, sinA, evs, ntok=ntok)
    P.emit(evs)
    return nc


B_W = [64, 256, 1024]
B_DIL = [1, 4, 16]
B_CMAX = [w + 512 for w in B_W]
B_TW = 3200


def mask_tables_b():
    tabs = np.zeros((3, 128, B_TW), dtype=np.float32)
    kk = np.arange(128)[:, None]
    c = np.arange(B_TW)[None, :]
    for g in range(3):
        d = kk - c + B_CMAX[g]
        tabs[g] = ((np.abs(d) <= B_W[g]) & (d % B_DIL[g] == 0)).astype(np.float32)
    return tabs


def rope_tables_seq(dim):
    pos = np.arange(SEQ, dtype=np.float32)
    inv = 10000.0 ** (-np.arange(0, dim, 2, dtype=np.float32) / dim)
    ang = pos[:, None] * inv[None, :]
    return np.cos(ang).astype(np.float32), np.sin(ang).astype(np.float32)


def emit_mixer_b(P, C, x_in, x_out, S, w_in_d, w_out_d, g_d, b_d, cosB_d, sinB_d, mask_d, evs, ntok=NTOK):
    for g in range(3):
        mk = P.mark()
        wb = P.sb("b_wb", [128, 16, 3072], BF16)
        wv = load_wres(P, wb, w_in_d.h.ap()[g], 3072, 16)
        cs = P.sb("b_cos", [128, 16, 64], F32)
        sn = P.sb("b_sin", [128, 16, 64], F32)
        P.dma("sp", lambda e, cs=cs: e.dma_start(out=cs[:], in_=cosB_d.h.ap().rearrange("(t p) j -> p t j", p=128)),
              cs, writes=[cs])
        P.dma("sp", lambda e, sn=sn: e.dma_start(out=sn[:], in_=sinB_d.h.ap().rearrange("(t p) j -> p t j", p=128)),
              sn, writes=[sn])
        tmp = P.sb("b_tmp", [128, 4 * 16 * 64], F32)
        st = P.sb("b_st", [128, 16, 512], BF16)
        vst = [P.sb("b_vst%d" % i, [128, 1024], BF16) for i in range(2)]

        def post(i, q, g=g, cs=cs, sn=sn, tmp=tmp, st=st, vst=vst):
            m = i % 4
            tpos = i % 16
            emit_rope(P, "dve", q, q, 16, 1, 64, cs[:, tpos, :], sn[:, tpos, :], tmp, tmp, tabs=[cs, sn])
            emit_headT(P, C, q, q, [hh * 128 for hh in range(16)], 16, 128, st, st, 0, m)
            vs = vst[i % 2]
            P.op("act", lambda e: e.copy(out=vs[:], in_=q[:, 2048:3072]), reads=[q], writes=[vs])
            P.dma("sp", lambda e: e.dma_start(out=S["V"].h.ap()[i * 128:(i + 1) * 128, g * 1024:(g + 1) * 1024],
                                              in_=vs[:]), vs, reads=[vs])
            if m == 3:
                t0 = (i // 4) * 512
                P.dma("sp", lambda e: e.dma_start(
                    out=S["QT"].h.ap().rearrange("h d t -> d h t")[:, g * 8:(g + 1) * 8, t0:t0 + 512],
                    in_=st[:, 0:8, :]), st, reads=[st])
                P.dma("sp", lambda e: e.dma_start(
                    out=S["KT"].h.ap().rearrange("h d t -> d h t")[:, g * 8:(g + 1) * 8, t0:t0 + 512],
                    in_=st[:, 8:16, :]), st, reads=[st])

        emit_proj_pass(P, C, x_in, lambda i: i * 128, ntok // 128, wv, wb, 3072, post)
        P.release(mk)
    mk = P.mark()
    tabs = []
    for g in range(3):
        mt = P.sb("b_mask%d" % g, [128, B_TW], BF16)
        P.dma("pool", lambda e, mt=mt, g=g: e.dma_start(out=mt[:], in_=mask_d.h.ap()[g]), mt, writes=[mt])
        tabs.append(mt)
    masks = {"W": B_W, "CMAX": B_CMAX, "tab": tabs}
    emit_attn_core(P, C, ntok // SEQ, 8,
                   lambda h: [dict(q=(S["QT"], g * 8 + h), k=(S["KT"], g * 8 + h), v=(S["V"], g * 1024 + h * 128), mask=g)
                              for g in range(3)],
                   128 ** -0.5, S["OT"], masks=masks)
    P.release(mk)
    mk = P.mark()
    emit_oproj(P, C, x_in, x_out, S["OT"], 8, w_out_d, g_d, b_d, evs, ntok=ntok)
    P.release(mk)


def lay_w_in_b(w):
    w = np.asarray(w, dtype=np.float32).reshape(16, 128, 3, 3072)
    return np.ascontiguousarray(w.transpose(2, 1, 0, 3)).reshape(3, 128, 16 * 3072)


def build_mixb_prog(ntok=NTOK):
    nc = bass.Bass("TRN2", target_bir_lowering=False)
    P = Prog(nc)
    x = P.dram("x", [ntok, D], F32, kind="ExternalInput")
    w_in = P.dram("w_in", [3, 128, 16 * 3072], F32, kind="ExternalInput")
    w_out = P.dram("w_out", [128, 8 * 2048], F32, kind="ExternalInput")
    g = P.dram("g", [D], F32, kind="ExternalInput")
    b = P.dram("b", [D], F32, kind="ExternalInput")
    cosB = P.dram("cosB", [SEQ, 64], F32, kind="ExternalInput")
    sinB = P.dram("sinB", [SEQ, 64], F32, kind="ExternalInput")
    maskd = P.dram("maskB", [3, 128, B_TW], F32, kind="ExternalInput")
    ident = P.dram("ident", [128, 128], F32, kind="ExternalInput")
    y = P.dram("y", [ntok, D], F32, kind="ExternalOutput")
    S = {"QT": P.dram("s_qt", [24, 128, ntok], BF16), "KT": P.dram("s_kt", [24, 128, ntok], BF16),
         "V": P.dram("s_v", [ntok, 3072], BF16), "OT": P.dram("s_ot", [8, 128, ntok], BF16)}
    P.init_pool(40)
    C = Common(P, ident)
    P.persist += P.stage_tiles
    evs = []
    emit_mixer_b(P, C, x, y, S, w_in, w_out, g, b, cosB, sinB, maskd, evs, ntok=ntok)
    P.emit(evs)
    return nc


def emit_mixer_c(P, C, x_in, x_out, S, w_in_d, qg_d, kvg_d, wq_d, wkv_d, w_out_d, g_d, b_d, cosC_d, sinC_d, evs,
                 ntok=NTOK, only_p1=False):
    mk = P.mark()
    wb = P.sb("c_wb", [128, 16, 1536], BF16)
    wv = load_wres(P, wb, w_in_d.h.ap(), 1536, 16)
    gq = P.sb("c_gq", [128, 1280], F32)
    P.dma("sp", lambda e: e.dma_start(out=gq[:, 0:768], in_=qg_d.h.ap().partition_broadcast(128)), gq, writes=[gq])
    P.dma("sp", lambda e: e.dma_start(out=gq[:, 768:1280], in_=kvg_d.h.ap().partition_broadcast(128)), gq, writes=[gq])
    cs = P.sb("c_cos", [128, 16, 32], F32)
    sn = P.sb("c_sin", [128, 16, 32], F32)
    P.dma("sp", lambda e: e.dma_start(out=cs[:], in_=cosC_d.h.ap().rearrange("(t p) j -> p t j", p=128)), cs, writes=[cs])
    P.dma("sp", lambda e: e.dma_start(out=sn[:], in_=sinC_d.h.ap().rearrange("(t p) j -> p t j", p=128)), sn, writes=[sn])
    tmp = P.sb("c_tmp", [128, 1280], F32)
    ss = P.sb("c_ss", [128, 2], F32)
    stl = P.sb("c_stl", [128, 10, 512], BF16)
    stk4 = P.sb("c_stk4", [64, 4, 512], BF16)
    ctab = [P.sb("c_ctab%d" % i, [128, 64], F32) for i in range(2)]

    def post1(i, q):
        m = i % 4
        tpos = i % 16
        P.op("dve", lambda e: e.tensor_tensor(out=tmp[:], in0=q[:, 0:1280], in1=q[:, 0:1280], op=ALU.mult),
             reads=[q], writes=[tmp])
        P.op("dve", lambda e: e.tensor_reduce(out=ss[:, 0:1], in_=tmp[:, 0:768], axis=mybir.AxisListType.X, op=ALU.add),
             reads=[tmp], writes=[ss])
        P.op("dve", lambda e: e.tensor_reduce(out=ss[:, 1:2], in_=tmp[:, 768:1280], axis=mybir.AxisListType.X,
                                              op=ALU.add), reads=[tmp], writes=[ss])
        P.op("dve", lambda e: e.tensor_scalar(out=ss[:, 0:1], in0=ss[:, 0:1], scalar1=1.0 / 768, scalar2=RMS_EPS,
                                              op0=ALU.mult, op1=ALU.add), reads=[], writes=[ss])
        P.op("dve", lambda e: e.tensor_scalar(out=ss[:, 1:2], in0=ss[:, 1:2], scalar1=1.0 / 512, scalar2=RMS_EPS,
                                              op0=ALU.mult, op1=ALU.add), reads=[], writes=[ss])
        P.op("act", lambda e: e.sqrt(out=ss[:], in_=ss[:]), reads=[], writes=[ss])
        P.op("dve", lambda e: e.reciprocal(out=ss[:], in_=ss[:]), reads=[], writes=[ss])
        P.op("dve", lambda e: e.scalar_tensor_tensor(out=q[:, 0:768], in0=q[:, 0:768], scalar=ss[:, 0:1],
                                                     in1=gq[:, 0:768], op0=ALU.mult, op1=ALU.mult),
             reads=[ss, gq], writes=[q])
        P.op("dve", lambda e: e.scalar_tensor_tensor(out=q[:, 768:1280], in0=q[:, 768:1280], scalar=ss[:, 1:2],
                                                     in1=gq[:, 768:1280], op0=ALU.mult, op1=ALU.mult),
             reads=[ss, gq], writes=[q])
        x0, x1 = q[:, 1280:1312], q[:, 1312:1344]
        ct = ctab[i % 2]
        P.dma("sp", lambda e: e.dma_start(out=ct[:, 0:32], in_=cosC_d.h.ap()[tpos * 128:(tpos + 1) * 128, :]), ct, writes=[ct])
        P.dma("sp", lambda e: e.dma_start(out=ct[:, 32:64], in_=sinC_d.h.ap()[tpos * 128:(tpos + 1) * 128, :]), ct, writes=[ct])
        cc, sc = ct[:, 0:32], ct[:, 32:64]
        cs = sn = ct
        if "dbg_early" in S and i == 5:
            evs.append(P.dma("sp", lambda e: e.dma_start(out=S["dbg_q"].h.ap()[:, 0, :], in_=q[:, 1280:1344]), q, reads=[q]))
            evs.append(P.dma("sp", lambda e: e.dma_start(out=S["dbg_q"].h.ap()[:, 2, 0:32], in_=cc), cs, reads=[cs]))
            evs.append(P.dma("sp", lambda e: e.dma_start(out=S["dbg_q"].h.ap()[:, 2, 32:64], in_=sc), sn, reads=[sn]))
        tt = [tmp[:, j * 32:(j + 1) * 32] for j in range(4)]
        P.op("dve", lambda e: e.tensor_tensor(out=tt[0], in0=x0, in1=cc, op=ALU.mult), reads=[q, cs], writes=[tmp])
        P.op("dve", lambda e: e.tensor_tensor(out=tt[1], in0=x1, in1=sc, op=ALU.mult), reads=[q, sn], writes=[tmp])
        P.op("dve", lambda e: e.tensor_tensor(out=tt[2], in0=x1, in1=cc, op=ALU.mult), reads=[q, cs], writes=[tmp])
        P.op("dve", lambda e: e.tensor_tensor(out=tt[3], in0=x0, in1=sc, op=ALU.mult), reads=[q, sn], writes=[tmp])
        P.op("pool", lambda e: e.tensor_tensor(out=x0, in0=tt[0], in1=tt[1], op=ALU.subtract), reads=[tmp], writes=[q])
        P.op("pool", lambda e: e.tensor_tensor(out=x1, in0=tt[2], in1=tt[3], op=ALU.add), reads=[tmp], writes=[q])
        if "dbg_early" in S and i == 5:
            evs.append(P.dma("sp", lambda e: e.dma_start(out=S["dbg_q"].h.ap()[:, 1, :], in_=q[:, 1280:1344]), q, reads=[q]))
            evs.append(P.dma("sp", lambda e: e.dma_start(out=S["dbg_q"].h.ap()[:, 3, :], in_=tmp[:, 0:64]), tmp, reads=[tmp]))
        emit_headT(P, C, q, q, [k * 128 for k in range(10)], 10, 128, stl, stl, 0, m)
        emit_headT(P, C, q, q, [1280, 1280, 1280, 1280], 4, 64, stk4, stk4, 0, m)
        if m == 3:
            t0 = (i // 4) * 512
            P.dma("sp", lambda e: e.dma_start(out=S["LT"].h.ap().rearrange("h d t -> d h t")[:, 0:10, t0:t0 + 512],
                                              in_=stl[:]), stl, reads=[stl])
            P.dma("sp", lambda e: e.dma_start(out=S["LT"].h.ap()[10][0:64, t0:t0 + 512], in_=stk4[:, 0, :]),
                  stk4, reads=[stk4])

    emit_proj_pass(P, C, x_in, lambda i: i * 128, ntok // 128, wv, wb, 1536, post1)
    P.release(mk)
    if "dbg_early" in S:
        scr_e = P.sb("dbg_early_scr", [1, 8], F32)
        evs.append(P.dma("sp", lambda e: e.dma_start(out=S["dbg_early"].h.ap(), in_=S["LT"].h.ap()[10]), scr_e))
    if only_p1:
        return
    mk = P.mark()
    banks = C.banks
    wq = P.sb("c_wq", [128, 6, 3072], BF16)
    wqv = load_wres(P, wq, wq_d.h.ap(), 3072, 6)
    wkv = P.sb("c_wkv", [128, 4, 4096], BF16)
    wkvv = load_wres(P, wkv, wkv_d.h.ap(), 4096, 4)
    cs = P.sb("c_cos2", [128, 16, 32], F32)
    sn = P.sb("c_sin2", [128, 16, 32], F32)
    P.dma("sp", lambda e: e.dma_start(out=cs[:], in_=cosC_d.h.ap().rearrange("(t p) j -> p t j", p=128)), cs, writes=[cs])
    P.dma("sp", lambda e: e.dma_start(out=sn[:], in_=sinC_d.h.ap().rearrange("(t p) j -> p t j", p=128)), sn, writes=[sn])
    lat = [P.sb("c_lat%d" % i, [128, 11, 512], BF16) for i in range(2)]
    q2 = P.sb("c_q2", [128, 3072], F32)
    kv2 = P.sb("c_kv2", [128, 4096], F32)
    tmp2 = P.sb("c_tmp2", [128, 4 * 16 * 32], F32)
    sqt = P.sb("c_sqt", [128, 16, 512], BF16)
    sqr = P.sb("c_sqr", [64, 16, 512], BF16)
    skt = P.sb("c_skt", [128, 16, 512], BF16)
    vst = [P.sb("c_vst%d" % i, [128, 16, 128], BF16) for i in range(2)]

    def tile2(i):
        m = i % 4
        tpos = i % 16
        la = lat[(i // 4) % 2]
        if m == 0:
            t0 = (i // 4) * 512
            P.dma("sp", lambda e: e.dma_start(out=la[:], in_=S["LT"].h.ap().rearrange("h d t -> d h t")[:, :, t0:t0 + 512]),
                  la, writes=[la])
        for b in range(6):
            bank = banks[b % 4]
            for k in range(6):
                P.op("pe", lambda e, bank=bank, k=k, b=b: e.matmul(
                    bank[:], lhsT=la[:, k, m * 128:(m + 1) * 128], rhs=wq[:, k, b * 512:(b + 1) * 512],
                    start=(k == 0), stop=(k == 5)), reads=[la, wqv[b]], writes=[bank])
            emit_copy(P, C.copy_eng(), q2[:, b * 512:(b + 1) * 512], bank[:], reads=[], writes=[bank, q2])
        for b in range(8):
            bank = banks[(b + 2) % 4]
            for k in range(4):
                P.op("pe", lambda e, bank=bank, k=k, b=b: e.matmul(
                    bank[:], lhsT=la[:, 6 + k, m * 128:(m + 1) * 128], rhs=wkv[:, k, b * 512:(b + 1) * 512],
                    start=(k == 0), stop=(k == 3)), reads=[la, wkvv[b]], writes=[bank])
            emit_copy(P, C.copy_eng(), kv2[:, b * 512:(b + 1) * 512], bank[:], reads=[], writes=[bank, kv2])
        emit_rope(P, "dve", q2, q2, 16, 1, 32, cs[:, tpos, :], sn[:, tpos, :], tmp2, tmp2, col0=128, hstride=192, tabs=[cs, sn])
        emit_headT(P, C, q2, q2, [hh * 192 for hh in range(16)], 16, 128, sqt, sqt, 0, m)
        emit_headT(P, C, q2, q2, [hh * 192 + 128 for hh in range(16)], 16, 64, sqr, sqr, 0, m)
        emit_headT(P, C, kv2, kv2, [hh * 256 for hh in range(16)], 16, 128, skt, skt, 0, m)
        vs = vst[i % 2]
        P.op("act", lambda e: e.copy(out=vs[:], in_=kv2[:].rearrange("p (h c) -> p h c", h=16)[:, :, 128:256]),
             reads=[kv2], writes=[vs])
        P.dma("sp", lambda e: e.dma_start(out=S["V"].h.ap()[i * 128:(i + 1) * 128, :],
                                          in_=vs[:].rearrange("p h c -> p (h c)")), vs, reads=[vs])
        if m == 3:
            t0 = (i // 4) * 512
            for dst, stg in ((S["QT"], sqt), (S["QR"], sqr), (S["KT"], skt)):
                P.dma("sp", lambda e, dst=dst, stg=stg: e.dma_start(
                    out=dst.h.ap().rearrange("h d t -> d h t")[:, :, t0:t0 + 512], in_=stg[:]), stg, reads=[stg])

    for i in range(ntok // 128):
        tile2(i)
    P.release(mk)
    mk = P.mark()
    emit_attn_core(P, C, ntok // SEQ, 16,
                   lambda h: [dict(q=(S["QT"], h), k=(S["KT"], h), v=(S["V"], h * 128), qr=(S["QR"], h), kr=(S["LT"], 10))],
                   192 ** -0.5, S["OT"])
    P.release(mk)
    mk = P.mark()
    emit_oproj(P, C, x_in, x_out, S["OT"], 16, w_out_d, g_d, b_d, evs, ntok=ntok)
    P.release(mk)


def build_mixc_prog(ntok=NTOK, debug=False):
    nc = bass.Bass("TRN2", target_bir_lowering=False)
    P = Prog(nc)
    x = P.dram("x", [ntok, D], F32, kind="ExternalInput")
    w_in = P.dram("w_in", [128, 16 * 1536], F32, kind="ExternalInput")
    wq = P.dram("wq", [128, 6 * 3072], F32, kind="ExternalInput")
    wkv = P.dram("wkv", [128, 4 * 4096], F32, kind="ExternalInput")
    w_out = P.dram("w_out", [128, 16 * 2048], F32, kind="ExternalInput")
    qg = P.dram("qg", [768], F32, kind="ExternalInput")
    kvg = P.dram("kvg", [512], F32, kind="ExternalInput")
    g = P.dram("g", [D], F32, kind="ExternalInput")
    b = P.dram("b", [D], F32, kind="ExternalInput")
    cosC = P.dram("cosC", [SEQ, 32], F32, kind="ExternalInput")
    sinC = P.dram("sinC", [SEQ, 32], F32, kind="ExternalInput")
    ident = P.dram("ident", [128, 128], F32, kind="ExternalInput")
    y = P.dram("y", [ntok, D], F32, kind="ExternalOutput")
    S = {"LT": P.dram("s_lt", [11, 128, ntok], BF16),
         "QT": P.dram("s_qt", [16, 128, ntok], BF16), "QR": P.dram("s_qr", [16, 64, ntok], BF16),
         "KT": P.dram("s_kt", [16, 128, ntok], BF16), "V": P.dram("s_v", [ntok, 2048], BF16),
         "OT": P.dram("s_ot", [16, 128, ntok], BF16)}
    if debug:
        S["dbg_early"] = P.dram("dbg_early", [128, ntok], BF16, kind="ExternalOutput")
        S["dbg_q"] = P.dram("dbg_q", [128, 4, 64], F32, kind="ExternalOutput")
    P.init_pool(40)
    C = Common(P, ident)
    P.persist += P.stage_tiles
    evs = []
    emit_mixer_c(P, C, x, y, S, w_in, qg, kvg, wq, wkv, w_out, g, b, cosC, sinC, evs, ntok=ntok, only_p1=(debug == 2))
    if debug:
        for k, t in S.items():
            if k in ("dbg_early", "dbg_q"):
                continue
            if debug == 2 and k not in ("LT",):
                continue
            shp = [int(v) for v in t.h.shape]
            do = P.dram("dbg_" + k, shp, BF16, kind="ExternalOutput")
            scr = P.sb("dbgscr_" + k, [1, 8], F32)
            evs.append(P.dma("sp", lambda e, do=do, t=t: e.dma_start(out=do.h.ap(), in_=t.h.ap()), scr, writes=[do]))
    P.emit(evs)
    return nc


def lay_w_in_c(w):
    w = np.asarray(w, dtype=np.float32)
    wp = np.zeros((2048, 1536), dtype=np.float32)
    wp[:, :1344] = w
    return lay_kn(wp, 16)


MIX_KIND = ["a", "b", "c", "a"]


def build_full_prog(ntok=NTOK, nlayers=DEPTH):
    nc = bass.Bass("TRN2", target_bir_lowering=False)
    P = Prog(nc)
    I = {}

    def inp(name, shape):
        I[name] = P.dram(name, shape, F32, kind="ExternalInput")
        return I[name]

    x = inp("x", [ntok, D])
    ident = inp("ident", [128, 128])
    inp("cosA", [SEQ, 64]); inp("sinA", [SEQ, 64])
    inp("cosB", [SEQ, 64]); inp("sinB", [SEQ, 64])
    inp("maskB", [3, 128, B_TW])
    inp("cosC", [SEQ, 32]); inp("sinC", [SEQ, 32])
    for i in range(nlayers):
        for f in (1, 2):
            inp("wgu%d_%d" % (f, i), [NCH, 128, 16 * 256])
            inp("wo%d_%d" % (f, i), [4, 128, NCH * 512])
        for n in (1, 2, 3):
            inp("ln%d_g_%d" % (n, i), [D]); inp("ln%d_b_%d" % (n, i), [D])
        k = MIX_KIND[i]
        if k == "a":
            inp("a_w_in_%d" % i, [128, 16 * 3072]); inp("a_w_out_%d" % i, [128, 16 * 2048])
            inp("a_qg_%d" % i, [128]); inp("a_kg_%d" % i, [128])
        elif k == "b":
            inp("b_w_in_%d" % i, [3, 128, 16 * 3072]); inp("b_w_out_%d" % i, [128, 8 * 2048])
        else:
            inp("c_w_in_%d" % i, [128, 16 * 1536]); inp("c_wq_%d" % i, [128, 6 * 3072])
            inp("c_wkv_%d" % i, [128, 4 * 4096]); inp("c_w_out_%d" % i, [128, 16 * 2048])
            inp("c_qg_%d" % i, [768]); inp("c_kvg_%d" % i, [512])
    y = P.dram("y", [ntok, D], F32, kind="ExternalOutput")
    bufs = [P.dram("hbuf%d" % i, [ntok, D], F32) for i in range(2)]
    SA = {"QT": P.dram("sa_qt", [16, 128, ntok], BF16), "KT": P.dram("sa_kt", [4, 128, ntok], BF16),
          "V": P.dram("sa_v", [ntok, 512], BF16), "OT": P.dram("sa_ot", [16, 128, ntok], BF16)}
    SB = {"QT": P.dram("sb_qt", [24, 128, ntok], BF16), "KT": P.dram("sb_kt", [24, 128, ntok], BF16),
          "V": P.dram("sb_v", [ntok, 3072], BF16), "OT": P.dram("sb_ot", [8, 128, ntok], BF16)}
    SC = {"LT": P.dram("sc_lt", [11, 128, ntok], BF16),
          "QT": P.dram("sc_qt", [16, 128, ntok], BF16), "QR": P.dram("sc_qr", [16, 64, ntok], BF16),
          "KT": P.dram("sc_kt", [16, 128, ntok], BF16), "V": P.dram("sc_v", [ntok, 2048], BF16),
          "OT": P.dram("sc_ot", [16, 128, ntok], BF16)}
    P.init_pool(40)
    C = Common(P, ident)
    P.persist += P.stage_tiles
    evs = []
    nstage = 3 * nlayers
    src = x
    k = 0

    def nxt():
        return y if k == nstage - 1 else bufs[k % 2]

    def ffn(f, i, src, dst):
        mk = P.mark()
        B = alloc_ffn(P)
        xt = [P.newT(None, "xi") for _ in range(ntok // 128)]
        yt = [P.newT(None, "yo") for _ in range(ntok // 128)]
        e2 = []
        emit_ffn(P, C, B, src, xt, dst, yt, I["wgu%d_%d" % (f, i)], I["wo%d_%d" % (f, i)],
                 I["ln%d_g_%d" % (1 if f == 1 else 3, i)], I["ln%d_b_%d" % (1 if f == 1 else 3, i)], e2, ntok=ntok)
        P.release(mk)
        return e2

    for i in range(nlayers):
        dst = nxt()
        last = ffn(1, i, src, dst)
        src = dst
        k += 1
        dst = nxt()
        kind = MIX_KIND[i]
        last = []
        if kind == "a":
            emit_mixer_a(P, C, src, dst, SA, I["a_w_in_%d" % i], I["a_qg_%d" % i], I["a_kg_%d" % i], I["a_w_out_%d" % i],
                         I["ln2_g_%d" % i], I["ln2_b_%d" % i], I["cosA"], I["sinA"], last, ntok=ntok)
        elif kind == "b":
            emit_mixer_b(P, C, src, dst, SB, I["b_w_in_%d" % i], I["b_w_out_%d" % i], I["ln2_g_%d" % i], I["ln2_b_%d" % i],
                         I["cosB"], I["sinB"], I["maskB"], last, ntok=ntok)
        else:
            emit_mixer_c(P, C, src, dst, SC, I["c_w_in_%d" % i], I["c_qg_%d" % i], I["c_kvg_%d" % i], I["c_wq_%d" % i],
                         I["c_wkv_%d" % i], I["c_w_out_%d" % i], I["ln2_g_%d" % i], I["ln2_b_%d" % i],
                         I["cosC"], I["sinC"], last, ntok=ntok)
        src = dst
        k += 1
        dst = nxt()
        last = ffn(2, i, src, dst)
        src = dst
        k += 1
    P.emit(last)
    return nc, P


def host_inputs(inputs, nlayers=DEPTH):
    f32 = lambda a: np.ascontiguousarray(np.asarray(a, dtype=np.float32))
    H = {"ident": np.eye(128, dtype=np.float32)}
    H["cosA"], H["sinA"] = rope_tables_axial()
    H["cosB"], H["sinB"] = rope_tables_seq(128)
    H["cosC"], H["sinC"] = rope_tables_seq(64)
    H["maskB"] = mask_tables_b()
    for i in range(nlayers):
        for f in (1, 2):
            H["wgu%d_%d" % (f, i)] = lay_wgu(inputs["ffn%d_w_in_%d" % (f, i)])
            H["wo%d_%d" % (f, i)] = lay_wo(inputs["ffn%d_w_out_%d" % (f, i)])
        for n in (1, 2, 3):
            H["ln%d_g_%d" % (n, i)] = f32(inputs["ln%d_g_%d" % (n, i)])
            H["ln%d_b_%d" % (n, i)] = f32(inputs["ln%d_b_%d" % (n, i)])
        k = MIX_KIND[i]
        if k == "a":
            H["a_w_in_%d" % i] = lay_kn(inputs["a_w_in_%d" % i], 16)
            H["a_w_out_%d" % i] = lay_kn(inputs["a_w_out_%d" % i], 16)
            H["a_qg_%d" % i] = f32(inputs["a_q_gain_%d" % i])
            H["a_kg_%d" % i] = f32(inputs["a_k_gain_%d" % i])
        elif k == "b":
            H["b_w_in_%d" % i] = lay_w_in_b(inputs["b_w_in_%d" % i])
            H["b_w_out_%d" % i] = lay_kn(inputs["b_w_out_%d" % i], 8)
        else:
            H["c_w_in_%d" % i] = lay_w_in_c(inputs["c_w_in_%d" % i])
            H["c_wq_%d" % i] = lay_kn(inputs["c_w_q_up_%d" % i], 6)
            H["c_wkv_%d" % i] = lay_kn(inputs["c_w_kv_up_%d" % i], 4)
            H["c_w_out_%d" % i] = lay_kn(inputs["c_w_out_%d" % i], 16)
            H["c_qg_%d" % i] = f32(inputs["c_q_gain_%d" % i])
            H["c_kvg_%d" % i] = f32(inputs["c_kv_gain_%d" % i])
    return H


ALL_INPUT_NAMES = (
    "x",
    "ffn1_w_in_0",
    "ffn1_w_out_0",
    "ln1_g_0",
    "ln1_b_0",
    "a_w_in_0",
    "a_q_gain_0",
    "a_k_gain_0",
    "a_w_out_0",
    "ln2_g_0",
    "ln2_b_0",
    "ffn2_w_in_0",
    "ffn2_w_out_0",
    "ln3_g_0",
    "ln3_b_0",
    "ffn1_w_in_1",
    "ffn1_w_out_1",
    "ln1_g_1",
    "ln1_b_1",
    "b_w_in_1",
    "b_w_out_1",
    "ln2_g_1",
    "ln2_b_1",
    "ffn2_w_in_1",
    "ffn2_w_out_1",
    "ln3_g_1",
    "ln3_b_1",
    "ffn1_w_in_2",
    "ffn1_w_out_2",
    "ln1_g_2",
    "ln1_b_2",
    "c_w_in_2",
    "c_q_gain_2",
    "c_kv_gain_2",
    "c_w_q_up_2",
    "c_w_kv_up_2",
    "c_w_out_2",
    "ln2_g_2",
    "ln2_b_2",
    "ffn2_w_in_2",
    "ffn2_w_out_2",
    "ln3_g_2",
    "ln3_b_2",
    "ffn1_w_in_3",
    "ffn1_w_out_3",
    "ln1_g_3",
    "ln1_b_3",
    "a_w_in_3",
    "a_q_gain_3",
    "a_k_gain_3",
    "a_w_out_3",
    "ln2_g_3",
    "ln2_b_3",
    "ffn2_w_in_3",
    "ffn2_w_out_3",
    "ln3_g_3",
    "ln3_b_3",
)


def kernel(**inputs):
    missing = [n for n in ALL_INPUT_NAMES if n not in inputs]
    assert not missing, missing
    x = np.asarray(inputs["x"], dtype=np.float32).reshape(16 * SEQ, D)
    H = host_inputs(inputs)
    nc, _ = build_full_prog()
    in_maps = []
    for c in range(NCORES):
        m = dict(H)
        m["x"] = np.ascontiguousarray(x[c * NTOK:(c + 1) * NTOK])
        in_maps.append(m)
    res = run_bass_kernel_spmd(nc, in_maps, core_ids=list(range(NCORES)))
    out = np.concatenate([res.results[c]["y"] for c in range(NCORES)], axis=0)
    return out.reshape(16, SEQ, D).astype(np.float32)
```

```python
import numpy as np
import concourse.bass as bass
import concourse.mybir as mybir
from concourse.bass_utils import run_bass_kernel_spmd

F32 = mybir.dt.float32
BF16 = mybir.dt.bfloat16
AF = mybir.ActivationFunctionType
ALU = mybir.AluOpType

NCORES = 8
D = 2048
SEQ = 2048
NTOK = 4096
DFF = 5632
NCH = DFF // 128
DEPTH = 4
ALPHA = (2 * DEPTH) ** 0.25
LN_EPS = 1e-5
RMS_EPS = 1e-6

ENGS = ("pe", "act", "dve", "pool", "sp")
SAME_ENGINE_SYNC_ALL = False
SMALL_OPS = ("bn_stats", "bn_aggr", "sqrt", "reciprocal", "tensor_reduce", "memset")


class T:
    __slots__ = ("h", "name", "w", "r", "dsem", "dcnt")

    def __init__(self, h, name=""):
        self.h = h
        self.name = name
        self.w = []
        self.r = []
        self.dsem = None
        self.dcnt = 0

    def __getitem__(self, k):
        return self.h[k]


class Prog:
    def __init__(self, nc):
        self.nc = nc
        self.ops = {e: [] for e in ENGS}
        self.seen = {e: {} for e in ENGS}
        self.ctx = []
        self.nsb = 0
        self.tiles = []
        self.stage_tiles = []
        self.sempool = []
        self.bar = None

    def newT(self, h, name=""):
        t = T(h, name)
        self.tiles.append(t)
        self.stage_tiles.append(t)
        return t

    def view(self, t, name=""):
        return self.newT(t.h, name)

    def init_pool(self, n):
        for i in range(n):
            self.sempool.append((self.enter(self.nc.semaphore("dp%d" % i)), 0))
        self.bscr = {e: self.sb("bscr_" + e, [128, 8], F32) for e in ("act", "dve", "pool", "sp")}
        for en in ("act", "dve", "pool", "sp"):
            self.op("dve", lambda e, en=en: e.memset(self.bscr[en][:], 0.0), writes=[self.bscr[en]])
        self.bdram = self.dram("bar_dram", [128, 8], F32)
        self.bar = self.newT(None, "BAR")
        self.persist = list(self.stage_tiles)
        self.stage_tiles = []

    def mark(self):
        self.stage_tiles = []
        return len(self.ctx)

    def barrier(self):
        waits = []
        for t in self.stage_tiles + self.persist:
            for ev in t.w + t.r:
                self._need("act", ev, waits)
        scr = self.bscr
        idx = len(self.ops["act"])
        self.ops["act"].append({"fn": lambda e: e.copy(out=scr["act"][:, 0:1], in_=scr["act"][:, 1:2]),
                                "waits": waits, "flag": False, "dma": None})
        self.bar.w = [("eng", "act", idx)]
        self.bar.r = []
        for en in ("dve", "pool"):
            self.op(en, lambda e, en=en: e.memset(scr[en][:], 0.0), writes=[self.bar])
        self.dma("sp", lambda e: e.dma_start(out=self.bdram.h.ap(), in_=scr["sp"][:]), scr["sp"], writes=[self.bar])

    def release(self, mark):
        self.barrier()
        for t in self.stage_tiles:
            if t.dsem is not None:
                self.sempool.append((t.dsem, t.dcnt))
        self.stage_tiles = []
        while len(self.ctx) > mark:
            self.ctx.pop().__exit__(None, None, None)

    def enter(self, cm):
        v = cm.__enter__()
        self.ctx.append(cm)
        return v

    def sb(self, name, shape, dtype):
        self.nsb += 1
        name = "%s_u%d" % (name, self.nsb)
        return self.newT(self.enter(self.nc.sbuf_tensor(name, list(shape), dtype)), name)

    def ps(self, name, shape, dtype=F32):
        return self.newT(self.enter(self.nc.psum_tensor(name, list(shape), dtype)), name)

    def dram(self, name, shape, dtype, kind="Internal"):
        return self.newT(self.nc.dram_tensor(name, list(shape), dtype, kind=kind), name)

    def close(self):
        for cm in reversed(self.ctx):
            cm.__exit__(None, None, None)
        self.ctx = []

    def _need(self, eng, ev, waits):
        if ev[0] == "eng":
            _, e2, idx = ev
            if e2 == eng and (eng in ("pe", "sp") or not self.ops[e2][idx].get("small", True)):
                return
            if self.seen[eng].get(e2, -1) >= idx:
                return
            self.seen[eng][e2] = idx
            self.ops[e2][idx]["flag"] = True
            waits.append(ev)
        else:
            _, t, val = ev
            key = ("d", id(t))
            if self.seen[eng].get(key, -1) >= val:
                return
            self.seen[eng][key] = val
            waits.append(ev)

    def _deps(self, eng, reads, writes):
        waits = []
        for t in reads:
            for ev in t.w:
                self._need(eng, ev, waits)
        for t in writes:
            for ev in t.w:
                self._need(eng, ev, waits)
            for ev in t.r:
                self._need(eng, ev, waits)
        return waits

    def _commit(self, ev, reads, writes):
        src = ev[1] if ev[0] == "eng" else id(ev[1])
        for t in reads:
            if t in writes:
                continue
            t.r = [e for e in t.r if (e[1] if e[0] == "eng" else id(e[1])) != src]
            t.r.append(ev)
        for t in writes:
            t.w = [ev]
            t.r = []

    def op(self, eng, fn, reads=(), writes=(), small=False):
        waits = self._deps(eng, reads, writes)
        idx = len(self.ops[eng])
        small = small or SAME_ENGINE_SYNC_ALL or any(n in SMALL_OPS for n in fn.__code__.co_names)
        self.ops[eng].append({"fn": fn, "waits": waits, "flag": False, "dma": None, "small": small})
        self._commit(("eng", eng, idx), reads, writes)

    def dma(self, eng, fn, sb_tile, reads=(), writes=()):
        t = sb_tile
        if t.dsem is None:
            if self.sempool:
                t.dsem, t.dcnt = self.sempool.pop()
            else:
                t.dsem = self.enter(self.nc.semaphore("ds%d_%s" % (self.nsb, t.name)))
                self.nsb += 1
        waits = self._deps(eng, reads, writes)
        if t.dcnt > 0:
            self._need(eng, ("dma", t, t.dcnt), waits)
        t.dcnt += 16
        ev = ("dma", t, t.dcnt)
        self.ops[eng].append({"fn": fn, "waits": waits, "flag": False, "dma": t})
        self._commit(ev, reads, writes)
        return ev

    def emit(self, final_events=()):
        nc = self.nc
        fw = []
        for ev in final_events:
            self._need("sp", ev, fw)
        esem = {e: self.enter(nc.semaphore("es_" + e)) for e in ENGS}
        pref = {}
        for e in ENGS:
            c = 0
            p = []
            for o in self.ops[e]:
                if o["flag"]:
                    c += 1
                p.append(c)
            pref[e] = p

        def dowait(eo, ev):
            if ev[0] == "eng":
                eo.wait_ge(esem[ev[1]], pref[ev[1]][ev[2]])
            else:
                eo.wait_ge(ev[1].dsem, ev[2])

        def run(e, eo):
            for o in self.ops[e]:
                for ev in o["waits"]:
                    dowait(eo, ev)
                ins = o["fn"](eo)
                if o["dma"] is not None:
                    ins.then_inc(o["dma"].dsem, 16)
                elif o["flag"]:
                    ins.then_inc(esem[e], 1)
            if e == "sp":
                for ev in fw:
                    dowait(eo, ev)

        with nc.Block() as block:
            @block.tensor
            def _(eo):
                run("pe", eo)

            @block.scalar
            def _(eo):
                run("act", eo)

            @block.vector
            def _(eo):
                run("dve", eo)

            @block.gpsimd
            def _(eo):
                run("pool", eo)

            @block.sync
            def _(eo):
                run("sp", eo)
        self.close()


class Common:
    def __init__(self, P, ident_d):
        self.P = P
        self.banks = [P.ps("bank%d" % i, [128, 512], F32) for i in range(8)]
        self.ident = P.sb("ident_sb", [128, 128], F32)
        P.dma("sp", lambda e: e.dma_start(out=self.ident[:], in_=ident_d.h.ap()), self.ident,
              writes=[self.ident])
        self.cp = 0

    force = None

    def copy_eng(self):
        self.cp += 1
        if self.force is not None:
            return self.force
        return "dve" if self.cp % 2 else "act"


def emit_copy(P, eng, out, in_, reads, writes):
    if eng == "act":
        P.op("act", lambda e: e.copy(out=out, in_=in_), reads=reads, writes=writes)
    else:
        P.op(eng, lambda e: e.tensor_copy(out=out, in_=in_), reads=reads, writes=writes)


def emit_ln_store(P, L, xsm, xs_h, M, g_t, b_t, out_rows_fn, out_tiles, evs):
    for m in range(M):
        for j in range(4):
            P.op("dve", lambda e, m=m, j=j: e.bn_stats(out=L["st"][:, m, j, :],
                                                       in_=xs_h[:, m, j * 512:(j + 1) * 512]),
                 reads=[xsm[m]], writes=[L["stT"]])
        P.op("dve", lambda e, m=m: e.bn_aggr(out=L["mv"][:, m, :], in_=L["st"][:, m, :, :]),
             reads=[L["stT"]], writes=[L["mvT"]])
    P.op("dve", lambda e: e.tensor_scalar(out=L["rs"][:, 0:M], in0=L["mv"][:, 0:M, 1], scalar1=LN_EPS,
                                          scalar2=None, op0=ALU.add),
         reads=[L["mvT"]], writes=[L["rsT"]])
    P.op("act", lambda e: e.sqrt(out=L["rs"][:, 0:M], in_=L["rs"][:, 0:M]), reads=[], writes=[L["rsT"]])
    P.op("dve", lambda e: e.reciprocal(out=L["rs"][:, 0:M], in_=L["rs"][:, 0:M]), reads=[], writes=[L["rsT"]])
    for m in range(M):
        P.op("dve", lambda e, m=m: e.scalar_tensor_tensor(out=xs_h[:, m, :], in0=xs_h[:, m, :],
                                                          scalar=L["mv"][:, m, 0:1], in1=g_t[:],
                                                          op0=ALU.subtract, op1=ALU.mult),
             reads=[L["mvT"], g_t], writes=[xsm[m]])
        P.op("dve", lambda e, m=m: e.scalar_tensor_tensor(out=xs_h[:, m, :], in0=xs_h[:, m, :],
                                                          scalar=L["rs"][:, m:m + 1], in1=b_t[:],
                                                          op0=ALU.mult, op1=ALU.add),
             reads=[L["rsT"], b_t], writes=[xsm[m]])
        evs.append(P.dma("sp", lambda e, m=m: e.dma_start(out=out_rows_fn(m), in_=xs_h[:, m, :]), xsm[m],
                         reads=[xsm[m]], writes=[out_tiles[m]]))


def alloc_ln(P, pfx, M):
    L = {}
    L["st"] = P.sb(pfx + "st", [128, M, 4, 6], F32)
    L["stT"] = L["st"]
    L["mv"] = P.sb(pfx + "mv", [128, M, 2], F32)
    L["mvT"] = L["mv"]
    L["rs"] = P.sb(pfx + "rs", [128, M], F32)
    L["rsT"] = L["rs"]
    return L


def alloc_ffn(P):
    B = {}
    B["xs"] = [P.sb("f_xs%d" % i, [128, 4, D], F32) for i in range(2)]
    B["xsm"] = [[P.view(B["xs"][i], "f_xs%d_%d" % (i, m)) for m in range(4)] for i in range(2)]
    B["hT"] = P.sb("f_hT", [128, 16, 512], BF16)
    B["hTm"] = [P.view(B["hT"], "f_hT%d" % m) for m in range(4)]
    B["hid"] = P.sb("f_hid", [128, NCH, 512], BF16)
    B["hidc"] = [P.view(B["hid"], "f_hid%d" % c) for c in range(NCH)]
    B["wgu"] = [P.sb("f_wgu%d" % i, [128, 16, 256], BF16) for i in range(3)]
    B["wo"] = [P.sb("f_wo%d" % i, [128, 11, 512], BF16) for i in range(3)]
    B["sg"] = [P.sb("f_sg%d" % i, [128, 512], F32) for i in range(2)]
    B["g"] = P.sb("f_g", [128, D], F32)
    B["b"] = P.sb("f_b", [128, D], F32)
    B["ln"] = alloc_ln(P, "f_", 4)
    return B


def emit_ffn(P, C, B, x_in, x_in_tiles, x_out, x_out_tiles, wgu_d, wo_d, g_d, b_d, evs, ntok=NTOK):
    TT = 512
    ntile = ntok // TT
    banks = C.banks
    P.dma("sp", lambda e: e.dma_start(out=B["g"][:], in_=g_d.h.ap().partition_broadcast(128)), B["g"],
          writes=[B["g"]])
    P.dma("sp", lambda e: e.dma_start(out=B["b"][:], in_=b_d.h.ap().partition_broadcast(128)), B["b"],
          writes=[B["b"]])

    items = []
    for t in range(ntile):
        for c in range(NCH):
            items.append(("gu", t, c))
        for n in range(4):
            for s in range(4):
                items.append(("wo", t, n, s))
    cnt = {"gu": 0, "wo": 0}
    slot_of = {}

    def load(i):
        it = items[i]
        if it[0] == "gu":
            k = cnt["gu"]
            cnt["gu"] += 1
            w = B["wgu"][k % 3]
            slot_of[i] = w
            c = it[2]
            P.dma("pool", lambda e: e.dma_start(out=w[:], in_=wgu_d.h.ap()[c].rearrange("p (k j) -> p k j", k=16)),
                  w, writes=[w])
        else:
            k = cnt["wo"]
            cnt["wo"] += 1
            w = B["wo"][k % 3]
            slot_of[i] = w
            n, s = it[2], it[3]
            P.dma("pool", lambda e: e.dma_start(
                out=w[:], in_=wo_d.h.ap()[n].rearrange("p (c j) -> p c j", c=NCH)[:, s * 11:(s + 1) * 11, :]),
                w, writes=[w])

    def load_x(t):
        b = t % 2
        xs = B["xs"][b]
        P.dma("sp", lambda e: e.dma_start(
            out=xs[:], in_=x_in.h.ap()[t * TT:(t + 1) * TT, :].rearrange("(m p) d -> p m d", p=128)),
            B["xsm"][b][0], reads=x_in_tiles[t * 4:(t + 1) * 4], writes=B["xsm"][b])

    PD = 2
    for i in range(min(PD, len(items))):
        load(i)
    load_x(0)
    gi = 0
    for t in range(ntile):
        b = t % 2
        xs = B["xs"][b]
        xsm = B["xsm"][b]
        for m in range(4):
            for kb in range(4):
                bank = banks[4 + (m * 4 + kb) % 4]
                for j in range(4):
                    k = kb * 4 + j
                    P.op("pe", lambda e, bank=bank, m=m, k=k, j=j, xs=xs: e.transpose(
                        out=bank[:, j * 128:(j + 1) * 128], in_=xs[:, m, k * 128:(k + 1) * 128],
                        identity=C.ident[:]), reads=[xsm[m], C.ident], writes=[bank])
                emit_copy(P, C.copy_eng(), B["hT"][:, kb * 4:(kb + 1) * 4, m * 128:(m + 1) * 128],
                          bank[:].rearrange("p (k t) -> p k t", k=4), reads=[], writes=[bank, B["hTm"][m]])
            P.op("pool", lambda e, m=m, xs=xs: e.tensor_scalar(out=xs[:, m, :], in0=xs[:, m, :], scalar1=ALPHA,
                                                        scalar2=0.0, op0=ALU.mult, op1=ALU.add),
                 reads=[], writes=[xsm[m]])
        for c in range(NCH):
            if gi + PD < len(items):
                load(gi + PD)
            w = slot_of[gi]
            gi += 1
            G = banks[(c % 2) * 2]
            U = banks[(c % 2) * 2 + 1]
            sg = B["sg"][c % 2]
            for half, bk in ((0, G), (1, U)):
                for ko in range(16):
                    P.op("pe", lambda e, bk=bk, w=w, ko=ko, half=half: e.matmul(
                        bk[:], lhsT=w[:, ko, half * 128:(half + 1) * 128], rhs=B["hT"][:, ko, :],
                        start=(ko == 0), stop=(ko == 15)), reads=[w] + B["hTm"], writes=[bk])
            P.op("act", lambda e, G=G, sg=sg: e.activation(out=sg[:], in_=G[:], func=AF.Silu),
                 reads=[], writes=[G, sg])
            P.op("dve", lambda e, U=U, sg=sg, c=c: e.tensor_tensor(out=B["hid"][:, c, :], in0=U[:], in1=sg[:],
                                                                   op=ALU.mult),
                 reads=[sg], writes=[U, B["hidc"][c]])
        if t + 1 < ntile:
            load_x(t + 1)
        for n in range(4):
            for s in range(4):
                if gi + PD < len(items):
                    load(gi + PD)
                w = slot_of[gi]
                gi += 1
                for m in range(4):
                    acc = banks[4 + m]
                    for cc in range(11):
                        c = s * 11 + cc
                        P.op("pe", lambda e, acc=acc, w=w, c=c, cc=cc, m=m: e.matmul(
                            acc[:], lhsT=B["hid"][:, c, m * 128:(m + 1) * 128], rhs=w[:, cc, :],
                            start=(c == 0), stop=(c == NCH - 1)), reads=[w, B["hidc"][c]], writes=[acc])
            for m in range(4):
                acc = banks[4 + m]
                P.op("dve", lambda e, acc=acc, m=m, n=n, xs=xs: e.scalar_tensor_tensor(
                    out=xs[:, m, n * 512:(n + 1) * 512], in0=acc[:], scalar=0.5,
                    in1=xs[:, m, n * 512:(n + 1) * 512], op0=ALU.mult, op1=ALU.add),
                    reads=[], writes=[acc, xsm[m]])
        emit_ln_store(P, B["ln"], xsm, xs, 4, B["g"], B["b"],
                      lambda m, t=t: x_out.h.ap()[t * TT + m * 128:t * TT + (m + 1) * 128, :],
                      x_out_tiles[t * 4:(t + 1) * 4], evs)


def build_ffn_prog(ntok=NTOK, debug=False):
    nc = bass.Bass("TRN2", target_bir_lowering=False)
    P = Prog(nc)
    x = P.dram("x", [ntok, D], F32, kind="ExternalInput")
    wgu = P.dram("wgu", [NCH, 128, 16 * 256], F32, kind="ExternalInput")
    wo = P.dram("wo", [4, 128, NCH * 512], F32, kind="ExternalInput")
    g = P.dram("g", [D], F32, kind="ExternalInput")
    b = P.dram("b", [D], F32, kind="ExternalInput")
    ident = P.dram("ident", [128, 128], F32, kind="ExternalInput")
    y = P.dram("y", [ntok, D], F32, kind="ExternalOutput")
    C = Common(P, ident)
    B = alloc_ffn(P)
    xt = [P.newT(None, "xin%d" % i) for i in range(ntok // 128)]
    yt = [P.newT(None, "yout%d" % i) for i in range(ntok // 128)]
    evs = []
    emit_ffn(P, C, B, x, xt, y, yt, wgu, wo, g, b, evs, ntok=ntok)
    if debug:
        dh = P.dram("dbg_hT", [128, 16 * 512], BF16, kind="ExternalOutput")
        dhid = P.dram("dbg_hid", [128, NCH * 512], BF16, kind="ExternalOutput")
        evs.append(P.dma("sp", lambda e: e.dma_start(out=dh.h.ap(), in_=B["hT"][:].rearrange("p k t -> p (k t)")),
                         B["hT"], reads=B["hTm"], writes=[]))
        evs.append(P.dma("sp", lambda e: e.dma_start(out=dhid.h.ap(), in_=B["hid"][:].rearrange("p c t -> p (c t)")),
                         B["hid"], reads=B["hidc"], writes=[]))
    P.emit(evs)
    return nc


def lay_wgu(w_in):
    w = np.asarray(w_in, dtype=np.float32).reshape(16, 128, 2, NCH, 128)
    return np.ascontiguousarray(w.transpose(3, 1, 0, 2, 4)).reshape(NCH, 128, 16 * 256)


def lay_wo(w_out):
    w = np.asarray(w_out, dtype=np.float32).reshape(NCH, 128, 4, 512)
    return np.ascontiguousarray(w.transpose(2, 1, 0, 3)).reshape(4, 128, NCH * 512)


_PROGS = {}


def run_ffn(h, w_in, w_out, g, b):
    if "ffn" not in _PROGS:
        _PROGS["ffn"] = build_ffn_prog()
    nc = _PROGS["ffn"]
    wgu = lay_wgu(w_in)
    wo = lay_wo(w_out)
    ident = np.eye(128, dtype=np.float32)
    g = np.ascontiguousarray(g, dtype=np.float32)
    b = np.ascontiguousarray(b, dtype=np.float32)
    in_maps = [{"x": h[i], "wgu": wgu, "wo": wo, "g": g, "b": b, "ident": ident} for i in range(NCORES)]
    res = run_bass_kernel_spmd(nc, in_maps, core_ids=list(range(NCORES)))
    return [res.results[i]["y"] for i in range(NCORES)]


def load_wres(P, wb, w_ap, ncols, kch):
    views = []
    nb = (ncols + 511) // 512
    for b in range(nb):
        c0, c1 = b * 512, min(ncols, (b + 1) * 512)
        v = P.view(wb, "wres%d" % b)
        views.append(v)
        P.dma("pool", lambda e, c0=c0, c1=c1: e.dma_start(
            out=wb[:, :, c0:c1], in_=w_ap.rearrange("p (k n) -> p k n", k=kch)[:, :, c0:c1]),
            v, writes=[v])
    return views


def emit_rope(P, eng, x_h, xT, H, Gt, J, cos_ap, sin_ap, tmp, tmpT, col0=0, hstride=None, tabs=()):
    hs = hstride if hstride is not None else Gt * 2 * J
    W = Gt * 2 * J

    def xv(f):
        if hs == W:
            v = x_h[:, col0:col0 + H * hs].rearrange("p (h g f j) -> p h g f j", h=H, g=Gt, f=2, j=J)
        else:
            v = x_h[:, 0:H * hs].rearrange("p (h w) -> p h w", h=H)[:, :, col0:col0 + W].rearrange(
                "p h (g f j) -> p h g f j", g=Gt, f=2, j=J)
        return v[:, :, :, f, :]

    def tv(i):
        return tmp[:, i * H * Gt * J:(i + 1) * H * Gt * J].rearrange("p (h g j) -> p h g j", h=H, g=Gt, j=J)

    def tb(ap):
        return ap.rearrange("p (g j) -> p g j", g=Gt, j=J).unsqueeze(1).broadcast_to([128, H, Gt, J])

    c, s_ = tb(cos_ap), tb(sin_ap)
    P.op(eng, lambda e: e.tensor_tensor(out=tv(0), in0=xv(0), in1=c, op=ALU.mult), reads=[xT] + list(tabs), writes=[tmpT])
    P.op(eng, lambda e: e.tensor_tensor(out=tv(1), in0=xv(1), in1=s_, op=ALU.mult), reads=[xT] + list(tabs), writes=[tmpT])
    P.op(eng, lambda e: e.tensor_tensor(out=tv(2), in0=xv(1), in1=c, op=ALU.mult), reads=[xT] + list(tabs), writes=[tmpT])
    P.op(eng, lambda e: e.tensor_tensor(out=tv(3), in0=xv(0), in1=s_, op=ALU.mult), reads=[xT] + list(tabs), writes=[tmpT])
    P.op(eng, lambda e: e.tensor_tensor(out=xv(0), in0=tv(0), in1=tv(1), op=ALU.subtract), reads=[tmpT], writes=[xT])
    P.op(eng, lambda e: e.tensor_tensor(out=xv(1), in0=tv(2), in1=tv(3), op=ALU.add), reads=[tmpT], writes=[xT])


def emit_proj_pass(P, C, x_in, row0_fn, ntiles, wviews, wb, ncols, post):
    banks = C.banks
    xs = [P.sb("p_xs%d" % i, [128, D], F32) for i in range(2)]
    hT = [P.sb("p_hT%d" % i, [128, 16, 128], BF16) for i in range(2)]
    qkv = [P.sb("p_qkv%d" % i, [128, ncols], F32) for i in range(2)]
    nb = (ncols + 511) // 512

    def load_x(i):
        x = xs[i % 2]
        r0 = row0_fn(i)
        P.dma("sp", lambda e: e.dma_start(out=x[:], in_=x_in.h.ap()[r0:r0 + 128, :]), x, writes=[x])

    load_x(0)
    for i in range(ntiles):
        if i + 1 < ntiles:
            load_x(i + 1)
        x, h, q = xs[i % 2], hT[i % 2], qkv[i % 2]
        for kb in range(4):
            bank = banks[4 + kb]
            for j in range(4):
                k = kb * 4 + j
                P.op("pe", lambda e, bank=bank, k=k, j=j, x=x: e.transpose(
                    out=bank[:, j * 128:(j + 1) * 128], in_=x[:, k * 128:(k + 1) * 128], identity=C.ident[:]),
                    reads=[x, C.ident], writes=[bank])
            emit_copy(P, C.copy_eng(), h[:, kb * 4:(kb + 1) * 4, :], bank[:].rearrange("p (k t) -> p k t", k=4),
                      reads=[], writes=[bank, h])
        for b in range(nb):
            c0, c1 = b * 512, min(ncols, (b + 1) * 512)
            bank = banks[b % 4]
            for k in range(16):
                P.op("pe", lambda e, bank=bank, k=k, c0=c0, c1=c1, h=h: e.matmul(
                    bank[:, 0:c1 - c0], lhsT=h[:, k, :], rhs=wb[:, k, c0:c1], start=(k == 0), stop=(k == 15)),
                    reads=[h, wviews[b]], writes=[bank])
            emit_copy(P, C.copy_eng(), q[:, c0:c1], bank[:, 0:c1 - c0], reads=[], writes=[bank, q])
        if i >= 1:
            post(i - 1, qkv[(i - 1) % 2])
    post(ntiles - 1, qkv[(ntiles - 1) % 2])


def emit_headT(P, C, src, srcT, col0, nh, width, stage, stageT, h0, m):
    banks = C.banks
    for g0 in range(0, nh, 4):
        g = min(4, nh - g0)
        bank = banks[4 + (C.cp % 4)]
        for j in range(g):
            c = col0[g0 + j]
            P.op("pe", lambda e, bank=bank, j=j, c=c: e.transpose(
                out=bank[0:width, j * 128:(j + 1) * 128], in_=src[:, c:c + width], identity=C.ident[:]),
                reads=[srcT, C.ident], writes=[bank])
        emit_copy(P, C.copy_eng(), stage[0:width, h0 + g0:h0 + g0 + g, m * 128:(m + 1) * 128],
                  bank[0:width, 0:g * 128].rearrange("p (k t) -> p k t", k=g), reads=[], writes=[bank, stageT])


def emit_attn_core(P, C, nseq, nout, sources_fn, scale, OT_d, masks=None):
    banks = C.banks
    NSRC = max(len(sources_fn(h)) for h in range(nout))
    kt_sb = [[P.sb("a_kt%d_%d" % (b, s), [128, SEQ], BF16) for s in range(NSRC)] for b in range(2)]
    v_sb = [[P.sb("a_v%d_%d" % (b, s), [128, 16, 128], BF16) for s in range(NSRC)] for b in range(2)]
    q_sb = [[P.sb("a_q%d_%d" % (b, s), [128, 512], BF16) for s in range(NSRC)] for b in range(2)]
    has_r = any(src.get("qr") is not None for src in sources_fn(0))
    if has_r:
        kr_sb = [P.sb("a_kr%d" % b, [64, SEQ], BF16) for b in range(2)]
        qr_sb = [P.sb("a_qr%d" % b, [64, 512], BF16) for b in range(2)]
    NPT = 6
    pt = [P.sb("a_pt%d" % i, [128, 512], BF16) for i in range(NPT)]
    ones = P.sb("a_ones", [128, 128], BF16)
    P.op("pool", lambda e: e.memset(ones[:], 1.0), writes=[ones])
    rden = [P.sb("a_rden%d" % i, [128, 512], F32) for i in range(2)]
    ot = [P.sb("a_ot%d" % i, [128, 512], BF16) for i in range(2)]
    ctr = {"hk": 0, "qk": 0, "pk": 0, "sk": 0}
    LOOK = 3
    stream = []

    class Unit:
        pass

    def make_unit(srcs, kb, t0, h, qt, head_loads):
        u = Unit()
        q0 = t0 + qt * 512
        qb = ctr["qk"] % 2
        ctr["qk"] += 1
        qk = ctr["qk"]
        work = []
        for si, src in enumerate(srcs):
            g = src.get("mask")
            if g is None:
                kts = list(range(16))
            else:
                Wd = masks["W"][g]
                kts = [kt for kt in range(16) if -Wd - 127 <= 128 * kt - 512 * qt <= Wd + 511]
            for kt in kts:
                work.append((si, kt, g))
        num = banks[4 + 2 * (qk % 2)]
        den = banks[5 + 2 * (qk % 2)]
        nw = len(work)
        pmap = {}

        def prologue():
            for si, src in enumerate(srcs):
                qd, qi = src["q"]
                P.dma("sp", lambda e, si=si, qd=qd, qi=qi: e.dma_start(out=q_sb[qb][si][:], in_=qd.h.ap()[qi][:, q0:q0 + 512]),
                      q_sb[qb][si], writes=[q_sb[qb][si]])
            if has_r:
                qd, qi = srcs[0]["qr"]
                P.dma("sp", lambda e, qd=qd, qi=qi: e.dma_start(out=qr_sb[qb][:], in_=qd.h.ap()[qi][:, q0:q0 + 512]),
                      qr_sb[qb], writes=[qr_sb[qb]])

        def issue_s(w):
            si, kt, g = work[w]
            bank = banks[ctr["sk"] % 4]
            ctr["sk"] += 1
            pmap[w] = bank
            P.op("pe", lambda e: e.matmul(
                bank[:], lhsT=kt_sb[kb][si][:, kt * 128:(kt + 1) * 128], rhs=q_sb[qb][si][:],
                start=True, stop=not has_r), reads=[kt_sb[kb][si], q_sb[qb][si]], writes=[bank])
            if has_r:
                P.op("pe", lambda e: e.matmul(
                    bank[:], lhsT=kr_sb[kb][:, kt * 128:(kt + 1) * 128], rhs=qr_sb[qb][:],
                    start=False, stop=True), reads=[kr_sb[kb], qr_sb[qb]], writes=[bank])

        def consume(w):
            si, kt, g = work[w]
            bank = pmap[w]
            p = pt[ctr["pk"] % NPT]
            ctr["pk"] += 1
            P.op("act", lambda e: e.activation(out=p[:], in_=bank[:], func=AF.Exp, scale=scale),
                 reads=[], writes=[bank, p])
            if g is not None:
                off = masks["CMAX"][g] - (128 * kt - 512 * qt)
                mt = masks["tab"][g]
                P.op("dve", lambda e: e.tensor_tensor(out=p[:], in0=p[:], in1=mt[:, off:off + 512], op=ALU.mult),
                     reads=[mt], writes=[p])
            P.op("pe", lambda e: e.matmul(num[:], lhsT=v_sb[kb][si][:, kt, :], rhs=p[:], start=(w == 0),
                                          stop=(w == nw - 1)), reads=[v_sb[kb][si], p], writes=[num])
            P.op("pe", lambda e: e.matmul(den[:], lhsT=ones[:], rhs=p[:], start=(w == 0), stop=(w == nw - 1)),
                 reads=[ones, p], writes=[den])
            if w == nw - 1:
                rd = rden[qk % 2]
                o = ot[qk % 2]
                P.op("dve", lambda e: e.reciprocal(out=rd[:], in_=den[:]), reads=[], writes=[den, rd])
                P.op("dve", lambda e: e.tensor_tensor(out=o[:], in0=num[:], in1=rd[:], op=ALU.mult),
                     reads=[rd], writes=[num, o])
                P.dma("sp", lambda e: e.dma_start(out=OT_d.h.ap()[h][:, q0:q0 + 512], in_=o[:]), o, reads=[o])

        u.issue_s, u.consume, u.nw, u.prologue, u.head_loads = issue_s, consume, nw, prologue, head_loads
        return u

    def head(sq, h):
        t0 = sq * SEQ
        srcs = sources_fn(h)
        kb = ctr["hk"] % 2
        ctr["hk"] += 1

        def head_loads():
            for si, src in enumerate(srcs):
                kd, ki = src["k"]
                vd, vc = src["v"]
                P.dma("sp", lambda e, si=si, kd=kd, ki=ki: e.dma_start(out=kt_sb[kb][si][:], in_=kd.h.ap()[ki][:, t0:t0 + SEQ]),
                      kt_sb[kb][si], writes=[kt_sb[kb][si]])
                P.dma("sp", lambda e, si=si, vd=vd, vc=vc: e.dma_start(
                    out=v_sb[kb][si][:], in_=vd.h.ap()[t0:t0 + SEQ, vc:vc + 128].rearrange("(t p) d -> p t d", p=128)),
                    v_sb[kb][si], writes=[v_sb[kb][si]])
            if has_r:
                kd, ki = srcs[0]["kr"]
                P.dma("sp", lambda e, kd=kd, ki=ki: e.dma_start(out=kr_sb[kb][:], in_=kd.h.ap()[ki][0:64, t0:t0 + SEQ]),
                      kr_sb[kb], writes=[kr_sb[kb]])

        for qt in range(SEQ // 512):
            u = make_unit(srcs, kb, t0, h, qt, head_loads if qt == 0 else None)
            units.append(u)
            for w in range(u.nw):
                stream.append((u, w))

    units = []
    for sq in range(nseq):
        for h in range(nout):
            head(sq, h)
    uidx = {id(u): i for i, u in enumerate(units)}
    units[0].head_loads()
    units[0].prologue()
    n = len(stream)
    for idx in range(n + LOOK):
        if idx < n:
            u, w = stream[idx]
            if w == 0:
                i = uidx[id(u)]
                if i + 1 < len(units):
                    units[i + 1].prologue()
                nq = SEQ // 512
                if i % nq == 1 and i + nq - 1 < len(units):
                    units[i + nq - 1].head_loads()
            u.issue_s(w)
        j = idx - LOOK
        if j >= 0:
            u, w = stream[j]
            u.consume(w)


def emit_oproj(P, C, x_in, x_out, OT_d, nh, wo_d, g_d, b_d, evs, ntok=NTOK):
    banks = C.banks
    wo = P.sb("o_wo", [128, nh, D], BF16)
    wov = load_wres(P, wo, wo_d.h.ap(), D, nh)
    g_t = P.sb("o_g", [128, D], F32)
    b_t = P.sb("o_b", [128, D], F32)
    P.dma("sp", lambda e: e.dma_start(out=g_t[:], in_=g_d.h.ap().partition_broadcast(128)), g_t, writes=[g_t])
    P.dma("sp", lambda e: e.dma_start(out=b_t[:], in_=b_d.h.ap().partition_broadcast(128)), b_t, writes=[b_t])
    xs = [P.sb("o_xs%d" % i, [128, 4, D], F32) for i in range(2)]
    xsm = [[P.view(xs[i], "o_xs%d_%d" % (i, m)) for m in range(4)] for i in range(2)]
    ots = [P.sb("o_ot%d" % i, [128, nh, 512], BF16) for i in range(2)]
    L = alloc_ln(P, "o_", 4)
    TT = 512
    dummy = [P.newT(None, "oy%d" % m) for m in range(4)]
    for t in range(ntok // TT):
        b = t % 2
        x, xm, o = xs[b], xsm[b], ots[b]
        P.dma("sp", lambda e, x=x, t=t: e.dma_start(
            out=x[:], in_=x_in.h.ap()[t * TT:(t + 1) * TT, :].rearrange("(m p) d -> p m d", p=128)), xm[0], writes=xm)
        P.dma("sp", lambda e, o=o, t=t: e.dma_start(
            out=o[:], in_=OT_d.h.ap().rearrange("h d t -> d h t")[:, :, t * TT:(t + 1) * TT]), o, writes=[o])
        for m in range(4):
            P.op("pool", lambda e, m=m, x=x: e.tensor_scalar(out=x[:, m, :], in0=x[:, m, :], scalar1=ALPHA, scalar2=0.0,
                                                        op0=ALU.mult, op1=ALU.add), reads=[], writes=[xm[m]])
        for n in range(4):
            for m in range(4):
                acc = banks[(n * 4 + m) % 8]
                for hh in range(nh):
                    P.op("pe", lambda e, acc=acc, hh=hh, m=m, n=n, o=o: e.matmul(
                        acc[:], lhsT=o[:, hh, m * 128:(m + 1) * 128], rhs=wo[:, hh, n * 512:(n + 1) * 512],
                        start=(hh == 0), stop=(hh == nh - 1)), reads=[o, wov[n]], writes=[acc])
                P.op("dve", lambda e, acc=acc, m=m, n=n, x=x: e.tensor_tensor(
                    out=x[:, m, n * 512:(n + 1) * 512], in0=acc[:], in1=x[:, m, n * 512:(n + 1) * 512], op=ALU.add),
                    reads=[], writes=[acc, xm[m]])
        emit_ln_store(P, L, xm, x, 4, g_t, b_t,
                      lambda m, t=t: x_out.h.ap()[t * TT + m * 128:t * TT + (m + 1) * 128, :], dummy, evs)


def emit_mixer_a(P, C, x_in, x_out, S, w_in_d, qg_d, kg_d, w_out_d, g_d, b_d, cosA_d, sinA_d, evs, ntok=NTOK):
    mk = P.mark()
    wb = P.sb("a_wb", [128, 16, 3072], BF16)
    wv = load_wres(P, wb, w_in_d.h.ap(), 3072, 16)
    gq = P.sb("a_gq", [128, 128], F32)
    gk = P.sb("a_gk", [128, 128], F32)
    P.dma("sp", lambda e: e.dma_start(out=gq[:], in_=qg_d.h.ap().partition_broadcast(128)), gq, writes=[gq])
    P.dma("sp", lambda e: e.dma_start(out=gk[:], in_=kg_d.h.ap().partition_broadcast(128)), gk, writes=[gk])
    cs = P.sb("a_cos", [128, 16, 64], F32)
    sn = P.sb("a_sin", [128, 16, 64], F32)
    P.dma("sp", lambda e: e.dma_start(out=cs[:], in_=cosA_d.h.ap().rearrange("(t p) j -> p t j", p=128)), cs, writes=[cs])
    P.dma("sp", lambda e: e.dma_start(out=sn[:], in_=sinA_d.h.ap().rearrange("(t p) j -> p t j", p=128)), sn, writes=[sn])
    ss = P.sb("a_ss", [128, 20], F32)
    tmp = P.sb("a_tmp", [128, 4 * 20 * 64], F32)
    sq = tmp
    stq = [P.sb("a_stq%d" % i, [128, 20, 512], BF16) for i in range(1)]
    vst = [P.sb("a_vst%d" % i, [128, 512], BF16) for i in range(2)]

    def post(i, q):
        m = i % 4
        st = stq[0]
        tpos = i % 16
        P.op("dve", lambda e: e.tensor_tensor(out=sq[:, 0:2560], in0=q[:, 0:2560], in1=q[:, 0:2560], op=ALU.mult),
             reads=[q], writes=[sq])
        P.op("dve", lambda e: e.tensor_reduce(out=ss[:], in_=sq[:, 0:2560].rearrange("p (h d) -> p h d", h=20),
                                              axis=mybir.AxisListType.X, op=ALU.add), reads=[sq], writes=[ss])
        P.op("dve", lambda e: e.tensor_scalar(out=ss[:], in0=ss[:], scalar1=1.0 / 128, scalar2=RMS_EPS,
                                              op0=ALU.mult, op1=ALU.add), reads=[], writes=[ss])
        P.op("act", lambda e: e.sqrt(out=ss[:], in_=ss[:]), reads=[], writes=[ss])
        P.op("dve", lambda e: e.reciprocal(out=ss[:], in_=ss[:]), reads=[], writes=[ss])
        q3 = q[:, 0:2560].rearrange("p (h d) -> p h d", h=20)
        P.op("dve", lambda e: e.tensor_tensor(out=q3, in0=q3, in1=ss[:].unsqueeze(2).broadcast_to([128, 20, 128]),
                                              op=ALU.mult), reads=[ss], writes=[q])
        P.op("pool", lambda e: e.tensor_tensor(out=q3[:, 0:16, :], in0=q3[:, 0:16, :],
                                               in1=gq[:].unsqueeze(1).broadcast_to([128, 16, 128]), op=ALU.mult),
             reads=[gq], writes=[q])
        P.op("pool", lambda e: e.tensor_tensor(out=q3[:, 16:20, :], in0=q3[:, 16:20, :],
                                               in1=gk[:].unsqueeze(1).broadcast_to([128, 4, 128]), op=ALU.mult),
             reads=[gk], writes=[q])
        emit_rope(P, "dve", q, q, 20, 2, 32, cs[:, tpos, :], sn[:, tpos, :], tmp, tmp, tabs=[cs, sn])
        emit_headT(P, C, q, q, [hh * 128 for hh in range(20)], 20, 128, st, st, 0, m)
        vs = vst[i % 2]
        P.op("act", lambda e: e.copy(out=vs[:], in_=q[:, 2560:3072]), reads=[q], writes=[vs])
        P.dma("sp", lambda e: e.dma_start(out=S["V"].h.ap()[i * 128:(i + 1) * 128, :], in_=vs[:]), vs, reads=[vs])
        if m == 3:
            t0 = (i // 4) * 512
            P.dma("sp", lambda e: e.dma_start(out=S["QT"].h.ap().rearrange("h d t -> d h t")[:, :, t0:t0 + 512],
                                              in_=st[:, 0:16, :]), st, reads=[st])
            P.dma("sp", lambda e: e.dma_start(out=S["KT"].h.ap().rearrange("h d t -> d h t")[:, :, t0:t0 + 512],
                                              in_=st[:, 16:20, :]), st, reads=[st])

    C.force = "act"
    emit_proj_pass(P, C, x_in, lambda i: i * 128, ntok // 128, wv, wb, 3072, post)
    C.force = None
    P.release(mk)
    _ph = 3
    if _ph < 2:
        return
    mk = P.mark()
    emit_attn_core(P, C, ntok // SEQ, 16,
                   lambda h: [dict(q=(S["QT"], h), k=(S["KT"], h // 4), v=(S["V"], (h // 4) * 128))],
                   128 ** -0.5, S["OT"])
    P.release(mk)
    if _ph < 3:
        return
    mk = P.mark()
    emit_oproj(P, C, x_in, x_out, S["OT"], 16, w_out_d, g_d, b_d, evs, ntok=ntok)
    P.release(mk)


def rope_tables_axial():
    pos = np.arange(SEQ)
    inv = 10000.0 ** (-np.arange(0, 64, 2, dtype=np.float32) / 64)
    ar = (pos // 64).astype(np.float32)[:, None] * inv[None, :]
    ac = (pos % 64).astype(np.float32)[:, None] * inv[None, :]
    cos = np.concatenate([np.cos(ar), np.cos(ac)], 1).astype(np.float32)
    sin = np.concatenate([np.sin(ar), np.sin(ac)], 1).astype(np.float32)
    return cos, sin


def lay_kn(w, kch):
    w = np.asarray(w, dtype=np.float32)
    n = w.shape[1]
    return np.ascontiguousarray(w.reshape(kch, 128, n).transpose(1, 0, 2)).reshape(128, kch * n)


def build_mixa_prog(ntok=NTOK):
    nc = bass.Bass("TRN2", target_bir_lowering=False)
    P = Prog(nc)
    x = P.dram("x", [ntok, D], F32, kind="ExternalInput")
    w_in = P.dram("w_in", [128, 16 * 3072], F32, kind="ExternalInput")
    w_out = P.dram("w_out", [128, 16 * 2048], F32, kind="ExternalInput")
    qg = P.dram("qg", [128], F32, kind="ExternalInput")
    kg = P.dram("kg", [128], F32, kind="ExternalInput")
    g = P.dram("g", [D], F32, kind="ExternalInput")
    b = P.dram("b", [D], F32, kind="ExternalInput")
    cosA = P.dram("cosA", [SEQ, 64], F32, kind="ExternalInput")
    sinA = P.dram("sinA", [SEQ, 64], F32, kind="ExternalInput")
    ident = P.dram("ident", [128, 128], F32, kind="ExternalInput")
    y = P.dram("y", [ntok, D], F32, kind="ExternalOutput")
    S = {"QT": P.dram("s_qt", [16, 128, ntok], BF16), "KT": P.dram("s_kt", [4, 128, ntok], BF16),
         "V": P.dram("s_v", [ntok, 512], BF16), "OT": P.dram("s_ot", [16, 128, ntok], BF16)}
    P.init_pool(40)
    C = Common(P, ident)
    P.persist += P.stage_tiles
    evs = []
    emit_mixer_a(P, C, x, y, S, w_in, qg, kg, w_out, g, b, cosA, sinA, evs, ntok=ntok)
    P.emit(evs)
    return nc


B_W = [64, 256, 1024]
B_DIL = [1, 4, 16]
B_CMAX = [w + 512 for w in B_W]
B_TW = 3200


def mask_tables_b():
    tabs = np.zeros((3, 128, B_TW), dtype=np.float32)
    kk = np.arange(128)[:, None]
    c = np.arange(B_TW)[None, :]
    for g in range(3):
        d = kk - c + B_CMAX[g]
        tabs[g] = ((np.abs(d) <= B_W[g]) & (d % B_DIL[g] == 0)).astype(np.float32)
    return tabs


def rope_tables_seq(dim):
    pos = np.arange(SEQ, dtype=np.float32)
    inv = 10000.0 ** (-np.arange(0, dim, 2, dtype=np.float32) / dim)
    ang = pos[:, None] * inv[None, :]
    return np.cos(ang).astype(np.float32), np.sin(ang).astype(np.float32)


def emit_mixer_b(P, C, x_in, x_out, S, w_in_d, w_out_d, g_d, b_d, cosB_d, sinB_d, mask_d, evs, ntok=NTOK):
    for g in range(3):
        mk = P.mark()
        wb = P.sb("b_wb", [128, 16, 3072], BF16)
        wv = load_wres(P, wb, w_in_d.h.ap()[g], 3072, 16)
        cs = P.sb("b_cos", [128, 16, 64], F32)
        sn = P.sb("b_sin", [128, 16, 64], F32)
        P.dma("sp", lambda e, cs=cs: e.dma_start(out=cs[:], in_=cosB_d.h.ap().rearrange("(t p) j -> p t j", p=128)),
              cs, writes=[cs])
        P.dma("sp", lambda e, sn=sn: e.dma_start(out=sn[:], in_=sinB_d.h.ap().rearrange("(t p) j -> p t j", p=128)),
              sn, writes=[sn])
        tmp = P.sb("b_tmp", [128, 4 * 16 * 64], F32)
        st = P.sb("b_st", [128, 16, 512], BF16)
        vst = [P.sb("b_vst%d" % i, [128, 1024], BF16) for i in range(2)]

        def post(i, q, g=g, cs=cs, sn=sn, tmp=tmp, st=st, vst=vst):
            m = i % 4
            tpos = i % 16
            emit_rope(P, "dve", q, q, 16, 1, 64, cs[:, tpos, :], sn[:, tpos, :], tmp, tmp, tabs=[cs, sn])
            emit_headT(P, C, q, q, [hh * 128 for hh in range(16)], 16, 128, st, st, 0, m)
            vs = vst[i % 2]
            P.op("act", lambda e: e.copy(out=vs[:], in_=q[:, 2048:3072]), reads=[q], writes=[vs])
            P.dma("sp", lambda e: e.dma_start(out=S["V"].h.ap()[i * 128:(i + 1) * 128, g * 1024:(g + 1) * 1024],
                                              in_=vs[:]), vs, reads=[vs])
            if m == 3:
                t0 = (i // 4) * 512
                P.dma("sp", lambda e: e.dma_start(
                    out=S["QT"].h.ap().rearrange("h d t -> d h t")[:, g * 8:(g + 1) * 8, t0:t0 + 512],
                    in_=st[:, 0:8, :]), st, reads=[st])
                P.dma("sp", lambda e: e.dma_start(
                    out=S["KT"].h.ap().rearrange("h d t -> d h t")[:, g * 8:(g + 1) * 8, t0:t0 + 512],
                    in_=st[:, 8:16, :]), st, reads=[st])

        C.force = "act"
        emit_proj_pass(P, C, x_in, lambda i: i * 128, ntok // 128, wv, wb, 3072, post)
        C.force = None
        P.release(mk)
    mk = P.mark()
    tabs = []
    for g in range(3):
        mt = P.sb("b_mask%d" % g, [128, B_TW], BF16)
        P.dma("pool", lambda e, mt=mt, g=g: e.dma_start(out=mt[:], in_=mask_d.h.ap()[g]), mt, writes=[mt])
        tabs.append(mt)
    masks = {"W": B_W, "CMAX": B_CMAX, "tab": tabs}
    emit_attn_core(P, C, ntok // SEQ, 8,
                   lambda h: [dict(q=(S["QT"], g * 8 + h), k=(S["KT"], g * 8 + h), v=(S["V"], g * 1024 + h * 128), mask=g)
                              for g in range(3)],
                   128 ** -0.5, S["OT"], masks=masks)
    P.release(mk)
    mk = P.mark()
    emit_oproj(P, C, x_in, x_out, S["OT"], 8, w_out_d, g_d, b_d, evs, ntok=ntok)
    P.release(mk)


def lay_w_in_b(w):
    w = np.asarray(w, dtype=np.float32).reshape(16, 128, 3, 3072)
    return np.ascontiguousarray(w.transpose(2, 1, 0, 3)).reshape(3, 128, 16 * 3072)


def build_mixb_prog(ntok=NTOK):
    nc = bass.Bass("TRN2", target_bir_lowering=False)
    P = Prog(nc)
    x = P.dram("x", [ntok, D], F32, kind="ExternalInput")
    w_in = P.dram("w_in", [3, 128, 16 * 3072], F32, kind="ExternalInput")
    w_out = P.dram("w_out", [128, 8 * 2048], F32, kind="ExternalInput")
    g = P.dram("g", [D], F32, kind="ExternalInput")
    b = P.dram("b", [D], F32, kind="ExternalInput")
    cosB = P.dram("cosB", [SEQ, 64], F32, kind="ExternalInput")
    sinB = P.dram("sinB", [SEQ, 64], F32, kind="ExternalInput")
    maskd = P.dram("maskB", [3, 128, B_TW], F32, kind="ExternalInput")
    ident = P.dram("ident", [128, 128], F32, kind="ExternalInput")
    y = P.dram("y", [ntok, D], F32, kind="ExternalOutput")
    S = {"QT": P.dram("s_qt", [24, 128, ntok], BF16), "KT": P.dram("s_kt", [24, 128, ntok], BF16),
         "V": P.dram("s_v", [ntok, 3072], BF16), "OT": P.dram("s_ot", [8, 128, ntok], BF16)}
    P.init_pool(40)
    C = Common(P, ident)
    P.persist += P.stage_tiles
    evs = []
    emit_mixer_b(P, C, x, y, S, w_in, w_out, g, b, cosB, sinB, maskd, evs, ntok=ntok)
    P.emit(evs)
    return nc


def emit_mixer_c(P, C, x_in, x_out, S, w_in_d, qg_d, kvg_d, wq_d, wkv_d, w_out_d, g_d, b_d, cosC_d, sinC_d, evs,
                 ntok=NTOK, only_p1=False):
    mk = P.mark()
    wb = P.sb("c_wb", [128, 16, 1536], BF16)
    wv = load_wres(P, wb, w_in_d.h.ap(), 1536, 16)
    gq = P.sb("c_gq", [128, 1280], F32)
    P.dma("sp", lambda e: e.dma_start(out=gq[:, 0:768], in_=qg_d.h.ap().partition_broadcast(128)), gq, writes=[gq])
    P.dma("sp", lambda e: e.dma_start(out=gq[:, 768:1280], in_=kvg_d.h.ap().partition_broadcast(128)), gq, writes=[gq])
    cs = P.sb("c_cos", [128, 16, 32], F32)
    sn = P.sb("c_sin", [128, 16, 32], F32)
    P.dma("sp", lambda e: e.dma_start(out=cs[:], in_=cosC_d.h.ap().rearrange("(t p) j -> p t j", p=128)), cs, writes=[cs])
    P.dma("sp", lambda e: e.dma_start(out=sn[:], in_=sinC_d.h.ap().rearrange("(t p) j -> p t j", p=128)), sn, writes=[sn])
    tmp = P.sb("c_tmp", [128, 1280], F32)
    ss = P.sb("c_ss", [128, 2], F32)
    stl = P.sb("c_stl", [128, 10, 512], BF16)
    stk4 = P.sb("c_stk4", [64, 4, 512], BF16)
    ctab = [P.sb("c_ctab%d" % i, [128, 64], F32) for i in range(2)]

    def post1(i, q):
        m = i % 4
        tpos = i % 16
        P.op("dve", lambda e: e.tensor_tensor(out=tmp[:], in0=q[:, 0:1280], in1=q[:, 0:1280], op=ALU.mult),
             reads=[q], writes=[tmp])
        P.op("dve", lambda e: e.tensor_reduce(out=ss[:, 0:1], in_=tmp[:, 0:768], axis=mybir.AxisListType.X, op=ALU.add),
             reads=[tmp], writes=[ss])
        P.op("dve", lambda e: e.tensor_reduce(out=ss[:, 1:2], in_=tmp[:, 768:1280], axis=mybir.AxisListType.X,
                                              op=ALU.add), reads=[tmp], writes=[ss])
        P.op("dve", lambda e: e.tensor_scalar(out=ss[:, 0:1], in0=ss[:, 0:1], scalar1=1.0 / 768, scalar2=RMS_EPS,
                                              op0=ALU.mult, op1=ALU.add), reads=[], writes=[ss])
        P.op("dve", lambda e: e.tensor_scalar(out=ss[:, 1:2], in0=ss[:, 1:2], scalar1=1.0 / 512, scalar2=RMS_EPS,
                                              op0=ALU.mult, op1=ALU.add), reads=[], writes=[ss])
        P.op("act", lambda e: e.sqrt(out=ss[:], in_=ss[:]), reads=[], writes=[ss])
        P.op("dve", lambda e: e.reciprocal(out=ss[:], in_=ss[:]), reads=[], writes=[ss])
        P.op("dve", lambda e: e.scalar_tensor_tensor(out=q[:, 0:768], in0=q[:, 0:768], scalar=ss[:, 0:1],
                                                     in1=gq[:, 0:768], op0=ALU.mult, op1=ALU.mult),
             reads=[ss, gq], writes=[q])
        P.op("dve", lambda e: e.scalar_tensor_tensor(out=q[:, 768:1280], in0=q[:, 768:1280], scalar=ss[:, 1:2],
                                                     in1=gq[:, 768:1280], op0=ALU.mult, op1=ALU.mult),
             reads=[ss, gq], writes=[q])
        x0, x1 = q[:, 1280:1312], q[:, 1312:1344]
        ct = ctab[i % 2]
        P.dma("sp", lambda e: e.dma_start(out=ct[:, 0:32], in_=cosC_d.h.ap()[tpos * 128:(tpos + 1) * 128, :]), ct, writes=[ct])
        P.dma("sp", lambda e: e.dma_start(out=ct[:, 32:64], in_=sinC_d.h.ap()[tpos * 128:(tpos + 1) * 128, :]), ct, writes=[ct])
        cc, sc = ct[:, 0:32], ct[:, 32:64]
        cs = sn = ct
        if "dbg_early" in S and i == 5:
            evs.append(P.dma("sp", lambda e: e.dma_start(out=S["dbg_q"].h.ap()[:, 0, :], in_=q[:, 1280:1344]), q, reads=[q]))
            evs.append(P.dma("sp", lambda e: e.dma_start(out=S["dbg_q"].h.ap()[:, 2, 0:32], in_=cc), cs, reads=[cs]))
            evs.append(P.dma("sp", lambda e: e.dma_start(out=S["dbg_q"].h.ap()[:, 2, 32:64], in_=sc), sn, reads=[sn]))
        tt = [tmp[:, j * 32:(j + 1) * 32] for j in range(4)]
        P.op("dve", lambda e: e.tensor_tensor(out=tt[0], in0=x0, in1=cc, op=ALU.mult), reads=[q, cs], writes=[tmp])
        P.op("dve", lambda e: e.tensor_tensor(out=tt[1], in0=x1, in1=sc, op=ALU.mult), reads=[q, sn], writes=[tmp])
        P.op("dve", lambda e: e.tensor_tensor(out=tt[2], in0=x1, in1=cc, op=ALU.mult), reads=[q, cs], writes=[tmp])
        P.op("dve", lambda e: e.tensor_tensor(out=tt[3], in0=x0, in1=sc, op=ALU.mult), reads=[q, sn], writes=[tmp])
        P.op("pool", lambda e: e.tensor_tensor(out=x0, in0=tt[0], in1=tt[1], op=ALU.subtract), reads=[tmp], writes=[q])
        P.op("pool", lambda e: e.tensor_tensor(out=x1, in0=tt[2], in1=tt[3], op=ALU.add), reads=[tmp], writes=[q])
        if "dbg_early" in S and i == 5:
            evs.append(P.dma("sp", lambda e: e.dma_start(out=S["dbg_q"].h.ap()[:, 1, :], in_=q[:, 1280:1344]), q, reads=[q]))
            evs.append(P.dma("sp", lambda e: e.dma_start(out=S["dbg_q"].h.ap()[:, 3, :], in_=tmp[:, 0:64]), tmp, reads=[tmp]))
        emit_headT(P, C, q, q, [k * 128 for k in range(10)], 10, 128, stl, stl, 0, m)
        emit_headT(P, C, q, q, [1280, 1280, 1280, 1280], 4, 64, stk4, stk4, 0, m)
        if m == 3:
            t0 = (i // 4) * 512
            P.dma("sp", lambda e: e.dma_start(out=S["LT"].h.ap().rearrange("h d t -> d h t")[:, 0:10, t0:t0 + 512],
                                              in_=stl[:]), stl, reads=[stl])
            P.dma("sp", lambda e: e.dma_start(out=S["LT"].h.ap()[10][0:64, t0:t0 + 512], in_=stk4[:, 0, :]),
                  stk4, reads=[stk4])

    C.force = "act"
    emit_proj_pass(P, C, x_in, lambda i: i * 128, ntok // 128, wv, wb, 1536, post1)
    C.force = None
    P.release(mk)
    if "dbg_early" in S:
        scr_e = P.sb("dbg_early_scr", [1, 8], F32)
        evs.append(P.dma("sp", lambda e: e.dma_start(out=S["dbg_early"].h.ap(), in_=S["LT"].h.ap()[10]), scr_e))
    if only_p1:
        return
    mk = P.mark()
    banks = C.banks
    wq = P.sb("c_wq", [128, 6, 3072], BF16)
    wqv = load_wres(P, wq, wq_d.h.ap(), 3072, 6)
    wkv = P.sb("c_wkv", [128, 4, 4096], BF16)
    wkvv = load_wres(P, wkv, wkv_d.h.ap(), 4096, 4)
    cs = P.sb("c_cos2", [128, 16, 32], F32)
    sn = P.sb("c_sin2", [128, 16, 32], F32)
    P.dma("sp", lambda e: e.dma_start(out=cs[:], in_=cosC_d.h.ap().rearrange("(t p) j -> p t j", p=128)), cs, writes=[cs])
    P.dma("sp", lambda e: e.dma_start(out=sn[:], in_=sinC_d.h.ap().rearrange("(t p) j -> p t j", p=128)), sn, writes=[sn])
    lat = [P.sb("c_lat%d" % i, [128, 11, 512], BF16) for i in range(2)]
    q2 = P.sb("c_q2", [128, 3072], F32)
    kv2 = P.sb("c_kv2", [128, 4096], F32)
    tmp2 = P.sb("c_tmp2", [128, 4 * 16 * 32], F32)
    sqt = P.sb("c_sqt", [128, 16, 512], BF16)
    sqr = P.sb("c_sqr", [64, 16, 512], BF16)
    skt = P.sb("c_skt", [128, 16, 512], BF16)
    vst = [P.sb("c_vst%d" % i, [128, 16, 128], BF16) for i in range(2)]

    def tile2(i):
        m = i % 4
        tpos = i % 16
        la = lat[(i // 4) % 2]
        if m == 0:
            t0 = (i // 4) * 512
            P.dma("sp", lambda e: e.dma_start(out=la[:], in_=S["LT"].h.ap().rearrange("h d t -> d h t")[:, :, t0:t0 + 512]),
                  la, writes=[la])
        for b in range(6):
            bank = banks[b % 4]
            for k in range(6):
                P.op("pe", lambda e, bank=bank, k=k, b=b: e.matmul(
                    bank[:], lhsT=la[:, k, m * 128:(m + 1) * 128], rhs=wq[:, k, b * 512:(b + 1) * 512],
                    start=(k == 0), stop=(k == 5)), reads=[la, wqv[b]], writes=[bank])
            emit_copy(P, C.copy_eng(), q2[:, b * 512:(b + 1) * 512], bank[:], reads=[], writes=[bank, q2])
        for b in range(8):
            bank = banks[(b + 2) % 4]
            for k in range(4):
                P.op("pe", lambda e, bank=bank, k=k, b=b: e.matmul(
                    bank[:], lhsT=la[:, 6 + k, m * 128:(m + 1) * 128], rhs=wkv[:, k, b * 512:(b + 1) * 512],
                    start=(k == 0), stop=(k == 3)), reads=[la, wkvv[b]], writes=[bank])
            emit_copy(P, C.copy_eng(), kv2[:, b * 512:(b + 1) * 512], bank[:], reads=[], writes=[bank, kv2])
        emit_rope(P, "dve", q2, q2, 16, 1, 32, cs[:, tpos, :], sn[:, tpos, :], tmp2, tmp2, col0=128, hstride=192, tabs=[cs, sn])
        emit_headT(P, C, q2, q2, [hh * 192 for hh in range(16)], 16, 128, sqt, sqt, 0, m)
        emit_headT(P, C, q2, q2, [hh * 192 + 128 for hh in range(16)], 16, 64, sqr, sqr, 0, m)
        emit_headT(P, C, kv2, kv2, [hh * 256 for hh in range(16)], 16, 128, skt, skt, 0, m)
        vs = vst[i % 2]
        P.op("act", lambda e: e.copy(out=vs[:], in_=kv2[:].rearrange("p (h c) -> p h c", h=16)[:, :, 128:256]),
             reads=[kv2], writes=[vs])
        P.dma("sp", lambda e: e.dma_start(out=S["V"].h.ap()[i * 128:(i + 1) * 128, :],
                                          in_=vs[:].rearrange("p h c -> p (h c)")), vs, reads=[vs])
        if m == 3:
            t0 = (i // 4) * 512
            for dst, stg in ((S["QT"], sqt), (S["QR"], sqr), (S["KT"], skt)):
                P.dma("sp", lambda e, dst=dst, stg=stg: e.dma_start(
                    out=dst.h.ap().rearrange("h d t -> d h t")[:, :, t0:t0 + 512], in_=stg[:]), stg, reads=[stg])

    C.force = "act"
    for i in range(ntok // 128):
        tile2(i)
    C.force = None
    P.release(mk)
    mk = P.mark()
    emit_attn_core(P, C, ntok // SEQ, 16,
                   lambda h: [dict(q=(S["QT"], h), k=(S["KT"], h), v=(S["V"], h * 128), qr=(S["QR"], h), kr=(S["LT"], 10))],
                   192 ** -0.5, S["OT"])
    P.release(mk)
    mk = P.mark()
    emit_oproj(P, C, x_in, x_out, S["OT"], 16, w_out_d, g_d, b_d, evs, ntok=ntok)
    P.release(mk)


def build_mixc_prog(ntok=NTOK, debug=False):
    nc = bass.Bass("TRN2", target_bir_lowering=False)
    P = Prog(nc)
    x = P.dram("x", [ntok, D], F32, kind="ExternalInput")
    w_in = P.dram("w_in", [128, 16 * 1536], F32, kind="ExternalInput")
    wq = P.dram("wq", [128, 6 * 3072], F32, kind="ExternalInput")
    wkv = P.dram("wkv", [128, 4 * 4096], F32, kind="ExternalInput")
    w_out = P.dram("w_out", [128, 16 * 2048], F32, kind="ExternalInput")
    qg = P.dram("qg", [768], F32, kind="ExternalInput")
    kvg = P.dram("kvg", [512], F32, kind="ExternalInput")
    g = P.dram("g", [D], F32, kind="ExternalInput")
    b = P.dram("b", [D], F32, kind="ExternalInput")
    cosC = P.dram("cosC", [SEQ, 32], F32, kind="ExternalInput")
    sinC = P.dram("sinC", [SEQ, 32], F32, kind="ExternalInput")
    ident = P.dram("ident", [128, 128], F32, kind="ExternalInput")
    y = P.dram("y", [ntok, D], F32, kind="ExternalOutput")
    S = {"LT": P.dram("s_lt", [11, 128, ntok], BF16),
         "QT": P.dram("s_qt", [16, 128, ntok], BF16), "QR": P.dram("s_qr", [16, 64, ntok], BF16),
         "KT": P.dram("s_kt", [16, 128, ntok], BF16), "V": P.dram("s_v", [ntok, 2048], BF16),
         "OT": P.dram("s_ot", [16, 128, ntok], BF16)}
    if debug:
        S["dbg_early"] = P.dram("dbg_early", [128, ntok], BF16, kind="ExternalOutput")
        S["dbg_q"] = P.dram("dbg_q", [128, 4, 64], F32, kind="ExternalOutput")
    P.init_pool(40)
    C = Common(P, ident)
    P.persist += P.stage_tiles
    evs = []
    emit_mixer_c(P, C, x, y, S, w_in, qg, kvg, wq, wkv, w_out, g, b, cosC, sinC, evs, ntok=ntok, only_p1=(debug == 2))
    if debug:
        for k, t in S.items():
            if k in ("dbg_early", "dbg_q"):
                continue
            if debug == 2 and k not in ("LT",):
                continue
            shp = [int(v) for v in t.h.shape]
            do = P.dram("dbg_" + k, shp, BF16, kind="ExternalOutput")
            scr = P.sb("dbgscr_" + k, [1, 8], F32)
            evs.append(P.dma("sp", lambda e, do=do, t=t: e.dma_start(out=do.h.ap(), in_=t.h.ap()), scr, writes=[do]))
    P.emit(evs)
    return nc


def lay_w_in_c(w):
    w = np.asarray(w, dtype=np.float32)
    wp = np.zeros((2048, 1536), dtype=np.float32)
    wp[:, :1344] = w
    return lay_kn(wp, 16)


MIX_KIND = ["a", "b", "c", "a"]


def build_full_prog(ntok=NTOK, nlayers=DEPTH):
    nc = bass.Bass("TRN2", target_bir_lowering=False)
    P = Prog(nc)
    I = {}

    def inp(name, shape):
        I[name] = P.dram(name, shape, F32, kind="ExternalInput")
        return I[name]

    x = inp("x", [ntok, D])
    ident = inp("ident", [128, 128])
    inp("cosA", [SEQ, 64]); inp("sinA", [SEQ, 64])
    inp("cosB", [SEQ, 64]); inp("sinB", [SEQ, 64])
    inp("maskB", [3, 128, B_TW])
    inp("cosC", [SEQ, 32]); inp("sinC", [SEQ, 32])
    for i in range(nlayers):
        for f in (1, 2):
            inp("wgu%d_%d" % (f, i), [NCH, 128, 16 * 256])
            inp("wo%d_%d" % (f, i), [4, 128, NCH * 512])
        for n in (1, 2, 3):
            inp("ln%d_g_%d" % (n, i), [D]); inp("ln%d_b_%d" % (n, i), [D])
        k = MIX_KIND[i]
        if k == "a":
            inp("a_w_in_%d" % i, [128, 16 * 3072]); inp("a_w_out_%d" % i, [128, 16 * 2048])
            inp("a_qg_%d" % i, [128]); inp("a_kg_%d" % i, [128])
        elif k == "b":
            inp("b_w_in_%d" % i, [3, 128, 16 * 3072]); inp("b_w_out_%d" % i, [128, 8 * 2048])
        else:
            inp("c_w_in_%d" % i, [128, 16 * 1536]); inp("c_wq_%d" % i, [128, 6 * 3072])
            inp("c_wkv_%d" % i, [128, 4 * 4096]); inp("c_w_out_%d" % i, [128, 16 * 2048])
            inp("c_qg_%d" % i, [768]); inp("c_kvg_%d" % i, [512])
    y = P.dram("y", [ntok, D], F32, kind="ExternalOutput")
    bufs = [P.dram("hbuf%d" % i, [ntok, D], F32) for i in range(2)]
    SA = {"QT": P.dram("sa_qt", [16, 128, ntok], BF16), "KT": P.dram("sa_kt", [4, 128, ntok], BF16),
          "V": P.dram("sa_v", [ntok, 512], BF16), "OT": P.dram("sa_ot", [16, 128, ntok], BF16)}
    SB = {"QT": P.dram("sb_qt", [24, 128, ntok], BF16), "KT": P.dram("sb_kt", [24, 128, ntok], BF16),
          "V": P.dram("sb_v", [ntok, 3072], BF16), "OT": P.dram("sb_ot", [8, 128, ntok], BF16)}
    SC = {"LT": P.dram("sc_lt", [11, 128, ntok], BF16),
          "QT": P.dram("sc_qt", [16, 128, ntok], BF16), "QR": P.dram("sc_qr", [16, 64, ntok], BF16),
          "KT": P.dram("sc_kt", [16, 128, ntok], BF16), "V": P.dram("sc_v", [ntok, 2048], BF16),
          "OT": P.dram("sc_ot", [16, 128, ntok], BF16)}
    P.init_pool(40)
    C = Common(P, ident)
    P.persist += P.stage_tiles
    evs = []
    nstage = 3 * nlayers
    src = x
    k = 0

    def nxt():
        return y if k == nstage - 1 else bufs[k % 2]

    def ffn(f, i, src, dst):
        mk = P.mark()
        B = alloc_ffn(P)
        xt = [P.newT(None, "xi") for _ in range(ntok // 128)]
        yt = [P.newT(None, "yo") for _ in range(ntok // 128)]
        e2 = []
        emit_ffn(P, C, B, src, xt, dst, yt, I["wgu%d_%d" % (f, i)], I["wo%d_%d" % (f, i)],
                 I["ln%d_g_%d" % (1 if f == 1 else 3, i)], I["ln%d_b_%d" % (1 if f == 1 else 3, i)], e2, ntok=ntok)
        P.release(mk)
        return e2

    for i in range(nlayers):
        dst = nxt()
        last = ffn(1, i, src, dst)
        src = dst
        k += 1
        dst = nxt()
        kind = MIX_KIND[i]
        last = []
        if kind == "a":
            emit_mixer_a(P, C, src, dst, SA, I["a_w_in_%d" % i], I["a_qg_%d" % i], I["a_kg_%d" % i], I["a_w_out_%d" % i],
                         I["ln2_g_%d" % i], I["ln2_b_%d" % i], I["cosA"], I["sinA"], last, ntok=ntok)
        elif kind == "b":
            emit_mixer_b(P, C, src, dst, SB, I["b_w_in_%d" % i], I["b_w_out_%d" % i], I["ln2_g_%d" % i], I["ln2_b_%d" % i],
                         I["cosB"], I["sinB"], I["maskB"], last, ntok=ntok)
        else:
            emit_mixer_c(P, C, src, dst, SC, I["c_w_in_%d" % i], I["c_qg_%d" % i], I["c_kvg_%d" % i], I["c_wq_%d" % i],
                         I["c_wkv_%d" % i], I["c_w_out_%d" % i], I["ln2_g_%d" % i], I["ln2_b_%d" % i],
                         I["cosC"], I["sinC"], last, ntok=ntok)
        src = dst
        k += 1
        dst = nxt()
        last = ffn(2, i, src, dst)
        src = dst
        k += 1
    P.emit(last)
    return nc, P


def host_inputs(inputs, nlayers=DEPTH):
    f32 = lambda a: np.ascontiguousarray(np.asarray(a, dtype=np.float32))
    H = {"ident": np.eye(128, dtype=np.float32)}
    H["cosA"], H["sinA"] = rope_tables_axial()
    H["cosB"], H["sinB"] = rope_tables_seq(128)
    H["cosC"], H["sinC"] = rope_tables_seq(64)
    H["maskB"] = mask_tables_b()
    for i in range(nlayers):
        for f in (1, 2):
            H["wgu%d_%d" % (f, i)] = lay_wgu(inputs["ffn%d_w_in_%d" % (f, i)])
            H["wo%d_%d" % (f, i)] = lay_wo(inputs["ffn%d_w_out_%d" % (f, i)])
        for n in (1, 2, 3):
            H["ln%d_g_%d" % (n, i)] = f32(inputs["ln%d_g_%d" % (n, i)])
            H["ln%d_b_%d" % (n, i)] = f32(inputs["ln%d_b_%d" % (n, i)])
        k = MIX_KIND[i]
        if k == "a":
            H["a_w_in_%d" % i] = lay_kn(inputs["a_w_in_%d" % i], 16)
            H["a_w_out_%d" % i] = lay_kn(inputs["a_w_out_%d" % i], 16)
            H["a_qg_%d" % i] = f32(inputs["a_q_gain_%d" % i])
            H["a_kg_%d" % i] = f32(inputs["a_k_gain_%d" % i])
        elif k == "b":
            H["b_w_in_%d" % i] = lay_w_in_b(inputs["b_w_in_%d" % i])
            H["b_w_out_%d" % i] = lay_kn(inputs["b_w_out_%d" % i], 8)
        else:
            H["c_w_in_%d" % i] = lay_w_in_c(inputs["c_w_in_%d" % i])
            H["c_wq_%d" % i] = lay_kn(inputs["c_w_q_up_%d" % i], 6)
            H["c_wkv_%d" % i] = lay_kn(inputs["c_w_kv_up_%d" % i], 4)
            H["c_w_out_%d" % i] = lay_kn(inputs["c_w_out_%d" % i], 16)
            H["c_qg_%d" % i] = f32(inputs["c_q_gain_%d" % i])
            H["c_kvg_%d" % i] = f32(inputs["c_kv_gain_%d" % i])
    return H


ALL_INPUT_NAMES = (
    "x",
    "ffn1_w_in_0",
    "ffn1_w_out_0",
    "ln1_g_0",
    "ln1_b_0",
    "a_w_in_0",
    "a_q_gain_0",
    "a_k_gain_0",
    "a_w_out_0",
    "ln2_g_0",
    "ln2_b_0",
    "ffn2_w_in_0",
    "ffn2_w_out_0",
    "ln3_g_0",
    "ln3_b_0",
    "ffn1_w_in_1",
    "ffn1_w_out_1",
    "ln1_g_1",
    "ln1_b_1",
    "b_w_in_1",
    "b_w_out_1",
    "ln2_g_1",
    "ln2_b_1",
    "ffn2_w_in_1",
    "ffn2_w_out_1",
    "ln3_g_1",
    "ln3_b_1",
    "ffn1_w_in_2",
    "ffn1_w_out_2",
    "ln1_g_2",
    "ln1_b_2",
    "c_w_in_2",
    "c_q_gain_2",
    "c_kv_gain_2",
    "c_w_q_up_2",
    "c_w_kv_up_2",
    "c_w_out_2",
    "ln2_g_2",
    "ln2_b_2",
    "ffn2_w_in_2",
    "ffn2_w_out_2",
    "ln3_g_2",
    "ln3_b_2",
    "ffn1_w_in_3",
    "ffn1_w_out_3",
    "ln1_g_3",
    "ln1_b_3",
    "a_w_in_3",
    "a_q_gain_3",
    "a_k_gain_3",
    "a_w_out_3",
    "ln2_g_3",
    "ln2_b_3",
    "ffn2_w_in_3",
    "ffn2_w_out_3",
    "ln3_g_3",
    "ln3_b_3",
)


def kernel(**inputs):
    missing = [n for n in ALL_INPUT_NAMES if n not in inputs]
    assert not missing, missing
    x = np.asarray(inputs["x"], dtype=np.float32).reshape(16 * SEQ, D)
    H = host_inputs(inputs)
    nc, _ = build_full_prog()
    in_maps = []
    for c in range(NCORES):
        m = dict(H)
        m["x"] = np.ascontiguousarray(x[c * NTOK:(c + 1) * NTOK])
        in_maps.append(m)
    res = run_bass_kernel_spmd(nc, in_maps, core_ids=list(range(NCORES)))
    out = np.concatenate([res.results[c]["y"] for c in range(NCORES)], axis=0)
    return out.reshape(16, SEQ, D).astype(np.float32)
```

```python
import numpy as np
import concourse.bass as bass
import concourse.mybir as mybir
from concourse.bass_utils import run_bass_kernel_spmd

F32 = mybir.dt.float32
BF16 = mybir.dt.bfloat16
AF = mybir.ActivationFunctionType
ALU = mybir.AluOpType

NCORES = 8
D = 2048
SEQ = 2048
NTOK = 4096
DFF = 5632
NCH = DFF // 128
DEPTH = 4
ALPHA = (2 * DEPTH) ** 0.25
LN_EPS = 1e-5
RMS_EPS = 1e-6

ENGS = ("pe", "act", "dve", "pool", "sp")
SAME_ENGINE_SYNC_ALL = False
SMALL_OPS = ("bn_stats", "bn_aggr", "sqrt", "reciprocal", "tensor_reduce", "memset")


class T:
    __slots__ = ("h", "name", "w", "r", "dsem", "dcnt")

    def __init__(self, h, name=""):
        self.h = h
        self.name = name
        self.w = []
        self.r = []
        self.dsem = None
        self.dcnt = 0

    def __getitem__(self, k):
        return self.h[k]


class Prog:
    def __init__(self, nc):
        self.nc = nc
        self.ops = {e: [] for e in ENGS}
        self.seen = {e: {} for e in ENGS}
        self.ctx = []
        self.nsb = 0
        self.tiles = []
        self.stage_tiles = []
        self.sempool = []
        self.bar = None

    def newT(self, h, name=""):
        t = T(h, name)
        self.tiles.append(t)
        self.stage_tiles.append(t)
        return t

    def view(self, t, name=""):
        return self.newT(t.h, name)

    def init_pool(self, n):
        for i in range(n):
            self.sempool.append((self.enter(self.nc.semaphore("dp%d" % i)), 0))
        self.bscr = {e: self.sb("bscr_" + e, [128, 8], F32) for e in ("act", "dve", "pool", "sp")}
        for en in ("act", "dve", "pool", "sp"):
            self.op("dve", lambda e, en=en: e.memset(self.bscr[en][:], 0.0), writes=[self.bscr[en]])
        self.bdram = self.dram("bar_dram", [128, 8], F32)
        self.bar = self.newT(None, "BAR")
        self.persist = list(self.stage_tiles)
        self.stage_tiles = []

    def mark(self):
        self.stage_tiles = []
        return len(self.ctx)

    def barrier(self):
        waits = []
        for t in self.stage_tiles + self.persist:
            for ev in t.w + t.r:
                self._need("act", ev, waits)
        scr = self.bscr
        idx = len(self.ops["act"])
        self.ops["act"].append({"fn": lambda e: e.copy(out=scr["act"][:, 0:1], in_=scr["act"][:, 1:2]),
                                "waits": waits, "flag": False, "dma": None})
        self.bar.w = [("eng", "act", idx)]
        self.bar.r = []
        for en in ("dve", "pool"):
            self.op(en, lambda e, en=en: e.memset(scr[en][:], 0.0), writes=[self.bar])
        self.dma("sp", lambda e: e.dma_start(out=self.bdram.h.ap(), in_=scr["sp"][:]), scr["sp"], writes=[self.bar])

    def release(self, mark):
        self.barrier()
        for t in self.stage_tiles:
            if t.dsem is not None:
                self.sempool.append((t.dsem, t.dcnt))
        self.stage_tiles = []
        while len(self.ctx) > mark:
            self.ctx.pop().__exit__(None, None, None)

    def enter(self, cm):
        v = cm.__enter__()
        self.ctx.append(cm)
        return v

    def sb(self, name, shape, dtype):
        self.nsb += 1
        name = "%s_u%d" % (name, self.nsb)
        return self.newT(self.enter(self.nc.sbuf_tensor(name, list(shape), dtype)), name)

    def ps(self, name, shape, dtype=F32):
        return self.newT(self.enter(self.nc.psum_tensor(name, list(shape), dtype)), name)

    def dram(self, name, shape, dtype, kind="Internal"):
        return self.newT(self.nc.dram_tensor(name, list(shape), dtype, kind=kind), name)

    def close(self):
        for cm in reversed(self.ctx):
            cm.__exit__(None, None, None)
        self.ctx = []

    def _need(self, eng, ev, waits):
        if ev[0] == "eng":
            _, e2, idx = ev
            if e2 == eng and (eng in ("pe", "sp") or not self.ops[e2][idx].get("small", True)):
                return
            if self.seen[eng].get(e2, -1) >= idx:
                return
            self.seen[eng][e2] = idx
            self.ops[e2][idx]["flag"] = True
            waits.append(ev)
        else:
            _, t, val = ev
            key = ("d", id(t))
            if self.seen[eng].get(key, -1) >= val:
                return
            self.seen[eng][key] = val
            waits.append(ev)

    def _deps(self, eng, reads, writes):
        waits = []
        for t in reads:
            for ev in t.w:
                self._need(eng, ev, waits)
        for t in writes:
            for ev in t.w:
                self._need(eng, ev, waits)
            for ev in t.r:
                self._need(eng, ev, waits)
        return waits

    def _commit(self, ev, reads, writes):
        src = ev[1] if ev[0] == "eng" else id(ev[1])
        for t in reads:
            if t in writes:
                continue
            t.r = [e for e in t.r if (e[1] if e[0] == "eng" else id(e[1])) != src]
            t.r.append(ev)
        for t in writes:
            t.w = [ev]
            t.r = []

    def op(self, eng, fn, reads=(), writes=(), small=False):
        waits = self._deps(eng, reads, writes)
        idx = len(self.ops[eng])
        small = small or SAME_ENGINE_SYNC_ALL or any(n in SMALL_OPS for n in fn.__code__.co_names)
        self.ops[eng].append({"fn": fn, "waits": waits, "flag": False, "dma": None, "small": small})
        self._commit(("eng", eng, idx), reads, writes)

    def dma(self, eng, fn, sb_tile, reads=(), writes=()):
        t = sb_tile
        if t.dsem is None:
            if self.sempool:
                t.dsem, t.dcnt = self.sempool.pop()
            else:
                t.dsem = self.enter(self.nc.semaphore("ds%d_%s" % (self.nsb, t.name)))
                self.nsb += 1
        waits = self._deps(eng, reads, writes)
        if t.dcnt > 0:
            self._need(eng, ("dma", t, t.dcnt), waits)
        t.dcnt += 16
        ev = ("dma", t, t.dcnt)
        self.ops[eng].append({"fn": fn, "waits": waits, "flag": False, "dma": t})
        self._commit(ev, reads, writes)
        return ev

    def emit(self, final_events=()):
        nc = self.nc
        fw = []
        for ev in final_events:
            self._need("sp", ev, fw)
        esem = {e: self.enter(nc.semaphore("es_" + e)) for e in ENGS}
        pref = {}
        for e in ENGS:
            c = 0
            p = []
            for o in self.ops[e]:
                if o["flag"]:
                    c += 1
                p.append(c)
            pref[e] = p

        def dowait(eo, ev):
            if ev[0] == "eng":
                eo.wait_ge(esem[ev[1]], pref[ev[1]][ev[2]])
            else:
                eo.wait_ge(ev[1].dsem, ev[2])

        def run(e, eo):
            for o in self.ops[e]:
                for ev in o["waits"]:
                    dowait(eo, ev)
                ins = o["fn"](eo)
                if o["dma"] is not None:
                    ins.then_inc(o["dma"].dsem, 16)
                elif o["flag"]:
                    ins.then_inc(esem[e], 1)
            if e == "sp":
                for ev in fw:
                    dowait(eo, ev)

        with nc.Block() as block:
            @block.tensor
            def _(eo):
                run("pe", eo)

            @block.scalar
            def _(eo):
                run("act", eo)

            @block.vector
            def _(eo):
                run("dve", eo)

            @block.gpsimd
            def _(eo):
                run("pool", eo)

            @block.sync
            def _(eo):
                run("sp", eo)
        self.close()


class Common:
    def __init__(self, P, ident_d):
        self.P = P
        self.banks = [P.ps("bank%d" % i, [128, 512], F32) for i in range(8)]
        self.ident = P.sb("ident_sb", [128, 128], F32)
        P.dma("sp", lambda e: e.dma_start(out=self.ident[:], in_=ident_d.h.ap()), self.ident,
              writes=[self.ident])
        self.cp = 0

    force = None

    def copy_eng(self):
        self.cp += 1
        if self.force is not None:
            return self.force
        return "dve" if self.cp % 2 else "act"


def emit_copy(P, eng, out, in_, reads, writes):
    if eng == "act":
        P.op("act", lambda e: e.copy(out=out, in_=in_), reads=reads, writes=writes)
    else:
        P.op(eng, lambda e: e.tensor_copy(out=out, in_=in_), reads=reads, writes=writes)


def emit_ln_store(P, L, xsm, xs_h, M, g_t, b_t, out_rows_fn, out_tiles, evs):
    for m in range(M):
        for j in range(4):
            P.op("dve", lambda e, m=m, j=j: e.bn_stats(out=L["st"][:, m, j, :],
                                                       in_=xs_h[:, m, j * 512:(j + 1) * 512]),
                 reads=[xsm[m]], writes=[L["stT"]])
        P.op("dve", lambda e, m=m: e.bn_aggr(out=L["mv"][:, m, :], in_=L["st"][:, m, :, :]),
             reads=[L["stT"]], writes=[L["mvT"]])
    P.op("dve", lambda e: e.tensor_scalar(out=L["rs"][:, 0:M], in0=L["mv"][:, 0:M, 1], scalar1=LN_EPS,
                                          scalar2=None, op0=ALU.add),
         reads=[L["mvT"]], writes=[L["rsT"]])
    P.op("act", lambda e: e.sqrt(out=L["rs"][:, 0:M], in_=L["rs"][:, 0:M]), reads=[], writes=[L["rsT"]])
    P.op("dve", lambda e: e.reciprocal(out=L["rs"][:, 0:M], in_=L["rs"][:, 0:M]), reads=[], writes=[L["rsT"]])
    for m in range(M):
        P.op("dve", lambda e, m=m: e.scalar_tensor_tensor(out=xs_h[:, m, :], in0=xs_h[:, m, :],
                                                          scalar=L["mv"][:, m, 0:1], in1=g_t[:],
                                                          op0=ALU.subtract, op1=ALU.mult),
             reads=[L["mvT"], g_t], writes=[xsm[m]])
        P.op("dve", lambda e, m=m: e.scalar_tensor_tensor(out=xs_h[:, m, :], in0=xs_h[:, m, :],
                                                          scalar=L["rs"][:, m:m + 1], in1=b_t[:],
                                                          op0=ALU.mult, op1=ALU.add),
             reads=[L["rsT"], b_t], writes=[xsm[m]])
        evs.append(P.dma("sp", lambda e, m=m: e.dma_start(out=out_rows_fn(m), in_=xs_h[:, m, :]), xsm[m],
                         reads=[xsm[m]], writes=[out_tiles[m]]))


def alloc_ln(P, pfx, M):
    L = {}
    L["st"] = P.sb(pfx + "st", [128, M, 4, 6], F32)
    L["stT"] = L["st"]
    L["mv"] = P.sb(pfx + "mv", [128, M, 2], F32)
    L["mvT"] = L["mv"]
    L["rs"] = P.sb(pfx + "rs", [128, M], F32)
    L["rsT"] = L["rs"]
    return L


def alloc_ffn(P):
    B = {}
    B["xs"] = [P.sb("f_xs%d" % i, [128, 4, D], F32) for i in range(2)]
    B["xsm"] = [[P.view(B["xs"][i], "f_xs%d_%d" % (i, m)) for m in range(4)] for i in range(2)]
    B["hT"] = P.sb("f_hT", [128, 16, 512], BF16)
    B["hTm"] = [P.view(B["hT"], "f_hT%d" % m) for m in range(4)]
    B["hid"] = P.sb("f_hid", [128, NCH, 512], BF16)
    B["hidc"] = [P.view(B["hid"], "f_hid%d" % c) for c in range(NCH)]
    B["wgu"] = [P.sb("f_wgu%d" % i, [128, 16, 256], BF16) for i in range(3)]
    B["wo"] = [P.sb("f_wo%d" % i, [128, 11, 512], BF16) for i in range(3)]
    B["sg"] = [P.sb("f_sg%d" % i, [128, 512], F32) for i in range(2)]
    B["g"] = P.sb("f_g", [128, D], F32)
    B["b"] = P.sb("f_b", [128, D], F32)
    B["ln"] = [alloc_ln(P, "f%d_" % m, 1) for m in range(4)]
    return B


def emit_ffn(P, C, B, x_in, x_in_tiles, x_out, x_out_tiles, wgu_d, wo_d, g_d, b_d, evs, ntok=NTOK):
    TT = 512
    ntile = ntok // TT
    banks = C.banks
    P.dma("sp", lambda e: e.dma_start(out=B["g"][:], in_=g_d.h.ap().partition_broadcast(128)), B["g"],
          writes=[B["g"]])
    P.dma("sp", lambda e: e.dma_start(out=B["b"][:], in_=b_d.h.ap().partition_broadcast(128)), B["b"],
          writes=[B["b"]])

    items = []
    for t in range(ntile):
        for c in range(NCH):
            items.append(("gu", t, c))
        for n in range(4):
            for s in range(4):
                items.append(("wo", t, n, s))
    cnt = {"gu": 0, "wo": 0}
    slot_of = {}

    def load(i):
        it = items[i]
        if it[0] == "gu":
            k = cnt["gu"]
            cnt["gu"] += 1
            w = B["wgu"][k % 3]
            slot_of[i] = w
            c = it[2]
            P.dma("pool", lambda e: e.dma_start(out=w[:], in_=wgu_d.h.ap()[c].rearrange("p (k j) -> p k j", k=16)),
                  w, writes=[w])
        else:
            k = cnt["wo"]
            cnt["wo"] += 1
            w = B["wo"][k % 3]
            slot_of[i] = w
            n, s = it[2], it[3]
            P.dma("pool", lambda e: e.dma_start(
                out=w[:], in_=wo_d.h.ap()[n].rearrange("p (c j) -> p c j", c=NCH)[:, s * 11:(s + 1) * 11, :]),
                w, writes=[w])

    def load_x(t):
        b = t % 2
        xs = B["xs"][b]
        P.dma("sp", lambda e: e.dma_start(
            out=xs[:], in_=x_in.h.ap()[t * TT:(t + 1) * TT, :].rearrange("(m p) d -> p m d", p=128)),
            B["xsm"][b][0], reads=x_in_tiles[t * 4:(t + 1) * 4], writes=B["xsm"][b])

    PD = 2
    for i in range(min(PD, len(items))):
        load(i)
    load_x(0)
    gi = 0
    pending = []
    for t in range(ntile):
        b = t % 2
        xs = B["xs"][b]
        xsm = B["xsm"][b]
        for m in range(4):
            for kb in range(4):
                bank = banks[4 + (m * 4 + kb) % 4]
                for j in range(4):
                    k = kb * 4 + j
                    P.op("pe", lambda e, bank=bank, m=m, k=k, j=j, xs=xs: e.transpose(
                        out=bank[:, j * 128:(j + 1) * 128], in_=xs[:, m, k * 128:(k + 1) * 128],
                        identity=C.ident[:]), reads=[xsm[m], C.ident], writes=[bank])
                emit_copy(P, C.copy_eng(), B["hT"][:, kb * 4:(kb + 1) * 4, m * 128:(m + 1) * 128],
                          bank[:].rearrange("p (k t) -> p k t", k=4), reads=[], writes=[bank, B["hTm"][m]])
            P.op("pool", lambda e, m=m, xs=xs: e.tensor_scalar(out=xs[:, m, :], in0=xs[:, m, :], scalar1=ALPHA,
                                                        scalar2=0.0, op0=ALU.mult, op1=ALU.add),
                 reads=[], writes=[xsm[m]])
        for c in range(NCH):
            if pending and c in (3, 9, 15, 21):
                pending.pop(0)()
            if gi + PD < len(items):
                load(gi + PD)
            w = slot_of[gi]
            gi += 1
            G = banks[(c % 2) * 2]
            U = banks[(c % 2) * 2 + 1]
            sg = B["sg"][c % 2]
            for half, bk in ((0, G), (1, U)):
                for ko in range(16):
                    P.op("pe", lambda e, bk=bk, w=w, ko=ko, half=half: e.matmul(
                        bk[:], lhsT=w[:, ko, half * 128:(half + 1) * 128], rhs=B["hT"][:, ko, :],
                        start=(ko == 0), stop=(ko == 15)), reads=[w] + B["hTm"], writes=[bk])
            P.op("act", lambda e, G=G, sg=sg: e.activation(out=sg[:], in_=G[:], func=AF.Silu),
                 reads=[], writes=[G, sg])
            P.op("dve", lambda e, U=U, sg=sg, c=c: e.tensor_tensor(out=B["hid"][:, c, :], in0=U[:], in1=sg[:],
                                                                   op=ALU.mult),
                 reads=[sg], writes=[U, B["hidc"][c]])
        if t + 1 < ntile:
            load_x(t + 1)
        for n in range(4):
            for s in range(4):
                if gi + PD < len(items):
                    load(gi + PD)
                w = slot_of[gi]
                gi += 1
                for m in range(4):
                    acc = banks[4 + m]
                    for cc in range(11):
                        c = s * 11 + cc
                        P.op("pe", lambda e, acc=acc, w=w, c=c, cc=cc, m=m: e.matmul(
                            acc[:], lhsT=B["hid"][:, c, m * 128:(m + 1) * 128], rhs=w[:, cc, :],
                            start=(c == 0), stop=(c == NCH - 1)), reads=[w, B["hidc"][c]], writes=[acc])
            for m in range(4):
                acc = banks[4 + m]
                P.op("dve", lambda e, acc=acc, m=m, n=n, xs=xs: e.scalar_tensor_tensor(
                    out=xs[:, m, n * 512:(n + 1) * 512], in0=acc[:], scalar=0.5,
                    in1=xs[:, m, n * 512:(n + 1) * 512], op0=ALU.mult, op1=ALU.add),
                    reads=[], writes=[acc, xsm[m]])
        for m in range(4):
            def ln_part(m=m, t=t, xs=xs, xsm=xsm):
                emit_ln_store(P, B["ln"][m], [xsm[m]], xs[:, m:m + 1, :], 1, B["g"], B["b"],
                              lambda _m: x_out.h.ap()[t * TT + m * 128:t * TT + (m + 1) * 128, :],
                              [x_out_tiles[t * 4 + m]], evs)
            pending.append(ln_part)
    while pending:
        pending.pop(0)()


def build_ffn_prog(ntok=NTOK, debug=False):
    nc = bass.Bass("TRN2", target_bir_lowering=False)
    P = Prog(nc)
    x = P.dram("x", [ntok, D], F32, kind="ExternalInput")
    wgu = P.dram("wgu", [NCH, 128, 16 * 256], F32, kind="ExternalInput")
    wo = P.dram("wo", [4, 128, NCH * 512], F32, kind="ExternalInput")
    g = P.dram("g", [D], F32, kind="ExternalInput")
    b = P.dram("b", [D], F32, kind="ExternalInput")
    ident = P.dram("ident", [128, 128], F32, kind="ExternalInput")
    y = P.dram("y", [ntok, D], F32, kind="ExternalOutput")
    C = Common(P, ident)
    B = alloc_ffn(P)
    xt = [P.newT(None, "xin%d" % i) for i in range(ntok // 128)]
    yt = [P.newT(None, "yout%d" % i) for i in range(ntok // 128)]
    evs = []
    emit_ffn(P, C, B, x, xt, y, yt, wgu, wo, g, b, evs, ntok=ntok)
    if debug:
        dh = P.dram("dbg_hT", [128, 16 * 512], BF16, kind="ExternalOutput")
        dhid = P.dram("dbg_hid", [128, NCH * 512], BF16, kind="ExternalOutput")
        evs.append(P.dma("sp", lambda e: e.dma_start(out=dh.h.ap(), in_=B["hT"][:].rearrange("p k t -> p (k t)")),
                         B["hT"], reads=B["hTm"], writes=[]))
        evs.append(P.dma("sp", lambda e: e.dma_start(out=dhid.h.ap(), in_=B["hid"][:].rearrange("p c t -> p (c t)")),
                         B["hid"], reads=B["hidc"], writes=[]))
    P.emit(evs)
    return nc


def lay_wgu(w_in):
    w = np.asarray(w_in, dtype=np.float32).reshape(16, 128, 2, NCH, 128)
    return np.ascontiguousarray(w.transpose(3, 1, 0, 2, 4)).reshape(NCH, 128, 16 * 256)


def lay_wo(w_out):
    w = np.asarray(w_out, dtype=np.float32).reshape(NCH, 128, 4, 512)
    return np.ascontiguousarray(w.transpose(2, 1, 0, 3)).reshape(4, 128, NCH * 512)


_PROGS = {}


def run_ffn(h, w_in, w_out, g, b):
    if "ffn" not in _PROGS:
        _PROGS["ffn"] = build_ffn_prog()
    nc = _PROGS["ffn"]
    wgu = lay_wgu(w_in)
    wo = lay_wo(w_out)
    ident = np.eye(128, dtype=np.float32)
    g = np.ascontiguousarray(g, dtype=np.float32)
    b = np.ascontiguousarray(b, dtype=np.float32)
    in_maps = [{"x": h[i], "wgu": wgu, "wo": wo, "g": g, "b": b, "ident": ident} for i in range(NCORES)]
    res = run_bass_kernel_spmd(nc, in_maps, core_ids=list(range(NCORES)))
    return [res.results[i]["y"] for i in range(NCORES)]


def load_wres(P, wb, w_ap, ncols, kch):
    views = []
    nb = (ncols + 511) // 512
    for b in range(nb):
        c0, c1 = b * 512, min(ncols, (b + 1) * 512)
        v = P.view(wb, "wres%d" % b)
        views.append(v)
        P.dma("pool", lambda e, c0=c0, c1=c1: e.dma_start(
            out=wb[:, :, c0:c1], in_=w_ap.rearrange("p (k n) -> p k n", k=kch)[:, :, c0:c1]),
            v, writes=[v])
    return views


def emit_rope(P, eng, x_h, xT, H, Gt, J, cos_ap, sin_ap, tmp, tmpT, col0=0, hstride=None, tabs=()):
    hs = hstride if hstride is not None else Gt * 2 * J
    W = Gt * 2 * J

    def xv(f):
        if hs == W:
            v = x_h[:, col0:col0 + H * hs].rearrange("p (h g f j) -> p h g f j", h=H, g=Gt, f=2, j=J)
        else:
            v = x_h[:, 0:H * hs].rearrange("p (h w) -> p h w", h=H)[:, :, col0:col0 + W].rearrange(
                "p h (g f j) -> p h g f j", g=Gt, f=2, j=J)
        return v[:, :, :, f, :]

    def tv(i):
        return tmp[:, i * H * Gt * J:(i + 1) * H * Gt * J].rearrange("p (h g j) -> p h g j", h=H, g=Gt, j=J)

    def tb(ap):
        return ap.rearrange("p (g j) -> p g j", g=Gt, j=J).unsqueeze(1).broadcast_to([128, H, Gt, J])

    c, s_ = tb(cos_ap), tb(sin_ap)
    P.op(eng, lambda e: e.tensor_tensor(out=tv(0), in0=xv(0), in1=c, op=ALU.mult), reads=[xT] + list(tabs), writes=[tmpT])
    P.op(eng, lambda e: e.tensor_tensor(out=tv(1), in0=xv(1), in1=s_, op=ALU.mult), reads=[xT] + list(tabs), writes=[tmpT])
    P.op(eng, lambda e: e.tensor_tensor(out=tv(2), in0=xv(1), in1=c, op=ALU.mult), reads=[xT] + list(tabs), writes=[tmpT])
    P.op(eng, lambda e: e.tensor_tensor(out=tv(3), in0=xv(0), in1=s_, op=ALU.mult), reads=[xT] + list(tabs), writes=[tmpT])
    P.op(eng, lambda e: e.tensor_tensor(out=xv(0), in0=tv(0), in1=tv(1), op=ALU.subtract), reads=[tmpT], writes=[xT])
    P.op(eng, lambda e: e.tensor_tensor(out=xv(1), in0=tv(2), in1=tv(3), op=ALU.add), reads=[tmpT], writes=[xT])


def emit_proj_pass(P, C, x_in, row0_fn, ntiles, wviews, wb, ncols, post):
    banks = C.banks
    xs = [P.sb("p_xs%d" % i, [128, D], F32) for i in range(2)]
    hT = [P.sb("p_hT%d" % i, [128, 16, 128], BF16) for i in range(2)]
    qkv = [P.sb("p_qkv%d" % i, [128, ncols], F32) for i in range(2)]
    nb = (ncols + 511) // 512

    def load_x(i):
        x = xs[i % 2]
        r0 = row0_fn(i)
        P.dma("sp", lambda e: e.dma_start(out=x[:], in_=x_in.h.ap()[r0:r0 + 128, :]), x, writes=[x])

    load_x(0)
    for i in range(ntiles):
        if i + 1 < ntiles:
            load_x(i + 1)
        x, h, q = xs[i % 2], hT[i % 2], qkv[i % 2]
        for kb in range(4):
            bank = banks[4 + kb]
            for j in range(4):
                k = kb * 4 + j
                P.op("pe", lambda e, bank=bank, k=k, j=j, x=x: e.transpose(
                    out=bank[:, j * 128:(j + 1) * 128], in_=x[:, k * 128:(k + 1) * 128], identity=C.ident[:]),
                    reads=[x, C.ident], writes=[bank])
            emit_copy(P, C.copy_eng(), h[:, kb * 4:(kb + 1) * 4, :], bank[:].rearrange("p (k t) -> p k t", k=4),
                      reads=[], writes=[bank, h])
        for b in range(nb):
            c0, c1 = b * 512, min(ncols, (b + 1) * 512)
            bank = banks[b % 4]
            for k in range(16):
                P.op("pe", lambda e, bank=bank, k=k, c0=c0, c1=c1, h=h: e.matmul(
                    bank[:, 0:c1 - c0], lhsT=h[:, k, :], rhs=wb[:, k, c0:c1], start=(k == 0), stop=(k == 15)),
                    reads=[h, wviews[b]], writes=[bank])
            emit_copy(P, C.copy_eng(), q[:, c0:c1], bank[:, 0:c1 - c0], reads=[], writes=[bank, q])
        if i >= 1:
            post(i - 1, qkv[(i - 1) % 2])
    post(ntiles - 1, qkv[(ntiles - 1) % 2])


def emit_headT(P, C, src, srcT, col0, nh, width, stage, stageT, h0, m):
    banks = C.banks
    for g0 in range(0, nh, 4):
        g = min(4, nh - g0)
        bank = banks[4 + (C.cp % 4)]
        for j in range(g):
            c = col0[g0 + j]
            P.op("pe", lambda e, bank=bank, j=j, c=c: e.transpose(
                out=bank[0:width, j * 128:(j + 1) * 128], in_=src[:, c:c + width], identity=C.ident[:]),
                reads=[srcT, C.ident], writes=[bank])
        emit_copy(P, C.copy_eng(), stage[0:width, h0 + g0:h0 + g0 + g, m * 128:(m + 1) * 128],
                  bank[0:width, 0:g * 128].rearrange("p (k t) -> p k t", k=g), reads=[], writes=[bank, stageT])


def emit_attn_core(P, C, nseq, nout, sources_fn, scale, OT_d, masks=None):
    banks = C.banks
    NSRC = max(len(sources_fn(h)) for h in range(nout))
    kt_sb = [[P.sb("a_kt%d_%d" % (b, s), [128, SEQ], BF16) for s in range(NSRC)] for b in range(2)]
    v_sb = [[P.sb("a_v%d_%d" % (b, s), [128, 16, 128], BF16) for s in range(NSRC)] for b in range(2)]
    q_sb = [[P.sb("a_q%d_%d" % (b, s), [128, 512], BF16) for s in range(NSRC)] for b in range(2)]
    has_r = any(src.get("qr") is not None for src in sources_fn(0))
    if has_r:
        kr_sb = [P.sb("a_kr%d" % b, [64, SEQ], BF16) for b in range(2)]
        qr_sb = [P.sb("a_qr%d" % b, [64, 512], BF16) for b in range(2)]
    NPT = 6
    pt = [P.sb("a_pt%d" % i, [128, 512], BF16) for i in range(NPT)]
    ones = P.sb("a_ones", [128, 128], BF16)
    P.op("pool", lambda e: e.memset(ones[:], 1.0), writes=[ones])
    rden = [P.sb("a_rden%d" % i, [128, 512], F32) for i in range(2)]
    ot = [P.sb("a_ot%d" % i, [128, 512], BF16) for i in range(2)]
    ctr = {"hk": 0, "qk": 0, "pk": 0, "sk": 0}
    LOOK = 3
    stream = []

    class Unit:
        pass

    def make_unit(srcs, kb, t0, h, qt, head_loads):
        u = Unit()
        q0 = t0 + qt * 512
        qb = ctr["qk"] % 2
        ctr["qk"] += 1
        qk = ctr["qk"]
        work = []
        for si, src in enumerate(srcs):
            g = src.get("mask")
            if g is None:
                kts = list(range(16))
            else:
                Wd = masks["W"][g]
                kts = [kt for kt in range(16) if -Wd - 127 <= 128 * kt - 512 * qt <= Wd + 511]
            for kt in kts:
                work.append((si, kt, g))
        num = banks[4 + 2 * (qk % 2)]
        den = banks[5 + 2 * (qk % 2)]
        nw = len(work)
        pmap = {}

        def prologue():
            for si, src in enumerate(srcs):
                qd, qi = src["q"]
                P.dma("sp", lambda e, si=si, qd=qd, qi=qi: e.dma_start(out=q_sb[qb][si][:], in_=qd.h.ap()[qi][:, q0:q0 + 512]),
                      q_sb[qb][si], writes=[q_sb[qb][si]])
            if has_r:
                qd, qi = srcs[0]["qr"]
                P.dma("sp", lambda e, qd=qd, qi=qi: e.dma_start(out=qr_sb[qb][:], in_=qd.h.ap()[qi][:, q0:q0 + 512]),
                      qr_sb[qb], writes=[qr_sb[qb]])

        def issue_s(w):
            si, kt, g = work[w]
            bank = banks[ctr["sk"] % 4]
            ctr["sk"] += 1
            pmap[w] = bank
            P.op("pe", lambda e: e.matmul(
                bank[:], lhsT=kt_sb[kb][si][:, kt * 128:(kt + 1) * 128], rhs=q_sb[qb][si][:],
                start=True, stop=not has_r), reads=[kt_sb[kb][si], q_sb[qb][si]], writes=[bank])
            if has_r:
                P.op("pe", lambda e: e.matmul(
                    bank[:], lhsT=kr_sb[kb][:, kt * 128:(kt + 1) * 128], rhs=qr_sb[qb][:],
                    start=False, stop=True), reads=[kr_sb[kb], qr_sb[qb]], writes=[bank])

        def consume(w):
            si, kt, g = work[w]
            bank = pmap[w]
            p = pt[ctr["pk"] % NPT]
            ctr["pk"] += 1
            P.op("act", lambda e: e.activation(out=p[:], in_=bank[:], func=AF.Exp, scale=scale),
                 reads=[], writes=[bank, p])
            if g is not None:
                off = masks["CMAX"][g] - (128 * kt - 512 * qt)
                mt = masks["tab"][g]
                P.op("dve", lambda e: e.tensor_tensor(out=p[:], in0=p[:], in1=mt[:, off:off + 512], op=ALU.mult),
                     reads=[mt], writes=[p])
            P.op("pe", lambda e: e.matmul(num[:], lhsT=v_sb[kb][si][:, kt, :], rhs=p[:], start=(w == 0),
                                          stop=(w == nw - 1)), reads=[v_sb[kb][si], p], writes=[num])
            P.op("pe", lambda e: e.matmul(den[:], lhsT=ones[:], rhs=p[:], start=(w == 0), stop=(w == nw - 1)),
                 reads=[ones, p], writes=[den])
            if w == nw - 1:
                rd = rden[qk % 2]
                o = ot[qk % 2]
                P.op("dve", lambda e: e.reciprocal(out=rd[:], in_=den[:]), reads=[], writes=[den, rd])
                P.op("dve", lambda e: e.tensor_tensor(out=o[:], in0=num[:], in1=rd[:], op=ALU.mult),
                     reads=[rd], writes=[num, o])
                P.dma("sp", lambda e: e.dma_start(out=OT_d.h.ap()[h][:, q0:q0 + 512], in_=o[:]), o, reads=[o])

        u.issue_s, u.consume, u.nw, u.prologue, u.head_loads = issue_s, consume, nw, prologue, head_loads
        return u

    def head(sq, h):
        t0 = sq * SEQ
        srcs = sources_fn(h)
        kb = ctr["hk"] % 2
        ctr["hk"] += 1

        def head_loads():
            for si, src in enumerate(srcs):
                kd, ki = src["k"]
                vd, vc = src["v"]
                P.dma("sp", lambda e, si=si, kd=kd, ki=ki: e.dma_start(out=kt_sb[kb][si][:], in_=kd.h.ap()[ki][:, t0:t0 + SEQ]),
                      kt_sb[kb][si], writes=[kt_sb[kb][si]])
                P.dma("sp", lambda e, si=si, vd=vd, vc=vc: e.dma_start(
                    out=v_sb[kb][si][:], in_=vd.h.ap()[t0:t0 + SEQ, vc:vc + 128].rearrange("(t p) d -> p t d", p=128)),
                    v_sb[kb][si], writes=[v_sb[kb][si]])
            if has_r:
                kd, ki = srcs[0]["kr"]
                P.dma("sp", lambda e, kd=kd, ki=ki: e.dma_start(out=kr_sb[kb][:], in_=kd.h.ap()[ki][0:64, t0:t0 + SEQ]),
                      kr_sb[kb], writes=[kr_sb[kb]])

        for qt in range(SEQ // 512):
            u = make_unit(srcs, kb, t0, h, qt, head_loads if qt == 0 else None)
            units.append(u)
            for w in range(u.nw):
                stream.append((u, w))

    units = []
    for sq in range(nseq):
        for h in range(nout):
            head(sq, h)
    uidx = {id(u): i for i, u in enumerate(units)}
    units[0].head_loads()
    units[0].prologue()
    n = len(stream)
    for idx in range(n + LOOK):
        if idx < n:
            u, w = stream[idx]
            if w == 0:
                i = uidx[id(u)]
                if i + 1 < len(units):
                    units[i + 1].prologue()
                nq = SEQ // 512
                if i % nq == 1 and i + nq - 1 < len(units):
                    units[i + nq - 1].head_loads()
            u.issue_s(w)
        j = idx - LOOK
        if j >= 0:
            u, w = stream[j]
            u.consume(w)


def emit_oproj(P, C, x_in, x_out, OT_d, nh, wo_d, g_d, b_d, evs, ntok=NTOK):
    banks = C.banks
    wo = P.sb("o_wo", [128, nh, D], BF16)
    wov = load_wres(P, wo, wo_d.h.ap(), D, nh)
    g_t = P.sb("o_g", [128, D], F32)
    b_t = P.sb("o_b", [128, D], F32)
    P.dma("sp", lambda e: e.dma_start(out=g_t[:], in_=g_d.h.ap().partition_broadcast(128)), g_t, writes=[g_t])
    P.dma("sp", lambda e: e.dma_start(out=b_t[:], in_=b_d.h.ap().partition_broadcast(128)), b_t, writes=[b_t])
    xs = [P.sb("o_xs%d" % i, [128, 4, D], F32) for i in range(2)]
    xsm = [[P.view(xs[i], "o_xs%d_%d" % (i, m)) for m in range(4)] for i in range(2)]
    ots = [P.sb("o_ot%d" % i, [128, nh, 512], BF16) for i in range(2)]
    L = alloc_ln(P, "o_", 4)
    TT = 512
    dummy = [P.newT(None, "oy%d" % m) for m in range(4)]
    for t in range(ntok // TT):
        b = t % 2
        x, xm, o = xs[b], xsm[b], ots[b]
        P.dma("sp", lambda e, x=x, t=t: e.dma_start(
            out=x[:], in_=x_in.h.ap()[t * TT:(t + 1) * TT, :].rearrange("(m p) d -> p m d", p=128)), xm[0], writes=xm)
        P.dma("sp", lambda e, o=o, t=t: e.dma_start(
            out=o[:], in_=OT_d.h.ap().rearrange("h d t -> d h t")[:, :, t * TT:(t + 1) * TT]), o, writes=[o])
        for m in range(4):
            P.op("pool", lambda e, m=m, x=x: e.tensor_scalar(out=x[:, m, :], in0=x[:, m, :], scalar1=ALPHA, scalar2=0.0,
                                                        op0=ALU.mult, op1=ALU.add), reads=[], writes=[xm[m]])
        for n in range(4):
            for m in range(4):
                acc = banks[(n * 4 + m) % 8]
                for hh in range(nh):
                    P.op("pe", lambda e, acc=acc, hh=hh, m=m, n=n, o=o: e.matmul(
                        acc[:], lhsT=o[:, hh, m * 128:(m + 1) * 128], rhs=wo[:, hh, n * 512:(n + 1) * 512],
                        start=(hh == 0), stop=(hh == nh - 1)), reads=[o, wov[n]], writes=[acc])
                P.op("dve", lambda e, acc=acc, m=m, n=n, x=x: e.tensor_tensor(
                    out=x[:, m, n * 512:(n + 1) * 512], in0=acc[:], in1=x[:, m, n * 512:(n + 1) * 512], op=ALU.add),
                    reads=[], writes=[acc, xm[m]])
        emit_ln_store(P, L, xm, x, 4, g_t, b_t,
                      lambda m, t=t: x_out.h.ap()[t * TT + m * 128:t * TT + (m + 1) * 128, :], dummy, evs)


def emit_mixer_a(P, C, x_in, x_out, S, w_in_d, qg_d, kg_d, w_out_d, g_d, b_d, cosA_d, sinA_d, evs, ntok=NTOK):
    mk = P.mark()
    wb = P.sb("a_wb", [128, 16, 3072], BF16)
    wv = load_wres(P, wb, w_in_d.h.ap(), 3072, 16)
    gq = P.sb("a_gq", [128, 128], F32)
    gk = P.sb("a_gk", [128, 128], F32)
    P.dma("sp", lambda e: e.dma_start(out=gq[:], in_=qg_d.h.ap().partition_broadcast(128)), gq, writes=[gq])
    P.dma("sp", lambda e: e.dma_start(out=gk[:], in_=kg_d.h.ap().partition_broadcast(128)), gk, writes=[gk])
    cs = P.sb("a_cos", [128, 16, 64], F32)
    sn = P.sb("a_sin", [128, 16, 64], F32)
    P.dma("sp", lambda e: e.dma_start(out=cs[:], in_=cosA_d.h.ap().rearrange("(t p) j -> p t j", p=128)), cs, writes=[cs])
    P.dma("sp", lambda e: e.dma_start(out=sn[:], in_=sinA_d.h.ap().rearrange("(t p) j -> p t j", p=128)), sn, writes=[sn])
    ss = P.sb("a_ss", [128, 20], F32)
    tmp = P.sb("a_tmp", [128, 4 * 20 * 64], F32)
    sq = tmp
    stq = [P.sb("a_stq%d" % i, [128, 20, 512], BF16) for i in range(1)]
    vst = [P.sb("a_vst%d" % i, [128, 512], BF16) for i in range(2)]

    def post(i, q):
        m = i % 4
        st = stq[0]
        tpos = i % 16
        P.op("dve", lambda e: e.tensor_tensor(out=sq[:, 0:2560], in0=q[:, 0:2560], in1=q[:, 0:2560], op=ALU.mult),
             reads=[q], writes=[sq])
        P.op("dve", lambda e: e.tensor_reduce(out=ss[:], in_=sq[:, 0:2560].rearrange("p (h d) -> p h d", h=20),
                                              axis=mybir.AxisListType.X, op=ALU.add), reads=[sq], writes=[ss])
        P.op("dve", lambda e: e.tensor_scalar(out=ss[:], in0=ss[:], scalar1=1.0 / 128, scalar2=RMS_EPS,
                                              op0=ALU.mult, op1=ALU.add), reads=[], writes=[ss])
        P.op("act", lambda e: e.sqrt(out=ss[:], in_=ss[:]), reads=[], writes=[ss])
        P.op("dve", lambda e: e.reciprocal(out=ss[:], in_=ss[:]), reads=[], writes=[ss])
        q3 = q[:, 0:2560].rearrange("p (h d) -> p h d", h=20)
        P.op("dve", lambda e: e.tensor_tensor(out=q3, in0=q3, in1=ss[:].unsqueeze(2).broadcast_to([128, 20, 128]),
                                              op=ALU.mult), reads=[ss], writes=[q])
        P.op("pool", lambda e: e.tensor_tensor(out=q3[:, 0:16, :], in0=q3[:, 0:16, :],
                                               in1=gq[:].unsqueeze(1).broadcast_to([128, 16, 128]), op=ALU.mult),
             reads=[gq], writes=[q])
        P.op("pool", lambda e: e.tensor_tensor(out=q3[:, 16:20, :], in0=q3[:, 16:20, :],
                                               in1=gk[:].unsqueeze(1).broadcast_to([128, 4, 128]), op=ALU.mult),
             reads=[gk], writes=[q])
        emit_rope(P, "dve", q, q, 20, 2, 32, cs[:, tpos, :], sn[:, tpos, :], tmp, tmp, tabs=[cs, sn])
        emit_headT(P, C, q, q, [hh * 128 for hh in range(20)], 20, 128, st, st, 0, m)
        vs = vst[i % 2]
        P.op("act", lambda e: e.copy(out=vs[:], in_=q[:, 2560:3072]), reads=[q], writes=[vs])
        P.dma("sp", lambda e: e.dma_start(out=S["V"].h.ap()[i * 128:(i + 1) * 128, :], in_=vs[:]), vs, reads=[vs])
        if m == 3:
            t0 = (i // 4) * 512
            P.dma("sp", lambda e: e.dma_start(out=S["QT"].h.ap().rearrange("h d t -> d h t")[:, :, t0:t0 + 512],
                                              in_=st[:, 0:16, :]), st, reads=[st])
            P.dma("sp", lambda e: e.dma_start(out=S["KT"].h.ap().rearrange("h d t -> d h t")[:, :, t0:t0 + 512],
                                              in_=st[:, 16:20, :]), st, reads=[st])

    C.force = "act"
    emit_proj_pass(P, C, x_in, lambda i: i * 128, ntok // 128, wv, wb, 3072, post)
    C.force = None
    P.release(mk)
    _ph = 3
    if _ph < 2:
        return
    mk = P.mark()
    emit_attn_core(P, C, ntok // SEQ, 16,
                   lambda h: [dict(q=(S["QT"], h), k=(S["KT"], h // 4), v=(S["V"], (h // 4) * 128))],
                   128 ** -0.5, S["OT"])
    P.release(mk)
    if _ph < 3:
        return
    mk = P.mark()
    emit_oproj(P, C, x_in, x_out, S["OT"], 16, w_out_d, g_d, b_d, evs, ntok=ntok)
    P.release(mk)


def rope_tables_axial():
    pos = np.arange(SEQ)
    inv = 10000.0 ** (-np.arange(0, 64, 2, dtype=np.float32) / 64)
    ar = (pos // 64).astype(np.float32)[:, None] * inv[None, :]
    ac = (pos % 64).astype(np.float32)[:, None] * inv[None, :]
    cos = np.concatenate([np.cos(ar), np.cos(ac)], 1).astype(np.float32)
    sin = np.concatenate([np.sin(ar), np.sin(ac)], 1).astype(np.float32)
    return cos, sin


def lay_kn(w, kch):
    w = np.asarray(w, dtype=np.float32)
    n = w.shape[1]
    return np.ascontiguousarray(w.reshape(kch, 128, n).transpose(1, 0, 2)).reshape(128, kch * n)


def build_mixa_prog(ntok=NTOK):
    nc = bass.Bass("TRN2", target_bir_lowering=False)
    P = Prog(nc)
    x = P.dram("x", [ntok, D], F32, kind="ExternalInput")
    w_in = P.dram("w_in", [128, 16 * 3072], F32, kind="ExternalInput")
    w_out = P.dram("w_out", [128, 16 * 2048], F32, kind="ExternalInput")
    qg = P.dram("qg", [128], F32, kind="ExternalInput")
    kg = P.dram("kg", [128], F32, kind="ExternalInput")
    g = P.dram("g", [D], F32, kind="ExternalInput")
    b = P.dram("b", [D], F32, kind="ExternalInput")
    cosA = P.dram("cosA", [SEQ, 64], F32, kind="ExternalInput")
    sinA = P.dram("sinA", [SEQ, 64], F32, kind="ExternalInput")
    ident = P.dram("ident", [128, 128], F32, kind="ExternalInput")
    y = P.dram("y", [ntok, D], F32, kind="ExternalOutput")
    S = {"QT": P.dram("s_qt", [16, 128, ntok], BF16), "KT": P.dram("s_kt", [4, 128, ntok], BF16),
         "V": P.dram("s_v", [ntok, 512], BF16), "OT": P.dram("s_ot", [16, 128, ntok], BF16)}
    P.init_pool(40)
    C = Common(P, ident)
    P.persist += P.stage_tiles
    evs = []
    emit_mixer_a(P, C, x, y, S, w_in, qg, kg, w_out, g, b, cosA, sinA, evs, ntok=ntok)
    P.emit(evs)
    return nc


B_W = [64, 256, 1024]
B_DIL = [1, 4, 16]
B_CMAX = [w + 512 for w in B_W]
B_TW = 3200


def mask_tables_b():
    tabs = np.zeros((3, 128, B_TW), dtype=np.float32)
    kk = np.arange(128)[:, None]
    c = np.arange(B_TW)[None, :]
    for g in range(3):
        d = kk - c + B_CMAX[g]
        tabs[g] = ((np.abs(d) <= B_W[g]) & (d % B_DIL[g] == 0)).astype(np.float32)
    return tabs


def rope_tables_seq(dim):
    pos = np.arange(SEQ, dtype=np.float32)
    inv = 10000.0 ** (-np.arange(0, dim, 2, dtype=np.float32) / dim)
    ang = pos[:, None] * inv[None, :]
    return np.cos(ang).astype(np.float32), np.sin(ang).astype(np.float32)


def emit_mixer_b(P, C, x_in, x_out, S, w_in_d, w_out_d, g_d, b_d, cosB_d, sinB_d, mask_d, evs, ntok=NTOK):
    for g in range(3):
        mk = P.mark()
        wb = P.sb("b_wb", [128, 16, 3072], BF16)
        wv = load_wres(P, wb, w_in_d.h.ap()[g], 3072, 16)
        cs = P.sb("b_cos", [128, 16, 64], F32)
        sn = P.sb("b_sin", [128, 16, 64], F32)
        P.dma("sp", lambda e, cs=cs: e.dma_start(out=cs[:], in_=cosB_d.h.ap().rearrange("(t p) j -> p t j", p=128)),
              cs, writes=[cs])
        P.dma("sp", lambda e, sn=sn: e.dma_start(out=sn[:], in_=sinB_d.h.ap().rearrange("(t p) j -> p t j", p=128)),
              sn, writes=[sn])
        tmp = P.sb("b_tmp", [128, 4 * 16 * 64], F32)
        st = P.sb("b_st", [128, 16, 512], BF16)
        vst = [P.sb("b_vst%d" % i, [128, 1024], BF16) for i in range(2)]

        def post(i, q, g=g, cs=cs, sn=sn, tmp=tmp, st=st, vst=vst):
            m = i % 4
            tpos = i % 16
            emit_rope(P, "dve", q, q, 16, 1, 64, cs[:, tpos, :], sn[:, tpos, :], tmp, tmp, tabs=[cs, sn])
            emit_headT(P, C, q, q, [hh * 128 for hh in range(16)], 16, 128, st, st, 0, m)
            vs = vst[i % 2]
            P.op("act", lambda e: e.copy(out=vs[:], in_=q[:, 2048:3072]), reads=[q], writes=[vs])
            P.dma("sp", lambda e: e.dma_start(out=S["V"].h.ap()[i * 128:(i + 1) * 128, g * 1024:(g + 1) * 1024],
                                              in_=vs[:]), vs, reads=[vs])
            if m == 3:
                t0 = (i // 4) * 512
                P.dma("sp", lambda e: e.dma_start(
                    out=S["QT"].h.ap().rearrange("h d t -> d h t")[:, g * 8:(g + 1) * 8, t0:t0 + 512],
                    in_=st[:, 0:8, :]), st, reads=[st])
                P.dma("sp", lambda e: e.dma_start(
                    out=S["KT"].h.ap().rearrange("h d t -> d h t")[:, g * 8:(g + 1) * 8, t0:t0 + 512],
                    in_=st[:, 8:16, :]), st, reads=[st])

        C.force = "act"
        emit_proj_pass(P, C, x_in, lambda i: i * 128, ntok // 128, wv, wb, 3072, post)
        C.force = None
        P.release(mk)
    mk = P.mark()
    tabs = []
    for g in range(3):
        mt = P.sb("b_mask%d" % g, [128, B_TW], BF16)
        P.dma("pool", lambda e, mt=mt, g=g: e.dma_start(out=mt[:], in_=mask_d.h.ap()[g]), mt, writes=[mt])
        tabs.append(mt)
    masks = {"W": B_W, "CMAX": B_CMAX, "tab": tabs}
    emit_attn_core(P, C, ntok // SEQ, 8,
                   lambda h: [dict(q=(S["QT"], g * 8 + h), k=(S["KT"], g * 8 + h), v=(S["V"], g * 1024 + h * 128), mask=g)
                              for g in range(3)],
                   128 ** -0.5, S["OT"], masks=masks)
    P.release(mk)
    mk = P.mark()
    emit_oproj(P, C, x_in, x_out, S["OT"], 8, w_out_d, g_d, b_d, evs, ntok=ntok)
    P.release(mk)


def lay_w_in_b(w):
    w = np.asarray(w, dtype=np.float32).reshape(16, 128, 3, 3072)
    return np.ascontiguousarray(w.transpose(2, 1, 0, 3)).reshape(3, 128, 16 * 3072)


def build_mixb_prog(ntok=NTOK):
    nc = bass.Bass("TRN2", target_bir_lowering=False)
    P = Prog(nc)
    x = P.dram("x", [ntok, D], F32, kind="ExternalInput")
    w_in = P.dram("w_in", [3, 128, 16 * 3072], F32, kind="ExternalInput")
    w_out = P.dram("w_out", [128, 8 * 2048], F32, kind="ExternalInput")
    g = P.dram("g", [D], F32, kind="ExternalInput")
    b = P.dram("b", [D], F32, kind="ExternalInput")
    cosB = P.dram("cosB", [SEQ, 64], F32, kind="ExternalInput")
    sinB = P.dram("sinB", [SEQ, 64], F32, kind="ExternalInput")
    maskd = P.dram("maskB", [3, 128, B_TW], F32, kind="ExternalInput")
    ident = P.dram("ident", [128, 128], F32, kind="ExternalInput")
    y = P.dram("y", [ntok, D], F32, kind="ExternalOutput")
    S = {"QT": P.dram("s_qt", [24, 128, ntok], BF16), "KT": P.dram("s_kt", [24, 128, ntok], BF16),
         "V": P.dram("s_v", [ntok, 3072], BF16), "OT": P.dram("s_ot", [8, 128, ntok], BF16)}
    P.init_pool(40)
    C = Common(P, ident)
    P.persist += P.stage_tiles
    evs = []
    emit_mixer_b(P, C, x, y, S, w_in, w_out, g, b, cosB, sinB, maskd, evs, ntok=ntok)
    P.emit(evs)
    return nc


def emit_mixer_c(P, C, x_in, x_out, S, w_in_d, qg_d, kvg_d, wq_d, wkv_d, w_out_d, g_d, b_d, cosC_d, sinC_d, evs,
                 ntok=NTOK, only_p1=False):
    mk = P.mark()
    wb = P.sb("c_wb", [128, 16, 1536], BF16)
    wv = load_wres(P, wb, w_in_d.h.ap(), 1536, 16)
    gq = P.sb("c_gq", [128, 1280], F32)
    P.dma("sp", lambda e: e.dma_start(out=gq[:, 0:768], in_=qg_d.h.ap().partition_broadcast(128)), gq, writes=[gq])
    P.dma("sp", lambda e: e.dma_start(out=gq[:, 768:1280], in_=kvg_d.h.ap().partition_broadcast(128)), gq, writes=[gq])
    cs = P.sb("c_cos", [128, 16, 32], F32)
    sn = P.sb("c_sin", [128, 16, 32], F32)
    P.dma("sp", lambda e: e.dma_start(out=cs[:], in_=cosC_d.h.ap().rearrange("(t p) j -> p t j", p=128)), cs, writes=[cs])
    P.dma("sp", lambda e: e.dma_start(out=sn[:], in_=sinC_d.h.ap().rearrange("(t p) j -> p t j", p=128)), sn, writes=[sn])
    tmp = P.sb("c_tmp", [128, 1280], F32)
    ss = P.sb("c_ss", [128, 2], F32)
    stl = P.sb("c_stl", [128, 10, 512], BF16)
    stk4 = P.sb("c_stk4", [64, 4, 512], BF16)
    ctab = [P.sb("c_ctab%d" % i, [128, 64], F32) for i in range(2)]

    def post1(i, q):
        m = i % 4
        tpos = i % 16
        P.op("dve", lambda e: e.tensor_tensor(out=tmp[:], in0=q[:, 0:1280], in1=q[:, 0:1280], op=ALU.mult),
             reads=[q], writes=[tmp])
        P.op("dve", lambda e: e.tensor_reduce(out=ss[:, 0:1], in_=tmp[:, 0:768], axis=mybir.AxisListType.X, op=ALU.add),
             reads=[tmp], writes=[ss])
        P.op("dve", lambda e: e.tensor_reduce(out=ss[:, 1:2], in_=tmp[:, 768:1280], axis=mybir.AxisListType.X,
                                              op=ALU.add), reads=[tmp], writes=[ss])
        P.op("dve", lambda e: e.tensor_scalar(out=ss[:, 0:1], in0=ss[:, 0:1], scalar1=1.0 / 768, scalar2=RMS_EPS,
                                              op0=ALU.mult, op1=ALU.add), reads=[], writes=[ss])
        P.op("dve", lambda e: e.tensor_scalar(out=ss[:, 1:2], in0=ss[:, 1:2], scalar1=1.0 / 512, scalar2=RMS_EPS,
                                              op0=ALU.mult, op1=ALU.add), reads=[], writes=[ss])
        P.op("act", lambda e: e.sqrt(out=ss[:], in_=ss[:]), reads=[], writes=[ss])
        P.op("dve", lambda e: e.reciprocal(out=ss[:], in_=ss[:]), reads=[], writes=[ss])
        P.op("dve", lambda e: e.scalar_tensor_tensor(out=q[:, 0:768], in0=q[:, 0:768], scalar=ss[:, 0:1],
                                                     in1=gq[:, 0:768], op0=ALU.mult, op1=ALU.mult),
             reads=[ss, gq], writes=[q])
        P.op("dve", lambda e: e.scalar_tensor_tensor(out=q[:, 768:1280], in0=q[:, 768:1280], scalar=ss[:, 1:2],
                                                     in1=gq[:, 768:1280], op0=ALU.mult, op1=ALU.mult),
             reads=[ss, gq], writes=[q])
        x0, x1 = q[:, 1280:1312], q[:, 1312:1344]
        ct = ctab[i % 2]
        P.dma("sp", lambda e: e.dma_start(out=ct[:, 0:32], in_=cosC_d.h.ap()[tpos * 128:(tpos + 1) * 128, :]), ct, writes=[ct])
        P.dma("sp", lambda e: e.dma_start(out=ct[:, 32:64], in_=sinC_d.h.ap()[tpos * 128:(tpos + 1) * 128, :]), ct, writes=[ct])
        cc, sc = ct[:, 0:32], ct[:, 32:64]
        cs = sn = ct
        if "dbg_early" in S and i == 5:
            evs.append(P.dma("sp", lambda e: e.dma_start(out=S["dbg_q"].h.ap()[:, 0, :], in_=q[:, 1280:1344]), q, reads=[q]))
            evs.append(P.dma("sp", lambda e: e.dma_start(out=S["dbg_q"].h.ap()[:, 2, 0:32], in_=cc), cs, reads=[cs]))
            evs.append(P.dma("sp", lambda e: e.dma_start(out=S["dbg_q"].h.ap()[:, 2, 32:64], in_=sc), sn, reads=[sn]))
        tt = [tmp[:, j * 32:(j + 1) * 32] for j in range(4)]
        P.op("dve", lambda e: e.tensor_tensor(out=tt[0], in0=x0, in1=cc, op=ALU.mult), reads=[q, cs], writes=[tmp])
        P.op("dve", lambda e: e.tensor_tensor(out=tt[1], in0=x1, in1=sc, op=ALU.mult), reads=[q, sn], writes=[tmp])
        P.op("dve", lambda e: e.tensor_tensor(out=tt[2], in0=x1, in1=cc, op=ALU.mult), reads=[q, cs], writes=[tmp])
        P.op("dve", lambda e: e.tensor_tensor(out=tt[3], in0=x0, in1=sc, op=ALU.mult), reads=[q, sn], writes=[tmp])
        P.op("pool", lambda e: e.tensor_tensor(out=x0, in0=tt[0], in1=tt[1], op=ALU.subtract), reads=[tmp], writes=[q])
        P.op("pool", lambda e: e.tensor_tensor(out=x1, in0=tt[2], in1=tt[3], op=ALU.add), reads=[tmp], writes=[q])
        if "dbg_early" in S and i == 5:
            evs.append(P.dma("sp", lambda e: e.dma_start(out=S["dbg_q"].h.ap()[:, 1, :], in_=q[:, 1280:1344]), q, reads=[q]))
            evs.append(P.dma("sp", lambda e: e.dma_start(out=S["dbg_q"].h.ap()[:, 3, :], in_=tmp[:, 0:64]), tmp, reads=[tmp]))
        emit_headT(P, C, q, q, [k * 128 for k in range(10)], 10, 128, stl, stl, 0, m)
        emit_headT(P, C, q, q, [1280, 1280, 1280, 1280], 4, 64, stk4, stk4, 0, m)
        if m == 3:
            t0 = (i // 4) * 512
            P.dma("sp", lambda e: e.dma_start(out=S["LT"].h.ap().rearrange("h d t -> d h t")[:, 0:10, t0:t0 + 512],
                                              in_=stl[:]), stl, reads=[stl])
            P.dma("sp", lambda e: e.dma_start(out=S["LT"].h.ap()[10][0:64, t0:t0 + 512], in_=stk4[:, 0, :]),
                  stk4, reads=[stk4])

    C.force = "act"
    emit_proj_pass(P, C, x_in, lambda i: i * 128, ntok // 128, wv, wb, 1536, post1)
    C.force = None
    P.release(mk)
    if "dbg_early" in S:
        scr_e = P.sb("dbg_early_scr", [1, 8], F32)
        evs.append(P.dma("sp", lambda e: e.dma_start(out=S["dbg_early"].h.ap(), in_=S["LT"].h.ap()[10]), scr_e))
    if only_p1:
        return
    mk = P.mark()
    banks = C.banks
    wq = P.sb("c_wq", [128, 6, 3072], BF16)
    wqv = load_wres(P, wq, wq_d.h.ap(), 3072, 6)
    wkv = P.sb("c_wkv", [128, 4, 4096], BF16)
    wkvv = load_wres(P, wkv, wkv_d.h.ap(), 4096, 4)
    cs = P.sb("c_cos2", [128, 16, 32], F32)
    sn = P.sb("c_sin2", [128, 16, 32], F32)
    P.dma("sp", lambda e: e.dma_start(out=cs[:], in_=cosC_d.h.ap().rearrange("(t p) j -> p t j", p=128)), cs, writes=[cs])
    P.dma("sp", lambda e: e.dma_start(out=sn[:], in_=sinC_d.h.ap().rearrange("(t p) j -> p t j", p=128)), sn, writes=[sn])
    lat = [P.sb("c_lat%d" % i, [128, 11, 512], BF16) for i in range(2)]
    q2 = P.sb("c_q2", [128, 3072], F32)
    kv2 = P.sb("c_kv2", [128, 4096], F32)
    tmp2 = P.sb("c_tmp2", [128, 4 * 16 * 32], F32)
    sqt = P.sb("c_sqt", [128, 16, 512], BF16)
    sqr = P.sb("c_sqr", [64, 16, 512], BF16)
    skt = P.sb("c_skt", [128, 16, 512], BF16)
    vst = [P.sb("c_vst%d" % i, [128, 16, 128], BF16) for i in range(2)]

    def tile2(i):
        m = i % 4
        tpos = i % 16
        la = lat[(i // 4) % 2]
        if m == 0:
            t0 = (i // 4) * 512
            P.dma("sp", lambda e: e.dma_start(out=la[:], in_=S["LT"].h.ap().rearrange("h d t -> d h t")[:, :, t0:t0 + 512]),
                  la, writes=[la])
        for b in range(6):
            bank = banks[b % 4]
            for k in range(6):
                P.op("pe", lambda e, bank=bank, k=k, b=b: e.matmul(
                    bank[:], lhsT=la[:, k, m * 128:(m + 1) * 128], rhs=wq[:, k, b * 512:(b + 1) * 512],
                    start=(k == 0), stop=(k == 5)), reads=[la, wqv[b]], writes=[bank])
            emit_copy(P, C.copy_eng(), q2[:, b * 512:(b + 1) * 512], bank[:], reads=[], writes=[bank, q2])
        for b in range(8):
            bank = banks[(b + 2) % 4]
            for k in range(4):
                P.op("pe", lambda e, bank=bank, k=k, b=b: e.matmul(
                    bank[:], lhsT=la[:, 6 + k, m * 128:(m + 1) * 128], rhs=wkv[:, k, b * 512:(b + 1) * 512],
                    start=(k == 0), stop=(k == 3)), reads=[la, wkvv[b]], writes=[bank])
            emit_copy(P, C.copy_eng(), kv2[:, b * 512:(b + 1) * 512], bank[:], reads=[], writes=[bank, kv2])
        emit_rope(P, "dve", q2, q2, 16, 1, 32, cs[:, tpos, :], sn[:, tpos, :], tmp2, tmp2, col0=128, hstride=192, tabs=[cs, sn])
        emit_headT(P, C, q2, q2, [hh * 192 for hh in range(16)], 16, 128, sqt, sqt, 0, m)
        emit_headT(P, C, q2, q2, [hh * 192 + 128 for hh in range(16)], 16, 64, sqr, sqr, 0, m)
        emit_headT(P, C, kv2, kv2, [hh * 256 for hh in range(16)], 16, 128, skt, skt, 0, m)
        vs = vst[i % 2]
        P.op("act", lambda e: e.copy(out=vs[:], in_=kv2[:].rearrange("p (h c) -> p h c", h=16)[:, :, 128:256]),
             reads=[kv2], writes=[vs])
        P.dma("sp", lambda e: e.dma_start(out=S["V"].h.ap()[i * 128:(i + 1) * 128, :],
                                          in_=vs[:].rearrange("p h c -> p (h c)")), vs, reads=[vs])
        if m == 3:
            t0 = (i // 4) * 512
            for dst, stg in ((S["QT"], sqt), (S["QR"], sqr), (S["KT"], skt)):
                P.dma("sp", lambda e, dst=dst, stg=stg: e.dma_start(
                    out=dst.h.ap().rearrange("h d t -> d h t")[:, :, t0:t0 + 512], in_=stg[:]), stg, reads=[stg])

    C.force = "act"
    for i in range(ntok // 128):
        tile2(i)
    C.force = None
    P.release(mk)
    mk = P.mark()
    emit_attn_core(P, C, ntok // SEQ, 16,
                   lambda h: [dict(q=(S["QT"], h), k=(S["KT"], h), v=(S["V"], h * 128), qr=(S["QR"], h), kr=(S["LT"], 10))],
                   192 ** -0.5, S["OT"])
    P.release(mk)
    mk = P.mark()
    emit_oproj(P, C, x_in, x_out, S["OT"], 16, w_out_d, g_d, b_d, evs, ntok=ntok)
    P.release(mk)


def build_mixc_prog(ntok=NTOK, debug=False):
    nc = bass.Bass("TRN2", target_bir_lowering=False)
    P = Prog(nc)
    x = P.dram("x", [ntok, D], F32, kind="ExternalInput")
    w_in = P.dram("w_in", [128, 16 * 1536], F32, kind="ExternalInput")
    wq = P.dram("wq", [128, 6 * 3072], F32, kind="ExternalInput")
    wkv = P.dram("wkv", [128, 4 * 4096], F32, kind="ExternalInput")
    w_out = P.dram("w_out", [128, 16 * 2048], F32, kind="ExternalInput")
    qg = P.dram("qg", [768], F32, kind="ExternalInput")
    kvg = P.dram("kvg", [512], F32, kind="ExternalInput")
    g = P.dram("g", [D], F32, kind="ExternalInput")
    b = P.dram("b", [D], F32, kind="ExternalInput")
    cosC = P.dram("cosC", [SEQ, 32], F32, kind="ExternalInput")
    sinC = P.dram("sinC", [SEQ, 32], F32, kind="ExternalInput")
    ident = P.dram("ident", [128, 128], F32, kind="ExternalInput")
    y = P.dram("y", [ntok, D], F32, kind="ExternalOutput")
    S = {"LT": P.dram("s_lt", [11, 128, ntok], BF16),
         "QT": P.dram("s_qt", [16, 128, ntok], BF16), "QR": P.dram("s_qr", [16, 64, ntok], BF16),
         "KT": P.dram("s_kt", [16, 128, ntok], BF16), "V": P.dram("s_v", [ntok, 2048], BF16),
         "OT": P.dram("s_ot", [16, 128, ntok], BF16)}
    if debug:
        S["dbg_early"] = P.dram("dbg_early", [128, ntok], BF16, kind="ExternalOutput")
        S["dbg_q"] = P.dram("dbg_q", [128, 4, 64], F32, kind="ExternalOutput")
    P.init_pool(40)
    C = Common(P, ident)
    P.persist += P.stage_tiles
    evs = []
    emit_mixer_c(P, C, x, y, S, w_in, qg, kvg, wq, wkv, w_out, g, b, cosC, sinC, evs, ntok=ntok, only_p1=(debug == 2))
    if debug:
        for k, t in S.items():
            if k in ("dbg_early", "dbg_q"):
                continue
            if debug == 2 and k not in ("LT",):
                continue
            shp = [int(v) for v in t.h.shape]
            do = P.dram("dbg_" + k, shp, BF16, kind="ExternalOutput")
            scr = P.sb("dbgscr_" + k, [1, 8], F32)
            evs.append(P.dma("sp", lambda e, do=do, t=t: e.dma_start(out=do.h.ap(), in_=t.h.ap()), scr, writes=[do]))
    P.emit(evs)
    return nc


def lay_w_in_c(w):
    w = np.asarray(w, dtype=np.float32)
    wp = np.zeros((2048, 1536), dtype=np.float32)
    wp[:, :1344] = w
    return lay_kn(wp, 16)


MIX_KIND = ["a", "b", "c", "a"]


def build_full_prog(ntok=NTOK, nlayers=DEPTH):
    nc = bass.Bass("TRN2", target_bir_lowering=False)
    P = Prog(nc)
    I = {}

    def inp(name, shape):
        I[name] = P.dram(name, shape, F32, kind="ExternalInput")
        return I[name]

    x = inp("x", [ntok, D])
    ident = inp("ident", [128, 128])
    inp("cosA", [SEQ, 64]); inp("sinA", [SEQ, 64])
    inp("cosB", [SEQ, 64]); inp("sinB", [SEQ, 64])
    inp("maskB", [3, 128, B_TW])
    inp("cosC", [SEQ, 32]); inp("sinC", [SEQ, 32])
    for i in range(nlayers):
        for f in (1, 2):
            inp("wgu%d_%d" % (f, i), [NCH, 128, 16 * 256])
            inp("wo%d_%d" % (f, i), [4, 128, NCH * 512])
        for n in (1, 2, 3):
            inp("ln%d_g_%d" % (n, i), [D]); inp("ln%d_b_%d" % (n, i), [D])
        k = MIX_KIND[i]
        if k == "a":
            inp("a_w_in_%d" % i, [128, 16 * 3072]); inp("a_w_out_%d" % i, [128, 16 * 2048])
            inp("a_qg_%d" % i, [128]); inp("a_kg_%d" % i, [128])
        elif k == "b":
            inp("b_w_in_%d" % i, [3, 128, 16 * 3072]); inp("b_w_out_%d" % i, [128, 8 * 2048])
        else:
            inp("c_w_in_%d" % i, [128, 16 * 1536]); inp("c_wq_%d" % i, [128, 6 * 3072])
            inp("c_wkv_%d" % i, [128, 4 * 4096]); inp("c_w_out_%d" % i, [128, 16 * 2048])
            inp("c_qg_%d" % i, [768]); inp("c_kvg_%d" % i, [512])
    y = P.dram("y", [ntok, D], F32, kind="ExternalOutput")
    bufs = [P.dram("hbuf%d" % i, [ntok, D], F32) for i in range(2)]
    SA = {"QT": P.dram("sa_qt", [16, 128, ntok], BF16), "KT": P.dram("sa_kt", [4, 128, ntok], BF16),
          "V": P.dram("sa_v", [ntok, 512], BF16), "OT": P.dram("sa_ot", [16, 128, ntok], BF16)}
    SB = {"QT": P.dram("sb_qt", [24, 128, ntok], BF16), "KT": P.dram("sb_kt", [24, 128, ntok], BF16),
          "V": P.dram("sb_v", [ntok, 3072], BF16), "OT": P.dram("sb_ot", [8, 128, ntok], BF16)}
    SC = {"LT": P.dram("sc_lt", [11, 128, ntok], BF16),
          "QT": P.dram("sc_qt", [16, 128, ntok], BF16), "QR": P.dram("sc_qr", [16, 64, ntok], BF16),
          "KT": P.dram("sc_kt", [16, 128, ntok], BF16), "V": P.dram("sc_v", [ntok, 2048], BF16),
          "OT": P.dram("sc_ot", [16, 128, ntok], BF16)}
    P.init_pool(40)
    C = Common(P, ident)
    P.persist += P.stage_tiles
    evs = []
    nstage = 3 * nlayers
    src = x
    k = 0

    def nxt():
        return y if k == nstage - 1 else bufs[k % 2]

    def ffn(f, i, src, dst):
        mk = P.mark()
        B = alloc_ffn(P)
        xt = [P.newT(None, "xi") for _ in range(ntok // 128)]
        yt = [P.newT(None, "yo") for _ in range(ntok // 128)]
        e2 = []
        emit_ffn(P, C, B, src, xt, dst, yt, I["wgu%d_%d" % (f, i)], I["wo%d_%d" % (f, i)],
                 I["ln%d_g_%d" % (1 if f == 1 else 3, i)], I["ln%d_b_%d" % (1 if f == 1 else 3, i)], e2, ntok=ntok)
        P.release(mk)
        return e2

    for i in range(nlayers):
        dst = nxt()
        last = ffn(1, i, src, dst)
        src = dst
        k += 1
        dst = nxt()
        kind = MIX_KIND[i]
        last = []
        if kind == "a":
            emit_mixer_a(P, C, src, dst, SA, I["a_w_in_%d" % i], I["a_qg_%d" % i], I["a_kg_%d" % i], I["a_w_out_%d" % i],
                         I["ln2_g_%d" % i], I["ln2_b_%d" % i], I["cosA"], I["sinA"], last, ntok=ntok)
        elif kind == "b":
            emit_mixer_b(P, C, src, dst, SB, I["b_w_in_%d" % i], I["b_w_out_%d" % i], I["ln2_g_%d" % i], I["ln2_b_%d" % i],
                         I["cosB"], I["sinB"], I["maskB"], last, ntok=ntok)
        else:
            emit_mixer_c(P, C, src, dst, SC, I["c_w_in_%d" % i], I["c_qg_%d" % i], I["c_kvg_%d" % i], I["c_wq_%d" % i],
                         I["c_wkv_%d" % i], I["c_w_out_%d" % i], I["ln2_g_%d" % i], I["ln2_b_%d" % i],
                         I["cosC"], I["sinC"], last, ntok=ntok)
        src = dst
        k += 1
        dst = nxt()
        last = ffn(2, i, src, dst)
        src = dst
        k += 1
    P.emit(last)
    return nc, P


def host_inputs(inputs, nlayers=DEPTH):
    f32 = lambda a: np.ascontiguousarray(np.asarray(a, dtype=np.float32))
    H = {"ident": np.eye(128, dtype=np.float32)}
    H["cosA"], H["sinA"] = rope_tables_axial()
    H["cosB"], H["sinB"] = rope_tables_seq(128)
    H["cosC"], H["sinC"] = rope_tables_seq(64)
    H["maskB"] = mask_tables_b()
    for i in range(nlayers):
        for f in (1, 2):
            H["wgu%d_%d" % (f, i)] = lay_wgu(inputs["ffn%d_w_in_%d" % (f, i)])
            H["wo%d_%d" % (f, i)] = lay_wo(inputs["ffn%d_w_out_%d" % (f, i)])
        for n in (1, 2, 3):
            H["ln%d_g_%d" % (n, i)] = f32(inputs["ln%d_g_%d" % (n, i)])
            H["ln%d_b_%d" % (n, i)] = f32(inputs["ln%d_b_%d" % (n, i)])
        k = MIX_KIND[i]
        if k == "a":
            H["a_w_in_%d" % i] = lay_kn(inputs["a_w_in_%d" % i], 16)
            H["a_w_out_%d" % i] = lay_kn(inputs["a_w_out_%d" % i], 16)
            H["a_qg_%d" % i] = f32(inputs["a_q_gain_%d" % i])
            H["a_kg_%d" % i] = f32(inputs["a_k_gain_%d" % i])
        elif k == "b":
            H["b_w_in_%d" % i] = lay_w_in_b(inputs["b_w_in_%d" % i])
            H["b_w_out_%d" % i] = lay_kn(inputs["b_w_out_%d" % i], 8)
        else:
            H["c_w_in_%d" % i] = lay_w_in_c(inputs["c_w_in_%d" % i])
            H["c_wq_%d" % i] = lay_kn(inputs["c_w_q_up_%d" % i], 6)
            H["c_wkv_%d" % i] = lay_kn(inputs["c_w_kv_up_%d" % i], 4)
            H["c_w_out_%d" % i] = lay_kn(inputs["c_w_out_%d" % i], 16)
            H["c_qg_%d" % i] = f32(inputs["c_q_gain_%d" % i])
            H["c_kvg_%d" % i] = f32(inputs["c_kv_gain_%d" % i])
    return H


ALL_INPUT_NAMES = (
    "x",
    "ffn1_w_in_0",
    "ffn1_w_out_0",
    "ln1_g_0",
    "ln1_b_0",
    "a_w_in_0",
    "a_q_gain_0",
    "a_k_gain_0",
    "a_w_out_0",
    "ln2_g_0",
    "ln2_b_0",
    "ffn2_w_in_0",
    "ffn2_w_out_0",
    "ln3_g_0",
    "ln3_b_0",
    "ffn1_w_in_1",
    "ffn1_w_out_1",
    "ln1_g_1",
    "ln1_b_1",
    "b_w_in_1",
    "b_w_out_1",
    "ln2_g_1",
    "ln2_b_1",
    "ffn2_w_in_1",
    "ffn2_w_out_1",
    "ln3_g_1",
    "ln3_b_1",
    "ffn1_w_in_2",
    "ffn1_w_out_2",
    "ln1_g_2",
    "ln1_b_2",
    "c_w_in_2",
    "c_q_gain_2",
    "c_kv_gain_2",
    "c_w_q_up_2",
    "c_w_kv_up_2",
    "c_w_out_2",
    "ln2_g_2",
    "ln2_b_2",
    "ffn2_w_in_2",
    "ffn2_w_out_2",
    "ln3_g_2",
    "ln3_b_2",
    "ffn1_w_in_3",
    "ffn1_w_out_3",
    "ln1_g_3",
    "ln1_b_3",
    "a_w_in_3",
    "a_q_gain_3",
    "a_k_gain_3",
    "a_w_out_3",
    "ln2_g_3",
    "ln2_b_3",
    "ffn2_w_in_3",
    "ffn2_w_out_3",
    "ln3_g_3",
    "ln3_b_3",
)


def kernel(**inputs):
    missing = [n for n in ALL_INPUT_NAMES if n not in inputs]
    assert not missing, missing
    x = np.asarray(inputs["x"], dtype=np.float32).reshape(16 * SEQ, D)
    H = host_inputs(inputs)
    nc, _ = build_full_prog()
    in_maps = []
    for c in range(NCORES):
        m = dict(H)
        m["x"] = np.ascontiguousarray(x[c * NTOK:(c + 1) * NTOK])
        in_maps.append(m)
    res = run_bass_kernel_spmd(nc, in_maps, core_ids=list(range(NCORES)))
    out = np.concatenate([res.results[c]["y"] for c in range(NCORES)], axis=0)
    return out.reshape(16, SEQ, D).astype(np.float32)
```
